# Optimizing a Trainium2 kernel written in Bass

```python
import math
import jax
import jax.numpy as jnp
from jax import lax
import numpy as np


D_MODEL = 2048
BATCH = 4
SEQ = 2048
DEPTH = 4

GDN_HEADS = 8
GDN_HEAD_DIM = 128
GDN_WIDTH = GDN_HEADS * GDN_HEAD_DIM
GDN_CHUNK = 64
QKV_CONV = 5
N_DIR = 2
SGU_GROUPS = 8
SGU_GROUP_DIM = 128
SGU_WIDTH = SGU_GROUPS * SGU_GROUP_DIM
SGU_BLOCK = 128
D_FF = 5632
FFN_CONV = 3
NORM_EPS = 1e-6
IN_SIZES = (3 * GDN_WIDTH, GDN_WIDTH, N_DIR * GDN_HEADS, N_DIR * GDN_HEADS, SGU_WIDTH, SGU_WIDTH, D_MODEL, D_MODEL)
N_IN = 3 * GDN_WIDTH + GDN_WIDTH + 2 * N_DIR * GDN_HEADS + 2 * SGU_WIDTH + 2 * D_MODEL

kernel_name = 'hybrid_gdn_sgu_convglu_encoder'


def rms_norm(x, g):
    xf = x.astype(jnp.float32)
    y = xf * lax.rsqrt(jnp.mean(xf * xf, axis=-1, keepdims=True) + NORM_EPS)
    return (y * g.astype(jnp.float32)).astype(x.dtype)


def l2_normalize(x):
    xf = x.astype(jnp.float32)
    return xf * lax.rsqrt(jnp.sum(xf * xf, axis=-1, keepdims=True) + NORM_EPS)


def depthwise_conv_centred(x, w):
    pad = w.shape[0] // 2
    return lax.conv_general_dilated(
        x, w[:, None, :].astype(x.dtype), window_strides=(1,), padding=[(pad, pad)],
        dimension_numbers=('NWC', 'WIO', 'NWC'), feature_group_count=x.shape[-1])


def split_columns(p):
    offsets = []
    acc = 0
    for s in IN_SIZES[:-1]:
        acc += s
        offsets.append(acc)
    return jnp.split(p, offsets, axis=-1)


def gated_delta_rule_chunked(q, k, v, g, beta):
    bsz, nh, seqlen, dk = k.shape
    dv = v.shape[-1]
    c = GDN_CHUNK
    nc = seqlen // c
    q = q * (dk ** -0.5)
    blk = lambda t: t.reshape(bsz, nh, nc, c, *t.shape[3:])
    q, k, v, g, beta = blk(q), blk(k), blk(v), blk(g), blk(beta)
    g = jnp.cumsum(g, axis=-1)
    lower = jnp.tril(jnp.ones((c, c), dtype=bool))
    decay = jnp.where(lower, jnp.exp(jnp.where(lower, g[..., :, None] - g[..., None, :], 0.0)), 0.0)
    k_beta = k * beta[..., None]
    kk = jnp.einsum('bhnid,bhnjd->bhnij', k_beta, k) * decay
    rhs = jnp.concatenate([v * beta[..., None], k_beta * jnp.exp(g)[..., None]], axis=-1)
    sol = lax.linalg.triangular_solve(kk, rhs, left_side=True, lower=True, unit_diagonal=True)
    u, w = sol[..., :dv], sol[..., dv:]
    qk = jnp.einsum('bhnid,bhnjd->bhnij', q, k) * decay
    q_dec = q * jnp.exp(g)[..., None]
    k_dec = k * jnp.exp(g[..., -1:] - g)[..., None]
    chunk_dec = jnp.exp(g[..., -1])

    def step(state, xs):
        q_i, qk_i, u_i, w_i, k_i, d_i = xs
        v_new = u_i - jnp.einsum('bhck,bhkv->bhcv', w_i, state)
        o_i = jnp.einsum('bhck,bhkv->bhcv', q_i, state) + jnp.einsum('bhcs,bhsv->bhcv', qk_i, v_new)
        state = state * d_i[..., None, None] + jnp.einsum('bhck,bhcv->bhkv', k_i, v_new)
        return state, o_i

    xs = tuple(jnp.moveaxis(t, 2, 0) for t in (q_dec, qk, u, w, k_dec, chunk_dec))
    state0 = jnp.zeros((bsz, nh, dk, dv), jnp.float32)
    _, o = lax.scan(step, state0, xs)
    return jnp.moveaxis(o, 0, 2).reshape(bsz, nh, seqlen, dv)


def bidirectional_gated_deltanet(qkv, z, a, b, conv_w, a_log, dt_bias, norm_g):
    bsz, seqlen, _ = qkv.shape
    f32 = jnp.float32
    qkv_c = jax.nn.silu(depthwise_conv_centred(qkv, conv_w))
    q, k, v = jnp.split(qkv_c, 3, axis=-1)
    heads = lambda t: t.reshape(bsz, seqlen, GDN_HEADS, GDN_HEAD_DIM).transpose(0, 2, 1, 3)
    q, k, v = l2_normalize(heads(q)), l2_normalize(heads(k)), heads(v).astype(f32)
    a = a.astype(f32).reshape(bsz, seqlen, N_DIR, GDN_HEADS).transpose(0, 2, 3, 1)
    b = b.astype(f32).reshape(bsz, seqlen, N_DIR, GDN_HEADS).transpose(0, 2, 3, 1)
    g = -jnp.exp(a_log.astype(f32))[None, :, :, None] * jax.nn.softplus(a + dt_bias.astype(f32)[None, :, :, None])
    beta = jax.nn.sigmoid(b)
    rev = lambda t: jnp.flip(t, axis=2)
    o2 = gated_delta_rule_chunked(
        jnp.concatenate([q, rev(q)], axis=1),
        jnp.concatenate([k, rev(k)], axis=1),
        jnp.concatenate([v, rev(v)], axis=1),
        jnp.concatenate([g[:, 0], rev(g[:, 1])], axis=1),
        jnp.concatenate([beta[:, 0], rev(beta[:, 1])], axis=1))
    o = o2[:, :GDN_HEADS] + rev(o2[:, GDN_HEADS:])
    o = o.transpose(0, 2, 1, 3)
    zg = jax.nn.silu(z.astype(f32)).reshape(bsz, seqlen, GDN_HEADS, GDN_HEAD_DIM)
    y = rms_norm(o, norm_g) * zg
    return y.reshape(bsz, seqlen, GDN_WIDTH).astype(qkv.dtype)


def chunked_spatial_gating(u, v, ln_g, ln_b, w_s, b_s):
    bsz, seqlen, _ = v.shape
    vf = v.astype(jnp.float32)
    mu = jnp.mean(vf, axis=-1, keepdims=True)
    var = jnp.mean(jnp.square(vf - mu), axis=-1, keepdims=True)
    vn = ((vf - mu) * lax.rsqrt(var + NORM_EPS) * ln_g.astype(jnp.float32) + ln_b.astype(jnp.float32)).astype(v.dtype)
    vn = vn.reshape(bsz, seqlen // SGU_BLOCK, SGU_BLOCK, SGU_GROUPS, SGU_GROUP_DIM)
    s = jnp.einsum('gts,bnsgc->bntgc', w_s, vn) + b_s.T[None, None, :, :, None]
    return u * s.reshape(bsz, seqlen, SGU_WIDTH)


def setup_inputs(seed: int = 0) -> dict:
    key = jax.random.key(seed)
    ks = jax.random.split(key, 24)
    f32 = jnp.float32
    nrm = lambda k, shape, scale: jax.random.normal(k, shape, f32) * scale
    x = nrm(ks[0], (BATCH, SEQ, D_MODEL), 1.0)
    norm_mix_g = 1.0 + nrm(ks[1], (DEPTH, D_MODEL), 0.02)
    w_in = nrm(ks[2], (DEPTH, D_MODEL, N_IN), D_MODEL ** -0.5)
    qkv_conv_w = nrm(ks[3], (DEPTH, QKV_CONV, 3 * GDN_WIDTH), QKV_CONV ** -0.5)
    a_log = jnp.log(jax.random.uniform(ks[4], (DEPTH, N_DIR, GDN_HEADS), f32, minval=1.0, maxval=16.0))
    dt = jnp.exp(jax.random.uniform(ks[5], (DEPTH, N_DIR, GDN_HEADS), f32, minval=math.log(1e-3), maxval=math.log(1e-1)))
    dt_bias = dt + jnp.log(-jnp.expm1(-dt))
    gdn_norm_g = 1.0 + nrm(ks[6], (DEPTH, GDN_HEAD_DIM), 0.02)
    w_branch_a = nrm(ks[7], (DEPTH, GDN_WIDTH, D_MODEL), GDN_WIDTH ** -0.5)
    sgu_ln_g = 1.0 + nrm(ks[8], (DEPTH, SGU_WIDTH), 0.02)
    sgu_ln_b = nrm(ks[9], (DEPTH, SGU_WIDTH), 0.02)
    sgu_w = nrm(ks[10], (DEPTH, SGU_GROUPS, SGU_BLOCK, SGU_BLOCK), SGU_BLOCK ** -0.5)
    sgu_b = 1.0 + nrm(ks[11], (DEPTH, SGU_GROUPS, SGU_BLOCK), 0.02)
    w_branch_b = nrm(ks[12], (DEPTH, SGU_WIDTH, D_MODEL), SGU_WIDTH ** -0.5)
    w_out = nrm(ks[13], (DEPTH, D_MODEL, D_MODEL), D_MODEL ** -0.5)
    norm_ffn_g = 1.0 + nrm(ks[14], (DEPTH, D_MODEL), 0.02)
    w_up = nrm(ks[15], (DEPTH, D_MODEL, 2 * D_FF), D_MODEL ** -0.5)
    ffn_conv_w = nrm(ks[16], (DEPTH, FFN_CONV, 2 * D_FF), FFN_CONV ** -0.5)
    ffn_conv_b = nrm(ks[17], (DEPTH, 2 * D_FF), 0.02)
    w_down = nrm(ks[18], (DEPTH, D_FF, D_MODEL), D_FF ** -0.5)
    final_norm_g = 1.0 + nrm(ks[19], (D_MODEL,), 0.02)
    return {'x': x, 'norm_mix_g': norm_mix_g, 'w_in': w_in, 'qkv_conv_w': qkv_conv_w,
            'a_log': a_log, 'dt_bias': dt_bias, 'gdn_norm_g': gdn_norm_g, 'w_branch_a': w_branch_a,
            'sgu_ln_g': sgu_ln_g, 'sgu_ln_b': sgu_ln_b, 'sgu_w': sgu_w, 'sgu_b': sgu_b,
            'w_branch_b': w_branch_b, 'w_out': w_out, 'norm_ffn_g': norm_ffn_g, 'w_up': w_up,
            'ffn_conv_w': ffn_conv_w, 'ffn_conv_b': ffn_conv_b, 'w_down': w_down,
            'final_norm_g': final_norm_g}


def reference(x, norm_mix_g, w_in, qkv_conv_w, a_log, dt_bias, gdn_norm_g, w_branch_a,
              sgu_ln_g, sgu_ln_b, sgu_w, sgu_b, w_branch_b, w_out, norm_ffn_g, w_up,
              ffn_conv_w, ffn_conv_b, w_down, final_norm_g):
    for l in range(DEPTH):
        h = rms_norm(x, norm_mix_g[l])
        qkv, z, a, b, u, v, gate_a, gate_b = split_columns(h @ w_in[l])
        y_a = bidirectional_gated_deltanet(qkv, z, a, b, qkv_conv_w[l], a_log[l], dt_bias[l], gdn_norm_g[l])
        y_b = chunked_spatial_gating(jax.nn.gelu(u), jax.nn.gelu(v), sgu_ln_g[l], sgu_ln_b[l], sgu_w[l], sgu_b[l])
        merged = jax.nn.sigmoid(gate_a) * (y_a @ w_branch_a[l]) + jax.nn.sigmoid(gate_b) * (y_b @ w_branch_b[l])
        x = x + merged @ w_out[l]
        h = rms_norm(x, norm_ffn_g[l])
        up = depthwise_conv_centred(h @ w_up[l], ffn_conv_w[l]) + ffn_conv_b[l]
        c_gate, c_val = jnp.split(up, 2, axis=-1)
        x = x + (jax.nn.silu(c_gate) * c_val) @ w_down[l]
    return rms_norm(x, final_norm_g)
```

```python
import os
import numpy as np
import concourse.bass as bass
import concourse.mybir as mybir
from concourse.bass_utils import run_bass_kernel_spmd

F32 = mybir.dt.float32
BF16 = mybir.dt.bfloat16
AF = mybir.ActivationFunctionType
ALU = mybir.AluOpType

D_MODEL = 2048
SEQ = 2048
BATCH = 4
T = 1024
NCK = 16
D_FF = 5632
N_IN = 10272
OFF_Z, OFF_A, OFF_B, OFF_U, OFF_V, OFF_GA, OFF_GB = 3072, 4096, 4112, 4128, 5152, 6176, 8224
EPS = 1e-6
NPC = 16 + 16 + 120 + 264 + 88 + 1
PC_NMG, PC_NFG, PC_QCW, PC_FCW, PC_FCB, PC_GNG = 0, 16, 32, 152, 416, 504
NS_DMA = 8


class Op:
    __slots__ = ("eng", "fn", "deps", "signal", "ticket", "is_dma", "idx")


class Prog:
    ENGS = ["pe", "act", "dve", "pool", "sp"]

    def __init__(self, nc):
        self.nc = nc
        self.ops = []
        self.last_w = {}
        self.readers = {}

    def op(self, eng, fn, reads=(), writes=(), is_dma=False):
        o = Op()
        o.eng, o.fn, o.is_dma, o.signal, o.ticket = eng, fn, is_dma, False, None
        o.idx = len(self.ops)
        writes = list(writes) + [r for r in reads if r[0] == "ps" and r not in writes]
        deps = set()
        for r in reads:
            w = self.last_w.get(r)
            if w is not None:
                deps.add(w)
        for r in writes:
            w = self.last_w.get(r)
            if w is not None:
                deps.add(w)
            for rd in self.readers.get(r, ()):
                deps.add(rd)
        for r in reads:
            self.readers.setdefault(r, []).append(o.idx)
        for r in writes:
            self.last_w[r] = o.idx
            self.readers[r] = []
        best = {}
        red = set()
        for d in deps:
            od = self.ops[d]
            if od.is_dma:
                red.add(d)
            else:
                if od.eng == "pe" and eng == "pe" and not is_dma:
                    continue
                if od.eng not in best or best[od.eng] < d:
                    best[od.eng] = d
        red.update(best.values())
        o.deps = red
        self.ops.append(o)
        return o

    def dma(self, q, out, in_, reads=(), writes=()):
        return self.op(q, lambda e: e.dma_start(out=out, in_=in_), reads, writes, is_dma=True)

    def emit(self):
        nc = self.nc
        ops = self.ops
        sem = {e: nc.alloc_semaphore("sem_" + e) for e in self.ENGS}
        dsem = {e: [nc.alloc_semaphore("dsem_%s_%d" % (e, i)) for i in range(NS_DMA)] for e in ("pool", "sp", "act")}
        dcount = {e: 0 for e in dsem}
        dlast = {e: [None] * NS_DMA for e in dsem}
        for o in ops:
            for d in o.deps:
                ops[d].signal = True
        cnt = {e: 0 for e in self.ENGS}
        for o in ops:
            if o.is_dma:
                k = dcount[o.eng]
                dcount[o.eng] += 1
                slot = k % NS_DMA
                o.ticket = (dsem[o.eng][slot], 16 * (k // NS_DMA + 1))
                if dlast[o.eng][slot] is not None:
                    o.deps.add(dlast[o.eng][slot])
                dlast[o.eng][slot] = o.idx
            elif o.signal:
                cnt[o.eng] += 1
                o.ticket = (sem[o.eng], cnt[o.eng])
        per = {e: [] for e in self.ENGS}
        for o in ops:
            per[o.eng].append(o)
        self.stats = {e: len(per[e]) for e in per}
        self.stats["sig"] = dict(cnt)

        def mk(ename):
            def body(e):
                waited = {}
                for o in per[ename]:
                    for d in sorted(o.deps):
                        s, v = ops[d].ticket
                        key = id(s)
                        if waited.get(key, 0) < v:
                            e.wait_ge(s, v)
                            waited[key] = v
                    inst = o.fn(e)
                    if inst is None:
                        continue
                    if o.is_dma:
                        inst.then_inc(o.ticket[0], 16)
                    elif o.signal:
                        inst.then_inc(o.ticket[0], 1)
            return body

        with nc.Block() as block:
            block.tensor(mk("pe"))
            block.scalar(mk("act"))
            block.vector(mk("dve"))
            block.gpsimd(mk("pool"))
            block.sync(mk("sp"))


class Buf:
    GR = 128

    def __init__(self, nc, name, n, dt):
        self.name = name
        self.n = n
        self.t = nc.alloc_sbuf_tensor(name, [128, n], dt)

    def k(self, s, n):
        return [(self.name, g) for g in range(s // self.GR, (s + n - 1) // self.GR + 1)]

    def v(self, s, n):
        return self.t[:, s:s + n]


def build(depth=4, dbg=None, rg=None, stage=99, nheads=8, noex=False, hstage=9):
    nc = bass.Bass("TRN2", target_bir_lowering=False)
    P = Prog(nc)
    dt_in = lambda name, shape: nc.dram_tensor(name, list(shape), F32, kind="ExternalInput").ap()
    x_d = dt_in("x", [128, NCK, T])
    w_in_d = dt_in("w_in", [depth, D_MODEL, N_IN])
    w_ab_d = dt_in("w_ab", [depth, D_MODEL, 32])
    w_a_d = dt_in("w_a", [depth, 1024, D_MODEL])
    w_b_d = dt_in("w_b", [depth, 1024, D_MODEL])
    w_out_d = dt_in("w_out", [depth, D_MODEL, D_MODEL])
    w_up_d = dt_in("w_up", [depth, D_MODEL, 2 * D_FF])
    w_down_d = dt_in("w_down", [depth, D_FF, D_MODEL])
    pcol_d = dt_in("pcol", [depth, 128, NPC])
    pbc_d = dt_in("pbc", [depth, 128, 32])
    sgu_ln_d = dt_in("sgu_ln", [depth, 128, 2048])
    sgu_bs_d = dt_in("sgu_bs", [depth, 128, 1024])
    sgu_wT_d = dt_in("sgu_wT", [depth, 128, 1024])
    fng_d = dt_in("fng", [128, NCK])
    flags_d = dt_in("flags", [128, 2])
    masks_d = dt_in("masks", [128, 4 * 128])
    nlm_d = dt_in("nlm", [128, 2 * 7 * 128])
    y_d = nc.dram_tensor("y", [128, NCK, T], F32, kind="ExternalOutput").ap()
    dbg_d = None
    if dbg:
        dbg_d = nc.dram_tensor("dbg", [128, dbg], F32, kind="ExternalOutput").ap()
    cc_h_src = nc.dram_tensor("cc_h_src", [128, 32], BF16)
    cc_h_dst = nc.dram_tensor("cc_h_dst", [256, 32], BF16)
    cc_s_src = [nc.dram_tensor("cc_s_src%d" % i, [128, 128], F32) for i in range(2)]
    cc_s_dst = [nc.dram_tensor("cc_s_dst%d" % i, [256, 128], F32) for i in range(2)]
    RG = rg or [[0, 1], [2, 3], [4, 5], [6, 7]]

    XS = Buf(nc, "XS", NCK * T, F32)
    HW = 1028
    HB = Buf(nc, "HB", NCK * HW, BF16)
    BIG = Buf(nc, "BIG", 24 * T, BF16)
    WS = [Buf(nc, "WS%d" % i, 4096, BF16) for i in range(2)]
    TW = 1032
    TA = Buf(nc, "TA", TW, F32)
    TB = Buf(nc, "TB", TW, F32)
    TC = Buf(nc, "TC", TW, F32)
    OB = Buf(nc, "OB", TW, F32)
    PG = Buf(nc, "PG", 25 * 128, F32)
    NET = Buf(nc, "NET", 7 * 128, F32)
    NLM = Buf(nc, "NLM", 2 * 7 * 128, BF16)
    PT = Buf(nc, "PT", 8 * 128, BF16)
    SS = Buf(nc, "SS", 3 * 128, F32)
    CF = Buf(nc, "CF", 6 * 128, F32)
    CB = Buf(nc, "CB", 2 * 128, BF16)
    PCOL = Buf(nc, "PCOL", NPC + 7, F32)
    PBC = Buf(nc, "PBC", 64, F32)
    FNG = Buf(nc, "FNG", 16, F32)
    FLG = Buf(nc, "FLG", 2, F32)
    WT = Buf(nc, "WT", 1024, BF16)
    HX = Buf(nc, "HX", 2 * 32 + 32, BF16)
    PS = nc.alloc_psum_tensor("ps", [128, 8, 512], F32)

    def ps(b, s=0, n=512):
        return PS[:, b, s:s + n]

    def psk(b):
        return [("ps", b)]

    def xs(c, s=0, n=T):
        return XS.v(c * T + s, n)

    def xsk(c, s=0, n=T):
        return XS.k(c * T + s, n)

    def hb(c, s=0, n=T):
        return HB.v(c * HW + s, n)

    def hbk(c, s=0, n=T):
        return [("HB", c, (s + i) // 512) for i in range(0, n, 512)] if s < 1024 else [("HBh", c)]

    def big(i, s=0, n=T):
        return BIG.v(i * T + s, n)

    def bigk(i, s=0, n=T):
        return BIG.k(i * T + s, n)

    UI, LS, LI, US, IDF, ONF = [(CF.v(i * 128, 128), CF.k(i * 128, 128)) for i in range(6)]
    IDB, ONB = [(CB.v(i * 128, 128), CB.k(i * 128, 128)) for i in range(2)]

    def pc(off, n=1):
        return PCOL.v(off, n)

    PCK = [("PCOL", 0)]

    def mm(out, lhsT, rhs, start, stop, R, W):
        P.op("pe", lambda e: e.matmul(out, lhsT, rhs, start=start, stop=stop), R, W)

    def act(out, in_, func, R, W, scale=None, bias=None):
        kw = {}
        if scale is not None:
            kw["scale"] = scale
        if bias is not None:
            kw["bias"] = bias
        P.op("act", lambda e: e.activation(out, in_, func, **kw), R, W)

    def tt(out, a, b, op, R, W, eng="dve"):
        P.op(eng, lambda e: e.tensor_tensor(out, a, b, op), R, W)

    def ts(out, in0, s1, s2, op0, op1, R, W, eng="dve"):
        if op1 is None:
            P.op(eng, lambda e: e.tensor_scalar(out, in0, s1, None, op0), R, W)
        else:
            P.op(eng, lambda e: e.tensor_scalar(out, in0, s1, s2, op0, op1), R, W)

    def stt(out, in0, sc, in1, op0, op1, R, W):
        P.op("dve", lambda e: e.scalar_tensor_tensor(out, in0, sc, in1, op0, op1), R, W)

    def cp(out, in_, R, W, eng="dve"):
        if eng == "act":
            P.op("act", lambda e: e.activation(out, in_, AF.Copy), R, W)
        else:
            P.op(eng, lambda e: e.tensor_copy(out, in_), R, W)

    def recip(out, in_, R, W):
        P.op("dve", lambda e: e.reciprocal(out, in_), R, W)

    def memset(ap, val, W, eng="dve"):
        P.op(eng, lambda e: e.memset(ap, val), (), W)

    ws_i = [0]

    def wslab():
        b = WS[ws_i[0] % 2]
        ws_i[0] += 1
        return b

    def wload(slab, off, src_ap, kc, ncols):
        dst = slab.v(off, kc * ncols).rearrange("p (k n) -> p k n", k=kc)
        src = src_ap.rearrange("(k p) n -> p k n", p=128)
        P.dma("pool", dst, src, (), slab.k(off, kc * ncols))

    def wv(slab, off, kc, ncols, k, c0=0, n=128):
        return slab.v(off + k * ncols + c0, n)

    P.dma("sp", CF.v(0, 512), masks_d[:, :], (), CF.k(0, 512))
    P.dma("pool", NLM.v(0, 1792), nlm_d[:, :], (), NLM.k(0, 1792))
    P.dma("sp", FNG.v(0, 16), fng_d[:, :], (), FNG.k(0, 16))
    P.dma("sp", FLG.v(0, 2), flags_d[:, :], (), FLG.k(0, 2))
    for c in range(NCK):
        P.dma("sp", xs(c), x_d[:, c, :], (), xsk(c))
    tt(IDF[0], UI[0], LI[0], ALU.mult, UI[1] + LI[1], IDF[1])
    memset(ONF[0], 1.0, ONF[1])
    cp(IDB[0], IDF[0], IDF[1], IDB[1])
    cp(ONB[0], ONF[0], ONF[1], ONB[1])
    memset(TA.v(0, TW), 0.0, TA.k(0, TW))
    memset(TB.v(0, TW), 0.0, TB.k(0, TW))
    memset(TC.v(0, TW), 0.0, TC.k(0, TW))
    memset(OB.v(0, TW), 0.0, OB.k(0, TW))
    memset(HB.v(0, NCK * HW), 0.0, [k for c in range(NCK) for k in hbk(c) + hbk(c, 1024, 4)])

    psrot = [0]

    def psb(n=1):
        b = psrot[0]
        psrot[0] = (psrot[0] + n) % 6
        if b + n > 6:
            b = 0
            psrot[0] = n % 6
        return b

    def rmsnorm(goff):
        for tb in range(2):
            s0 = tb * 512
            b = psb()
            for c in range(NCK):
                sq = PT.v((c % 2) * 512, 512)
                sqk = PT.k((c % 2) * 512, 512)
                act(sq, xs(c, s0, 512), AF.Square, xsk(c, s0, 512), sqk)
                mm(ps(b), ONB[0], sq, c == 0, c == NCK - 1, ONB[1] + sqk, psk(b))
            r = TA.v(0, 512)
            rk = TA.k(0, 512)
            act(r, ps(b), AF.Sqrt, psk(b), rk, scale=1.0 / D_MODEL, bias=EPS)
            recip(r, r, rk, rk)
            for c in range(NCK):
                stt(hb(c, s0, 512), xs(c, s0, 512), pc(goff + c), r, ALU.mult, ALU.mult,
                    xsk(c, s0, 512) + PCK + rk, hbk(c, s0, 512))

    def halo_exchange():
        src = HX.v(64, 32).rearrange("p (c t) -> p c t", t=2)
        hsrc = HB.t[:, :].rearrange("p (c w) -> p c w", w=HW)[:, :, 1022:1024]
        cp(src, hsrc, [k for c in range(NCK) for k in hbk(c, 512, 512)], HX.k(64, 32))
        P.dma("pool", cc_h_src.ap(), HX.v(64, 32), HX.k(64, 32), [("cc_h_src",)])
        P.op("pool", lambda g: g.collective_compute("AllGather", ALU.bypass, replica_groups=RG,
                                                    ins=[cc_h_src.ap().opt()], outs=[cc_h_dst.ap().opt()]),
             [("cc_h_src",)], [("cc_h_dst",)])
        P.dma("pool", HX.v(0, 32), cc_h_dst.ap()[0:128, :], [("cc_h_dst",)], HX.k(0, 32))
        P.dma("pool", HX.v(32, 32), cc_h_dst.ap()[128:256, :], [("cc_h_dst",)], HX.k(32, 32))
        ts(HX.v(0, 32), HX.v(0, 32), FLG.v(0, 1), None, ALU.mult, None, HX.k(0, 32) + FLG.k(0, 2), HX.k(0, 32))
        stt(HX.v(0, 32), HX.v(32, 32), FLG.v(1, 1), HX.v(0, 32), ALU.mult, ALU.add,
            HX.k(0, 64) + FLG.k(0, 2), HX.k(0, 32))
        pv = HX.v(0, 32).rearrange("p (c t) -> p c t", t=2)
        hdst = HB.t[:, :].rearrange("p (c w) -> p c w", w=HW)
        hk = [k for c in range(NCK) for k in hbk(c, 1024, 4)]
        cp(hdst[:, :, 1024:1025], pv[:, :, 1:2], HX.k(0, 32), hk)
        cp(hdst[:, :, 1025:1026], pv[:, :, 0:1], HX.k(0, 32), hk)

    def proj_fm_g(slab, off, ncols, col0, dst, dk, halo, evac="act", func=None):
        b = psb(2)
        for tb in range(2):
            for k in range(NCK):
                mm(ps(b + tb), wv(slab, off, NCK, ncols, k, col0), hb(k, tb * 512, 512), k == 0, k == NCK - 1,
                   slab.k(off, NCK * ncols) + hbk(k, tb * 512, 512), psk(b + tb))
        base = halo
        for tb in range(2):
            o = dst(base + tb * 512, 512)
            if func is not None:
                act(o[0], ps(b + tb), func, psk(b + tb), o[1])
            elif evac == "act":
                cp(o[0], ps(b + tb), psk(b + tb), o[1], eng="act")
            else:
                cp(o[0], ps(b + tb), psk(b + tb), o[1])
        if halo:
            b2 = psb()
            for k in range(NCK):
                mm(ps(b2, 0, 32), wv(slab, off, NCK, ncols, k, col0), hb(k, 996, 32), k == 0, k == NCK - 1,
                   slab.k(off, NCK * ncols) + hbk(k, 512, 512) + hbk(k, 1024, 4), psk(b2))
            o = dst(base + 1024, halo)
            cp(o[0], ps(b2, 28, halo), psk(b2), o[1])
        yield

    def proj_fm(*a, **kw):
        run(proj_fm_g(*a, **kw))

    def bufdst(buf):
        return lambda s, n: (buf.v(s, n), buf.k(s, n))

    def bigdst(i):
        return lambda s, n: (big(i, s, n), bigk(i, s, n))

    def l2n_g(accb, dsti, norm):
        a, ak = accb.v(0, T), accb.k(0, T)
        if not norm:
            act(big(dsti), a, AF.Silu, ak, bigk(dsti))
            yield
            return
        act(a, a, AF.Silu, ak, ak)
        yield
        b = psb(2)
        for tb in range(2):
            sq = big(dsti, tb * 512, 512)
            sqk = bigk(dsti, tb * 512, 512)
            act(sq, accb.v(tb * 512, 512), AF.Square, ak, sqk)
            mm(ps(b + tb), ONB[0], sq, True, True, ONB[1] + sqk, psk(b + tb))
        r, rk = TA.v(0, T), TA.k(0, T)
        for tb in range(2):
            act(TA.v(tb * 512, 512), ps(b + tb), AF.Sqrt, psk(b + tb), rk, bias=EPS)
        recip(r, r, rk, rk)
        yield
        tt(big(dsti), a, r, ALU.mult, ak + rk, bigk(dsti))
        yield

    def pg(i, n=1):
        return PG.v(i * 128, n * 128), PG.k(i * 128, n * 128)
    G_AB, G_G, G_BETA, G_NB, G_GC, G_EGC, G_EGR, G_DCH = 0, 2, 3, 4, 5, 6, 7, 8
    G_LG, G_DT, G_DTS, G_DTI, G_AT, G_Q, G_T0, G_T1, G_TT0, G_TT1, G_EGB = range(9, 20)
    def pco(n, j):
        s = 19 * T + (n * 5 + j) * 128
        return BIG.v(s, 128), BIG.k(s, 128)
    QKT_S = 0.08838834764831845

    def gdn_gates(l):
        slab = wslab()
        wload(slab, 0, w_ab_d[l], NCK, 32)
        ab, abk = pg(G_AB, 2)
        for n in range(8):
            b = psb()
            for k in range(NCK):
                mm(ps(b, 0, 32), hb(k, n * 128, 128), wv(slab, 0, NCK, 32, k, 0, 32), k == 0, k == NCK - 1,
                   slab.k(0, 512) + hbk(k, n * 128, 128), psk(b))
            cp(PG.v(G_AB * 128 + n * 32, 32), ps(b, 0, 32), psk(b), abk)
        ab3 = ab.rearrange("p (n c) -> p n c", c=32)
        g3 = pg(G_G)[0].rearrange("p (c n) -> p c n", n=8)
        gk = pg(G_G)[1]
        for n in range(8):
            tt(g3[:, :, n], ab3[:, n, 0:16], PBC.v(16, 16), ALU.add, abk + PBC.k(0, 64), gk)
        gfl = pg(G_G)[0]
        act(gfl, gfl, AF.Exp, gk, gk)
        act(gfl, gfl, AF.Ln, gk, gk, bias=1.0)
        for n in range(8):
            tt(g3[:, :, n], g3[:, :, n], PBC.v(32, 16), ALU.mult, gk + PBC.k(0, 64), gk)
        be3 = pg(G_BETA)[0].rearrange("p (c n) -> p c n", n=8)
        bek = pg(G_BETA)[1]
        for n in range(8):
            act(be3[:, :, n], ab3[:, n, 16:32], AF.Sigmoid, abk, bek)
        ts(pg(G_NB)[0], pg(G_BETA)[0], -1.0, None, ALU.mult, None, bek, pg(G_NB)[1])
        b = psb()
        mm(ps(b, 0, 64), UI[0], PG.v(G_G * 128, 64), True, True, UI[1] + gk, psk(b))
        mm(ps(b, 64, 64), LI[0], PG.v(G_G * 128 + 64, 64), True, True, LI[1] + gk, psk(b))
        b2 = psb()
        mm(ps(b2, 0, 128), ONF[0], gfl, True, True, ONF[1] + gk, psk(b2))
        cp(pg(G_GC)[0], ps(b, 0, 128), psk(b), pg(G_GC)[1])
        act(pg(G_EGC)[0], ps(b, 0, 128), AF.Exp, psk(b), pg(G_EGC)[1])
        tt(pg(G_EGR)[0], ps(b2, 0, 128), pg(G_GC)[0], ALU.subtract, psk(b2) + pg(G_GC)[1], pg(G_EGR)[1])
        act(pg(G_EGR)[0], pg(G_EGR)[0], AF.Exp, pg(G_EGR)[1], pg(G_EGR)[1])
        act(pg(G_DCH)[0], ps(b2, 0, 128), AF.Exp, psk(b2), pg(G_DCH)[1])

    def pgb(g0):
        return PG.v(g0 * 128, 256).bitcast(BF16), PG.k(g0 * 128, 256)

    def netb(i):
        return NET.v(i * 256, 256).bitcast(BF16), NET.k(i * 256, 256)

    ATH = [pgb(12), pgb(14)]
    def tcb(i):
        return TC.v(i * 256, 256).bitcast(BF16), TC.k(i * 256, 256)

    def wtb(i):
        return WT.v(i * 512, 512), WT.k(i * 512, 512)

    CSET = [(pgb(16), pgb(18), pgb(22), netb(0), netb(1), netb(2)),
            (tcb(0), tcb(1), tcb(2), tcb(3), wtb(0), wtb(1))]

    def c4(buf, j):
        return buf[0][:, j * 128:(j + 1) * 128]

    FSTOP = 999999999
    fcount = [0]

    def fgate():
        fcount[0] += 1
        return fcount[0] <= FSTOP

    def prep_front(d, h, Q_i, K_i, batch):
        TRI = UI if d == 0 else LI
        MINC, MSTR = (UI, US) if d == 0 else (LI, LS)
        ATh = ATH[batch]
        for j in range(4):
            n = batch * 4 + j
            col = (d * 8 + h) * 8 + n
            c0 = n * 128
            gcol = PG.v(G_G * 128 + col, 1)
            becol = PG.v(G_BETA * 128 + col, 1)
            gccol = PG.v(G_GC * 128 + col, 1)
            Lgk = pg(9)[1]
            Lhl = PG.v(9 * 128, 128).bitcast(BF16)
            Lh, Ll = Lhl[:, 0:128], Lhl[:, 128:256]
            if fgate():
                ts(Lh, TRI[0], gcol, None, ALU.mult, None, TRI[1] + pg(G_G)[1], Lgk)
            if fgate():
                stt(Ll, TRI[0], gcol, Lh, ALU.mult, ALU.subtract, TRI[1] + pg(G_G)[1] + Lgk, Lgk)
            b = psb()
            if fgate():
                mm(ps(b, 0, 128), ONB[0], Lh, True, False, ONB[1] + Lgk, psk(b))
                mm(ps(b, 0, 128), ONB[0], Ll, False, True, ONB[1] + Lgk, psk(b))
            EGB, EGBk = pg(11)
            if fgate():
                act(EGB, ps(b, 0, 128), AF.Exp, psk(b), EGBk)
            DT, DTk = pg(10)
            if fgate():
                ts(DT, ps(b, 0, 128), gccol, 0.0, ALU.subtract, ALU.min, psk(b) + pg(G_GC)[1], DTk)
            if fgate():
                act(DT, DT, AF.Exp, DTk, DTk)
            QgT, QgTk = pco(n, 2)
            if fgate():
                stt(QgT, big(Q_i, c0, 128), QKT_S, EGB, ALU.mult, ALU.mult, bigk(Q_i, c0, 128) + EGBk, QgTk)
            b = psb()
            if fgate():
                mm(ps(b, 0, 128), big(K_i, c0, 128), big(K_i, c0, 128), True, True, bigk(K_i, c0, 128), psk(b))
            if fgate():
                mm(ps(b, 128, 128), big(K_i, c0, 128), big(Q_i, c0, 128), True, True,
                   bigk(K_i, c0, 128) + bigk(Q_i, c0, 128), psk(b))
            if fgate():
                tt(DT, DT, MINC[0], ALU.mult, DTk + MINC[1], DTk)
            QKT, QKTk = pco(n, 3)
            if fgate():
                stt(QKT, ps(b, 128, 128), QKT_S, DT, ALU.mult, ALU.mult, psk(b) + DTk, QKTk)
            if fgate():
                tt(DT, DT, MSTR[0], ALU.mult, DTk + MSTR[1], DTk)
            if fgate():
                stt(c4(ATh, j), ps(b, 0, 128), becol, DT, ALU.mult, ALU.mult, psk(b) + pg(G_BETA)[1] + DTk, ATh[1])
                yield

    def prep_chain(d, batch):
        ATh = ATH[batch]
        NETH, QH, TH, TL, TTH, TTL = CSET[batch]
        r4 = lambda buf: buf[0].rearrange("p (j c) -> p j c", j=4)
        for lev in range(7):
            mask = NLM.v((d * 7 + lev) * 128, 128).unsqueeze(1).broadcast_to([128, 4, 128])
            tt(r4(NETH), r4(ATh), mask, ALU.mult, ATh[1] + NLM.k((d * 7 + lev) * 128, 128), NETH[1], eng="pool")
            bq = psb()
            for j in range(4):
                o = ps(bq, j * 128, 128)
                if lev == 0:
                    mm(o, IDB[0], IDB[0], True, False, IDB[1], psk(bq))
                    mm(o, c4(NETH, j), IDB[0], False, True, NETH[1] + IDB[1], psk(bq))
                else:
                    mm(o, IDB[0], IDB[0], True, False, IDB[1], psk(bq))
                    mm(o, c4(NETH, j), c4(TH, j), False, False, NETH[1] + TH[1], psk(bq))
                    mm(o, c4(NETH, j), c4(TL, j), False, True, NETH[1] + TL[1], psk(bq))
            if lev == 0:
                cp(TH[0], ps(bq), psk(bq), TH[1], eng="act")
                tt(TL[0], ps(bq), TH[0], ALU.subtract, psk(bq) + TH[1], TL[1])
                for j in range(4):
                    tt(c4(TTH, j), c4(NETH, j), IDB[0], ALU.add, NETH[1] + IDB[1], TTH[1])
                memset(TTL[0], 0.0, TTL[1])
                yield
                continue
            cp(QH[0], ps(bq), psk(bq), QH[1], eng="act")
            yield
            last = lev == 6
            if not last:
                bt = psb()
                for j in range(4):
                    o = ps(bt, j * 128, 128)
                    mm(o, c4(TTH, j), c4(QH, j), True, False, TTH[1] + QH[1], psk(bt))
                    mm(o, c4(TTL, j), c4(QH, j), False, True, TTL[1] + QH[1], psk(bt))
            btt = psb()
            for j in range(4):
                o = ps(btt, j * 128, 128)
                mm(o, c4(QH, j), c4(TTH, j), True, False, TTH[1] + QH[1], psk(btt))
                mm(o, c4(QH, j), c4(TTL, j), False, True, TTL[1] + QH[1], psk(btt))
            if not last:
                cp(TH[0], ps(bt), psk(bt), TH[1], eng="act")
                tt(TL[0], ps(bt), TH[0], ALU.subtract, psk(bt) + TH[1], TL[1])
            cp(TTH[0], ps(btt), psk(btt), TTH[1], eng="act")
            if not last:
                tt(TTL[0], ps(btt), TTH[0], ALU.subtract, psk(btt) + TTH[1], TTL[1])
            yield

    def prep_tail(d, h, K_i, V_i, batch):
        TTH = CSET[batch][4]
        for j in range(4):
            n = batch * 4 + j
            col = (d * 8 + h) * 8 + n
            c0 = n * 128
            becol = PG.v(G_BETA * 128 + col, 1)
            b = psb()
            mm(ps(b, 0, 128), big(K_i, c0, 128), IDB[0], True, True, bigk(K_i, c0, 128) + IDB[1], psk(b))
            mm(ps(b, 128, 128), big(V_i, c0, 128), IDB[0], True, True, bigk(V_i, c0, 128) + IDB[1], psk(b))
            Kg, Kgk = PT.v((j % 2) * 256, 128), PT.k((j % 2) * 256, 128)
            Vt, Vtk = PT.v((j % 2) * 256 + 128, 128), PT.k((j % 2) * 256 + 128, 128)
            Kd, Kdk = pco(n, 4)
            ts(Kg, ps(b, 0, 128), PG.v(G_EGC * 128 + col, 1), None, ALU.mult, None, psk(b) + pg(G_EGC)[1], Kgk)
            ts(Kd, ps(b, 0, 128), PG.v(G_EGR * 128 + col, 1), None, ALU.mult, None, psk(b) + pg(G_EGR)[1], Kdk)
            cp(Vt, ps(b, 128, 128), psk(b), Vtk, eng="act")
            b = psb()
            mm(ps(b, 0, 128), c4(TTH, j), Vt, True, True, TTH[1] + Vtk, psk(b))
            mm(ps(b, 128, 128), Kg, c4(TTH, j), True, True, Kgk + TTH[1], psk(b))
            Ub, Ubk = pco(n, 1)
            ts(Ub, ps(b, 0, 128), becol, None, ALU.mult, None, psk(b) + pg(G_BETA)[1], Ubk)
            WT_, WTk = pco(n, 0)
            cp(WT_, ps(b, 128, 128), psk(b), WTk, eng="act")
            yield

    def run(*gens):
        gens = list(gens)
        while gens:
            for g in list(gens):
                try:
                    next(g)
                except StopIteration:
                    gens.remove(g)

    PSTOP = 99

    def seq(*gens):
        for g in gens:
            yield from g

    def gdn_prep(d, h, Q_i, K_i, V_i, skip_front0=False):
        if not skip_front0:
            run(prep_front(d, h, Q_i, K_i, 0))
        run(prep_chain(d, 0), seq(prep_front(d, h, Q_i, K_i, 1), prep_chain(d, 1)))
        run(prep_tail(d, h, K_i, V_i, 0), prep_tail(d, h, K_i, V_i, 1))

    def gdn_scan_gen(d, h, first_dir):
        S, Sk = SS.v(0, 128), SS.k(0, 128)
        Sb, Sbk = PT.v(384, 128), PT.k(384, 128)
        vn, vnk = PT.v(512, 128), PT.k(512, 128)
        St, Stk = PT.v(640, 128), PT.k(640, 128)
        cp(Sb, S, Sk, Sbk)
        order = range(8) if d == 0 else range(7, -1, -1)
        for n in order:
            col = (d * 8 + h) * 8 + n
            c0 = n * 128
            WT_, WTk = pco(n, 0)
            Ub, Ubk = pco(n, 1)
            QgT, QgTk = pco(n, 2)
            QKT, QKTk = pco(n, 3)
            Kd, Kdk = pco(n, 4)
            b = psb()
            mm(ps(b, 0, 128), WT_, Sb, True, True, WTk + Sbk, psk(b))
            stt(vn, ps(b, 0, 128), PG.v(G_NB * 128 + col, 1), Ub, ALU.mult, ALU.add,
                psk(b) + pg(G_NB)[1] + Ubk, vnk)
            b = psb()
            mm(ps(b, 0, 128), Sb, QgT, True, False, Sbk + QgTk, psk(b))
            mm(ps(b, 0, 128), vn, QKT, False, True, vnk + QKTk, psk(b))
            b3 = psb()
            mm(ps(b3, 0, 128), Kd, vn, True, True, Kdk + vnk, psk(b3))
            if first_dir:
                cp(OB.v(c0, 128), ps(b, 0, 128), psk(b), OB.k(c0, 128), eng="act")
            else:
                tt(OB.v(c0, 128), OB.v(c0, 128), ps(b, 0, 128), ALU.add, psk(b) + OB.k(c0, 128), OB.k(c0, 128))
            stt(S, S, PG.v(G_DCH * 128 + col, 1), ps(b3, 0, 128), ALU.mult, ALU.add,
                Sk + pg(G_DCH)[1] + psk(b3), Sk)
            cp(Sb, S, Sk, Sbk, eng="act")
            yield

    def state_exchange_issue(i):
        S, Sk = SS.v(0, 128), SS.k(0, 128)
        P.dma("pool", cc_s_src[i].ap(), S, Sk, [("cc_s_src", i)])
        P.op("pool", lambda g: g.collective_compute("AllGather", ALU.bypass, replica_groups=RG,
                                                    ins=[cc_s_src[i].ap().opt()], outs=[cc_s_dst[i].ap().opt()]),
             [("cc_s_src", i)], [("cc_s_dst", i)])
        P.dma("pool", SS.v(128, 128), cc_s_dst[i].ap()[0:128, :], [("cc_s_dst", i)], SS.k(128, 128))
        P.dma("pool", SS.v(256, 128), cc_s_dst[i].ap()[128:256, :], [("cc_s_dst", i)], SS.k(256, 128))

    def state_exchange_finish():
        S, Sk = SS.v(0, 128), SS.k(0, 128)
        ts(S, SS.v(128, 128), FLG.v(0, 1), None, ALU.mult, None, SS.k(128, 128) + FLG.k(0, 2), Sk)
        stt(S, SS.v(256, 128), FLG.v(1, 1), S, ALU.mult, ALU.add, SS.k(256, 128) + FLG.k(0, 2) + Sk, Sk)

    def conv_taps_g(pbuf, accb, woff, ntap, boff=None):
        a, ak = accb.v(0, T), accb.k(0, T)
        pk = pbuf.k(0, TW)
        if boff is None:
            ts(a, pbuf.v(0, T), pc(woff), None, ALU.mult, None, pk + PCK, ak)
        else:
            ts(a, pbuf.v(0, T), pc(woff), pc(boff), ALU.mult, ALU.add, pk + PCK, ak)
        yield
        for tap in range(1, ntap):
            stt(a, pbuf.v(tap, T), pc(woff + tap), a, ALU.mult, ALU.add, pk + PCK + ak, ak)
            yield

    def conv_taps(*a, **kw):
        run(conv_taps_g(*a, **kw))

    def sgu(l):
        P.dma("pool", WT.v(0, 1024), sgu_wT_d[l], (), WT.k(0, 1024))
        LNG, LNGk = pg(9, 8)
        LNB, LNBk = pg(17, 8)
        P.dma("sp", LNG, sgu_ln_d[l][:, 0:1024], (), LNGk)
        P.dma("sp", LNB, sgu_ln_d[l][:, 1024:2048], (), LNBk)
        BSB, BSBk = OB.v(0, T), OB.k(0, T)
        P.dma("sp", BSB, sgu_bs_d[l], (), BSBk)

        def gelu(x, xk, t1, t1k, out, outk):
            act(t1, x, AF.Square, xk, t1k)
            ts(t1, t1, 0.044715, 1.0, ALU.mult, ALU.add, t1k, t1k)
            tt(t1, t1, x, ALU.mult, t1k + xk, t1k)
            act(t1, t1, AF.Sigmoid, t1k, t1k, scale=1.5957691216057308)
            tt(out, t1, x, ALU.mult, t1k + xk, outk)

        for g in range(8):
            if g % 2 == 0:
                slab = wslab()
                wload(slab, 0, w_in_d[l][:, OFF_U + g * 128: OFF_U + g * 128 + 256], NCK, 256)
            proj_fm(slab, 0, 256, (g % 2) * 128, bufdst(TA), None, 0)
            gelu(TA.v(0, T), TA.k(0, T), TB.v(0, T), TB.k(0, T), big(g), bigk(g))
        for n in range(8):
            bq = 6
            for half in range(2):
                for q in range(2):
                    cq = half * 512 + q * 256
                    slab = wslab()
                    wload(slab, 0, w_in_d[l][:, OFF_V + cq: OFF_V + cq + 256], NCK, 256)
                    for k in range(NCK):
                        mm(ps(bq + half, q * 256, 256), hb(k, n * 128, 128), wv(slab, 0, NCK, 256, k, 0, 256),
                           k == 0, k == NCK - 1, slab.k(0, 4096) + hbk(k, n * 128, 128), psk(bq + half))
            for half in range(2):
                cp(TA.v(half * 512, 512), ps(bq + half), psk(bq + half), TA.k(0, T), eng="act")
            gelu(TA.v(0, T), TA.k(0, T), TB.v(0, T), TB.k(0, T), TC.v(0, T), TC.k(0, T))
            st, stk = PT.v(0, 8), PT.k(0, 8)
            mu = SS.v(128, 1)
            muk = SS.k(128, 8)
            P.op("dve", lambda e: e.tensor_reduce(SS.v(128, 1), TC.v(0, T), mybir.AxisListType.X, ALU.add),
                 TC.k(0, T), muk)
            ts(SS.v(129, 1), SS.v(128, 1), -1.0 / 1024, None, ALU.mult, None, muk, muk)
            ts(TC.v(0, T), TC.v(0, T), SS.v(129, 1), None, ALU.add, None, TC.k(0, T) + muk, TC.k(0, T))
            act(TB.v(0, T), TC.v(0, T), AF.Square, TC.k(0, T), TB.k(0, T))
            P.op("dve", lambda e: e.tensor_reduce(SS.v(130, 1), TB.v(0, T), mybir.AxisListType.X, ALU.add),
                 TB.k(0, T), muk)
            act(SS.v(131, 1), SS.v(130, 1), AF.Sqrt, muk, muk, scale=1.0 / 1024, bias=EPS)
            recip(SS.v(131, 1), SS.v(131, 1), muk, muk)
            stt(TC.v(0, T), TC.v(0, T), SS.v(131, 1), LNG, ALU.mult, ALU.mult, TC.k(0, T) + muk + LNGk, TC.k(0, T))
            VN, VNk = BIG.v(16 * T, T), BIG.k(16 * T, T)
            tt(VN, TC.v(0, T), LNB, ALU.add, TC.k(0, T) + LNBk, VNk)
            for half in range(2):
                b = psb()
                for gq in range(4):
                    g = half * 4 + gq
                    mm(ps(b, gq * 128, 128), BIG.v(16 * T + g * 128, 128), WT.v(g * 128, 128), True, True,
                       VNk + WT.k(0, 1024), psk(b))
                tmp, tmpk = TA.v(0, 512), TA.k(0, 512)
                tt(tmp, ps(b), OB.v(half * 512, 512), ALU.add, psk(b) + BSBk, tmpk)
                for gq in range(4):
                    g = half * 4 + gq
                    tt(big(g, n * 128, 128), big(g, n * 128, 128), TA.v(gq * 128, 128), ALU.mult,
                       bigk(g, n * 128, 128) + tmpk, bigk(g, n * 128, 128))

    def mixer(l):
        P.dma("sp", PCOL.v(0, NPC), pcol_d[l], (), PCK)
        P.dma("sp", PBC.v(0, 32), pbc_d[l], (), PBC.k(0, 64))
        act(PBC.v(32, 16), PBC.v(0, 16), AF.Exp, PBC.k(0, 64), PBC.k(0, 64))
        ts(PBC.v(32, 16), PBC.v(32, 16), -1.0, None, ALU.mult, None, PBC.k(0, 64), PBC.k(0, 64))
        rmsnorm(PC_NMG)
        if stage < 1:
            return
        halo_exchange()
        if stage < 2:
            return
        sgu(l)
        if stage < 3:
            return
        gdn_gates(l)
        if stage < 4:
            return
        Q_i, K_i, V_i = 16, 17, 18

        def head_pre(h):
            s1 = wslab()
            wload(s1, 0, w_in_d[l][:, h * 128: h * 128 + 128], NCK, 128)
            wload(s1, 2048, w_in_d[l][:, 1024 + h * 128: 1024 + h * 128 + 128], NCK, 128)
            s2 = wslab()
            wload(s2, 0, w_in_d[l][:, 2048 + h * 128: 2048 + h * 128 + 128], NCK, 128)
            wload(s2, 2048, w_in_d[l][:, OFF_Z + h * 128: OFF_Z + h * 128 + 128], NCK, 128)
            yield from proj_fm_g(s2, 2048, 128, 0, bigdst(8 + h), None, 0, func=AF.Silu)
            for j, (slab, off, dsti, norm) in enumerate(((s1, 0, Q_i, True), (s1, 2048, K_i, True), (s2, 0, V_i, False))):
                pdst = lambda s, n: (TA.v(s, n), TA.k(s, n))
                memset(TA.v(0, 2), 0.0, TA.k(0, 2))
                yield from proj_fm_g(slab, off, 128, 0, pdst, None, 2)
                yield from conv_taps_g(TA, TB, PC_QCW + (j * 8 + h) * 5, 5)
                yield from l2n_g(TB, dsti, norm)

        def gnorm_g(h):
            b = psb(2)
            for tb in range(2):
                sq, sqk = PT.v(tb * 512, 512), PT.k(tb * 512, 512)
                act(sq, OB.v(tb * 512, 512), AF.Square, OB.k(0, T), sqk)
                mm(ps(b + tb), ONB[0], sq, True, True, ONB[1] + sqk, psk(b + tb))
            for tb in range(2):
                act(TC.v(tb * 512, 512), ps(b + tb), AF.Sqrt, psk(b + tb), TC.k(0, T), scale=1.0 / 128, bias=EPS)
            recip(TC.v(0, T), TC.v(0, T), TC.k(0, T), TC.k(0, T))
            yield
            tt(TC.v(0, T), TC.v(0, T), OB.v(0, T), ALU.mult, TC.k(0, T) + OB.k(0, T), TC.k(0, T))
            yield
            stt(big(8 + h), TC.v(0, T), pc(PC_GNG), big(8 + h), ALU.mult, ALU.mult,
                TC.k(0, T) + PCK + bigk(8 + h), bigk(8 + h))
            yield

        run(head_pre(0))
        for h in range(nheads):
            memset(SS.v(0, 128), 0.0, SS.k(0, 128))
            gdn_prep(0, h, Q_i, K_i, V_i)
            run(gdn_scan_gen(0, h, True), prep_front(1, h, Q_i, K_i, 0))
            state_exchange_issue(h % 2)
            gdn_prep(1, h, Q_i, K_i, V_i, skip_front0=True)
            state_exchange_finish()
            post = seq(gdn_scan_gen(1, h, False), gnorm_g(h))
            if h + 1 < nheads:
                run(post, head_pre(h + 1))
            else:
                run(post)
        if stage < 5:
            return
        for tb in range(2):
            s0 = tb * 512

            def mg(c):
                s = 16 * T + c * 512
                return BIG.v(s, 512), BIG.k(s, 512)
            for c in range(NCK):
                sa = wslab()
                wload(sa, 0, w_a_d[l][:, c * 128:(c + 1) * 128], 8, 128)
                wload(sa, 1024, w_b_d[l][:, c * 128:(c + 1) * 128], 8, 128)
                sg = wslab()
                wload(sg, 0, w_in_d[l][:, OFF_GA + c * 128: OFF_GA + (c + 1) * 128], NCK, 128)
                wload(sg, 2048, w_in_d[l][:, OFF_GB + c * 128: OFF_GB + (c + 1) * 128], NCK, 128)
                b = psb(4)
                for k in range(8):
                    mm(ps(b), wv(sa, 0, 8, 128, k), big(8 + k, s0, 512), k == 0, k == 7,
                       sa.k(0, 1024) + bigk(8 + k, s0, 512), psk(b))
                for k in range(8):
                    mm(ps(b + 1), wv(sa, 1024, 8, 128, k), big(k, s0, 512), k == 0, k == 7,
                       sa.k(1024, 1024) + bigk(k, s0, 512), psk(b + 1))
                for k in range(NCK):
                    mm(ps(b + 2), wv(sg, 0, NCK, 128, k), hb(k, s0, 512), k == 0, k == NCK - 1,
                       sg.k(0, 2048) + hbk(k, s0, 512), psk(b + 2))
                for k in range(NCK):
                    mm(ps(b + 3), wv(sg, 2048, NCK, 128, k), hb(k, s0, 512), k == 0, k == NCK - 1,
                       sg.k(2048, 2048) + hbk(k, s0, 512), psk(b + 3))
                act(TA.v(0, 512), ps(b + 2), AF.Sigmoid, psk(b + 2), TA.k(0, 512))
                act(TA.v(512, 512), ps(b + 3), AF.Sigmoid, psk(b + 3), TA.k(512, 512))
                tt(TB.v(0, 512), ps(b), TA.v(0, 512), ALU.mult, psk(b) + TA.k(0, 512), TB.k(0, 512))
                tt(TB.v(512, 512), ps(b + 1), TA.v(512, 512), ALU.mult, psk(b + 1) + TA.k(512, 512), TB.k(512, 512))
                m, mk_ = mg(c)
                tt(m, TB.v(0, 512), TB.v(512, 512), ALU.add, TB.k(0, T), mk_)
            for co in range(NCK):
                so = wslab()
                wload(so, 0, w_out_d[l][:, co * 128:(co + 1) * 128], NCK, 128)
                b = psb()
                for k in range(NCK):
                    m, mk_ = mg(k)
                    mm(ps(b), wv(so, 0, NCK, 128, k), m, k == 0, k == NCK - 1, so.k(0, 2048) + mk_, psk(b))
                tt(xs(co, s0, 512), xs(co, s0, 512), ps(b), ALU.add, xsk(co, s0, 512) + psk(b), xsk(co, s0, 512))

    def ffn(l):
        rmsnorm(PC_NFG)
        halo_exchange()
        memset(TA.v(0, 1), 0.0, TA.k(0, 1))
        memset(TC.v(0, 1), 0.0, TC.k(0, 1))
        for qd in range(4):
            for j in range(11):
                cg = qd * 11 + j
                slab = wslab()
                wload(slab, 0, w_up_d[l][:, cg * 128:(cg + 1) * 128], NCK, 128)
                wload(slab, 2048, w_up_d[l][:, D_FF + cg * 128: D_FF + (cg + 1) * 128], NCK, 128)
                proj_fm(slab, 0, 128, 0, lambda s, n: (TA.v(s, n), TA.k(s, n)), None, 1)
                conv_taps(TA, TB, PC_FCW + cg * 3, 3, PC_FCB + cg)
                proj_fm(slab, 2048, 128, 0, lambda s, n: (TC.v(s, n), TC.k(s, n)), None, 1)
                conv_taps(TC, OB, PC_FCW + (44 + cg) * 3, 3, PC_FCB + 44 + cg)
                act(TB.v(0, T), TB.v(0, T), AF.Silu, TB.k(0, T), TB.k(0, T))
                tt(big(j), TB.v(0, T), OB.v(0, T), ALU.mult, TB.k(0, T) + OB.k(0, T), bigk(j))
            for co in range(NCK):
                slab = wslab()
                wload(slab, 0, w_down_d[l][qd * 1408:(qd + 1) * 1408, co * 128:(co + 1) * 128], 11, 128)
                for tb in range(2):
                    b = psb()
                    for k in range(11):
                        mm(ps(b), wv(slab, 0, 11, 128, k), big(k, tb * 512, 512), k == 0, k == 10,
                           slab.k(0, 1408) + bigk(k, tb * 512, 512), psk(b))
                    tt(xs(co, tb * 512, 512), xs(co, tb * 512, 512), ps(b), ALU.add,
                       xsk(co, tb * 512, 512) + psk(b), xsk(co, tb * 512, 512))

    for l in range(depth):
        mixer(l)
        if stage >= 6:
            ffn(l)

    outs = []
    for tb in range(2):
        s0 = tb * 512
        b = psb()
        for c in range(NCK):
            sq = PT.v((c % 2) * 512, 512)
            sqk = PT.k((c % 2) * 512, 512)
            act(sq, xs(c, s0, 512), AF.Square, xsk(c, s0, 512), sqk)
            mm(ps(b), ONB[0], sq, c == 0, c == NCK - 1, ONB[1] + sqk, psk(b))
        r, rk = TA.v(0, 512), TA.k(0, 512)
        act(r, ps(b), AF.Sqrt, psk(b), rk, scale=1.0 / D_MODEL, bias=EPS)
        recip(r, r, rk, rk)
        for c in range(NCK):
            stt(xs(c, s0, 512), xs(c, s0, 512), FNG.v(c, 1), r, ALU.mult, ALU.mult,
                xsk(c, s0, 512) + FNG.k(0, 16) + rk, xsk(c, s0, 512))
    for c in range(NCK):
        outs.append(P.dma("sp", y_d[:, c, :], xs(c), xsk(c), [("y", c)]))
    P.op("sp", lambda e: None, [("y", c) for c in range(NCK)], ())
    P.emit()
    return nc, P


def _masks():
    i = np.arange(128)
    m, c = i[:, None], i[None, :]
    UI = (m <= c).astype(np.float32)
    LS = (m > c).astype(np.float32)
    LI = (m >= c).astype(np.float32)
    US = (m < c).astype(np.float32)
    masks = np.concatenate([UI, LS, LI, US], axis=1)
    nlm = np.zeros((2, 7, 128, 128), np.float32)
    for lev in range(7):
        b = 1 << lev
        same = (m // (2 * b)) == (c // (2 * b))
        s_first = (m % (2 * b)) < b
        c_first = (c % (2 * b)) < b
        nlm[0, lev] = -1.0 * (same & s_first & ~c_first)
        nlm[1, lev] = -1.0 * (same & ~s_first & c_first)
    nlm = nlm.transpose(2, 0, 1, 3).reshape(128, 2 * 7 * 128)
    return np.ascontiguousarray(masks), np.ascontiguousarray(nlm)


def _prep_inputs(inp, depth):
    f = lambda a: np.ascontiguousarray(np.asarray(a, dtype=np.float32))
    x = f(inp["x"])
    w_in = f(inp["w_in"])[:depth]
    shared = {
        "w_in": w_in,
        "w_a": f(inp["w_branch_a"])[:depth], "w_b": f(inp["w_branch_b"])[:depth],
        "w_out": f(inp["w_out"])[:depth], "w_up": f(inp["w_up"])[:depth], "w_down": f(inp["w_down"])[:depth],
        "fng": np.ascontiguousarray(f(inp["final_norm_g"]).reshape(16, 128).T),
    }
    masks, nlm = _masks()
    shared["masks"] = masks
    shared["nlm"] = nlm
    per_par = []
    for par in range(2):
        rev = par == 1
        d = {}
        ab = w_in[:, :, OFF_A:OFF_A + 32].copy()
        alog = f(inp["a_log"])[:depth].copy()
        dtb = f(inp["dt_bias"])[:depth].copy()
        qcw = f(inp["qkv_conv_w"])[:depth].copy()
        fcw = f(inp["ffn_conv_w"])[:depth].copy()
        sw = f(inp["sgu_w"])[:depth].copy()
        sb = f(inp["sgu_b"])[:depth].copy()
        if rev:
            ab = np.concatenate([ab[:, :, 8:16], ab[:, :, 0:8], ab[:, :, 24:32], ab[:, :, 16:24]], axis=2)
            alog = alog[:, ::-1]
            dtb = dtb[:, ::-1]
            qcw = qcw[:, ::-1]
            fcw = fcw[:, ::-1]
            sw = sw[:, :, ::-1, ::-1]
            sb = sb[:, :, ::-1]
        d["w_ab"] = np.ascontiguousarray(ab)
        pcol = np.zeros((depth, 128, NPC), np.float32)
        pcol[:, :, PC_NMG:PC_NMG + 16] = f(inp["norm_mix_g"])[:depth].reshape(depth, 16, 128).transpose(0, 2, 1)
        pcol[:, :, PC_NFG:PC_NFG + 16] = f(inp["norm_ffn_g"])[:depth].reshape(depth, 16, 128).transpose(0, 2, 1)
        pcol[:, :, PC_QCW:PC_QCW + 120] = qcw.reshape(depth, 5, 24, 128).transpose(0, 3, 2, 1).reshape(depth, 128, 120)
        pcol[:, :, PC_FCW:PC_FCW + 264] = fcw.reshape(depth, 3, 88, 128).transpose(0, 3, 2, 1).reshape(depth, 128, 264)
        pcol[:, :, PC_FCB:PC_FCB + 88] = f(inp["ffn_conv_b"])[:depth].reshape(depth, 88, 128).transpose(0, 2, 1)
        pcol[:, :, PC_GNG] = f(inp["gdn_norm_g"])[:depth]
        d["pcol"] = pcol
        pbc = np.zeros((depth, 128, 32), np.float32)
        pbc[:, :, 0:16] = alog.reshape(depth, 1, 16)
        pbc[:, :, 16:32] = dtb.reshape(depth, 1, 16)
        d["pbc"] = pbc
        d["sgu_ln"] = np.ascontiguousarray(np.broadcast_to(
            np.concatenate([f(inp["sgu_ln_g"])[:depth], f(inp["sgu_ln_b"])[:depth]], axis=1)[:, None, :], (depth, 128, 2048)))
        d["sgu_bs"] = np.ascontiguousarray(np.broadcast_to(sb.reshape(depth, 1, 1024), (depth, 128, 1024)))
        d["sgu_wT"] = np.ascontiguousarray(sw.transpose(0, 3, 1, 2).reshape(depth, 128, 1024))
        fl = np.zeros((128, 2), np.float32)
        fl[:, 1 - par] = 1.0
        d["flags"] = fl
        per_par.append(d)
    in_maps = []
    for core in range(8):
        b, par = core // 2, core % 2
        xx = x[b, par * T:(par + 1) * T]
        if par == 1:
            xx = xx[::-1]
        xt = np.ascontiguousarray(xx.T.reshape(16, 128, T).transpose(1, 0, 2))
        m = dict(shared)
        m.update(per_par[par])
        m["x"] = xt
        in_maps.append(m)
    return in_maps


def _assemble(res):
    out = np.zeros((BATCH, SEQ, D_MODEL), np.float32)
    for core in range(8):
        b, par = core // 2, core % 2
        y = np.asarray(res[core]["y"])
        yt = y.transpose(1, 0, 2).reshape(D_MODEL, T).T
        if par == 1:
            yt = yt[::-1]
        out[b, par * T:(par + 1) * T] = yt
    return out


_NC_CACHE = {}


def kernel(**inputs):
    depth = 4
    if depth not in _NC_CACHE:
        _NC_CACHE[depth] = build(depth)[0]
    nc = _NC_CACHE[depth]
    in_maps = _prep_inputs(inputs, depth)
    res = run_bass_kernel_spmd(nc, in_maps, core_ids=list(range(8)))
    return _assemble(res.results)
```

```python
import os
import numpy as np
import concourse.bass as bass
import concourse.mybir as mybir
from concourse.bass_utils import run_bass_kernel_spmd

F32 = mybir.dt.float32
BF16 = mybir.dt.bfloat16
AF = mybir.ActivationFunctionType
ALU = mybir.AluOpType

D_MODEL = 2048
SEQ = 2048
BATCH = 4
T = 1024
NCK = 16
D_FF = 5632
N_IN = 10272
OFF_Z, OFF_A, OFF_B, OFF_U, OFF_V, OFF_GA, OFF_GB = 3072, 4096, 4112, 4128, 5152, 6176, 8224
EPS = 1e-6
NPC = 16 + 16 + 120 + 264 + 88 + 1
PC_NMG, PC_NFG, PC_QCW, PC_FCW, PC_FCB, PC_GNG = 0, 16, 32, 152, 416, 504
NS_DMA = 8


class Op:
    __slots__ = ("eng", "fn", "deps", "signal", "ticket", "is_dma", "idx")


class Prog:
    ENGS = ["pe", "act", "dve", "pool", "sp"]

    def __init__(self, nc):
        self.nc = nc
        self.ops = []
        self.last_w = {}
        self.readers = {}

    def op(self, eng, fn, reads=(), writes=(), is_dma=False):
        o = Op()
        o.eng, o.fn, o.is_dma, o.signal, o.ticket = eng, fn, is_dma, False, None
        o.idx = len(self.ops)
        writes = list(writes) + [r for r in reads if r[0] == "ps" and r not in writes]
        deps = set()
        for r in reads:
            w = self.last_w.get(r)
            if w is not None:
                deps.add(w)
        for r in writes:
            w = self.last_w.get(r)
            if w is not None:
                deps.add(w)
            for rd in self.readers.get(r, ()):
                deps.add(rd)
        for r in reads:
            self.readers.setdefault(r, []).append(o.idx)
        for r in writes:
            self.last_w[r] = o.idx
            self.readers[r] = []
        best = {}
        red = set()
        for d in deps:
            od = self.ops[d]
            if od.is_dma:
                red.add(d)
            else:
                if od.eng == "pe" and eng == "pe" and not is_dma:
                    continue
                if od.eng not in best or best[od.eng] < d:
                    best[od.eng] = d
        red.update(best.values())
        o.deps = red
        self.ops.append(o)
        return o

    def dma(self, q, out, in_, reads=(), writes=()):
        return self.op(q, lambda e: e.dma_start(out=out, in_=in_), reads, writes, is_dma=True)

    def emit(self):
        nc = self.nc
        ops = self.ops
        sem = {e: nc.alloc_semaphore("sem_" + e) for e in self.ENGS}
        dsem = {e: [nc.alloc_semaphore("dsem_%s_%d" % (e, i)) for i in range(NS_DMA)] for e in ("pool", "sp", "act")}
        dcount = {e: 0 for e in dsem}
        dlast = {e: [None] * NS_DMA for e in dsem}
        for o in ops:
            for d in o.deps:
                ops[d].signal = True
        cnt = {e: 0 for e in self.ENGS}
        for o in ops:
            if o.is_dma:
                k = dcount[o.eng]
                dcount[o.eng] += 1
                slot = k % NS_DMA
                o.ticket = (dsem[o.eng][slot], 16 * (k // NS_DMA + 1))
                if dlast[o.eng][slot] is not None:
                    o.deps.add(dlast[o.eng][slot])
                dlast[o.eng][slot] = o.idx
            elif o.signal:
                cnt[o.eng] += 1
                o.ticket = (sem[o.eng], cnt[o.eng])
        per = {e: [] for e in self.ENGS}
        for o in ops:
            per[o.eng].append(o)
        self.stats = {e: len(per[e]) for e in per}
        self.stats["sig"] = dict(cnt)

        def mk(ename):
            def body(e):
                waited = {}
                for o in per[ename]:
                    for d in sorted(o.deps):
                        s, v = ops[d].ticket
                        key = id(s)
                        if waited.get(key, 0) < v:
                            e.wait_ge(s, v)
                            waited[key] = v
                    inst = o.fn(e)
                    if inst is None:
                        continue
                    if o.is_dma:
                        inst.then_inc(o.ticket[0], 16)
                    elif o.signal:
                        inst.then_inc(o.ticket[0], 1)
            return body

        with nc.Block() as block:
            block.tensor(mk("pe"))
            block.scalar(mk("act"))
            block.vector(mk("dve"))
            block.gpsimd(mk("pool"))
            block.sync(mk("sp"))


class Buf:
    GR = 128

    def __init__(self, nc, name, n, dt):
        self.name = name
        self.n = n
        self.t = nc.alloc_sbuf_tensor(name, [128, n], dt)

    def k(self, s, n):
        return [(self.name, g) for g in range(s // self.GR, (s + n - 1) // self.GR + 1)]

    def v(self, s, n):
        return self.t[:, s:s + n]


def build(depth=4, dbg=None, rg=None, stage=99, nheads=8, noex=False, hstage=9):
    nc = bass.Bass("TRN2", target_bir_lowering=False)
    P = Prog(nc)
    dt_in = lambda name, shape: nc.dram_tensor(name, list(shape), F32, kind="ExternalInput").ap()
    x_d = dt_in("x", [128, NCK, T])
    w_in_d = dt_in("w_in", [depth, D_MODEL, N_IN])
    w_ab_d = dt_in("w_ab", [depth, D_MODEL, 32])
    w_a_d = dt_in("w_a", [depth, 1024, D_MODEL])
    w_b_d = dt_in("w_b", [depth, 1024, D_MODEL])
    w_out_d = dt_in("w_out", [depth, D_MODEL, D_MODEL])
    w_up_d = dt_in("w_up", [depth, D_MODEL, 2 * D_FF])
    w_down_d = dt_in("w_down", [depth, D_FF, D_MODEL])
    pcol_d = dt_in("pcol", [depth, 128, NPC])
    pbc_d = dt_in("pbc", [depth, 128, 32])
    sgu_ln_d = dt_in("sgu_ln", [depth, 128, 2048])
    sgu_bs_d = dt_in("sgu_bs", [depth, 128, 1024])
    sgu_wT_d = dt_in("sgu_wT", [depth, 128, 1024])
    fng_d = dt_in("fng", [128, NCK])
    flags_d = dt_in("flags", [128, 2])
    masks_d = dt_in("masks", [128, 4 * 128])
    nlm_d = dt_in("nlm", [128, 2 * 7 * 128])
    y_d = nc.dram_tensor("y", [128, NCK, T], F32, kind="ExternalOutput").ap()
    dbg_d = None
    if dbg:
        dbg_d = nc.dram_tensor("dbg", [128, dbg], F32, kind="ExternalOutput").ap()
    cc_h_src = nc.dram_tensor("cc_h_src", [128, 32], BF16)
    cc_h_dst = nc.dram_tensor("cc_h_dst", [256, 32], BF16)
    cc_s_src = [nc.dram_tensor("cc_s_src%d" % i, [128, 128], F32) for i in range(2)]
    cc_s_dst = [nc.dram_tensor("cc_s_dst%d" % i, [256, 128], F32) for i in range(2)]
    RG = rg or [[0, 1], [2, 3], [4, 5], [6, 7]]

    XS = Buf(nc, "XS", NCK * T, F32)
    HW = 1028
    HB = Buf(nc, "HB", NCK * HW, BF16)
    BIG = Buf(nc, "BIG", 24 * T, BF16)
    WS = [Buf(nc, "WS%d" % i, 4096, BF16) for i in range(2)]
    TW = 1032
    TA = Buf(nc, "TA", TW, F32)
    TB = Buf(nc, "TB", TW, F32)
    TC = Buf(nc, "TC", TW, F32)
    OB = Buf(nc, "OB", TW, F32)
    PG = Buf(nc, "PG", 25 * 128, F32)
    NET = Buf(nc, "NET", 7 * 128, F32)
    NLM = Buf(nc, "NLM", 2 * 7 * 128, BF16)
    PT = Buf(nc, "PT", 8 * 128, BF16)
    SS = Buf(nc, "SS", 3 * 128, F32)
    CF = Buf(nc, "CF", 6 * 128, F32)
    CB = Buf(nc, "CB", 2 * 128, BF16)
    PCOL = Buf(nc, "PCOL", NPC + 7, F32)
    PBC = Buf(nc, "PBC", 64, F32)
    FNG = Buf(nc, "FNG", 16, F32)
    FLG = Buf(nc, "FLG", 2, F32)
    WT = Buf(nc, "WT", 1024, BF16)
    HX = Buf(nc, "HX", 2 * 32 + 32, BF16)
    PS = nc.alloc_psum_tensor("ps", [128, 8, 512], F32)

    def ps(b, s=0, n=512):
        return PS[:, b, s:s + n]

    def psk(b):
        return [("ps", b)]

    def xs(c, s=0, n=T):
        return XS.v(c * T + s, n)

    def xsk(c, s=0, n=T):
        return XS.k(c * T + s, n)

    def hb(c, s=0, n=T):
        return HB.v(c * HW + s, n)

    def hbk(c, s=0, n=T):
        return [("HB", c, (s + i) // 512) for i in range(0, n, 512)] if s < 1024 else [("HBh", c)]

    def big(i, s=0, n=T):
        return BIG.v(i * T + s, n)

    def bigk(i, s=0, n=T):
        return BIG.k(i * T + s, n)

    UI, LS, LI, US, IDF, ONF = [(CF.v(i * 128, 128), CF.k(i * 128, 128)) for i in range(6)]
    IDB, ONB = [(CB.v(i * 128, 128), CB.k(i * 128, 128)) for i in range(2)]

    def pc(off, n=1):
        return PCOL.v(off, n)

    PCK = [("PCOL", 0)]

    def mm(out, lhsT, rhs, start, stop, R, W):
        P.op("pe", lambda e: e.matmul(out, lhsT, rhs, start=start, stop=stop), R, W)

    def act(out, in_, func, R, W, scale=None, bias=None):
        kw = {}
        if scale is not None:
            kw["scale"] = scale
        if bias is not None:
            kw["bias"] = bias
        P.op("act", lambda e: e.activation(out, in_, func, **kw), R, W)

    def tt(out, a, b, op, R, W, eng="dve"):
        P.op(eng, lambda e: e.tensor_tensor(out, a, b, op), R, W)

    def ts(out, in0, s1, s2, op0, op1, R, W, eng="dve"):
        if op1 is None:
            P.op(eng, lambda e: e.tensor_scalar(out, in0, s1, None, op0), R, W)
        else:
            P.op(eng, lambda e: e.tensor_scalar(out, in0, s1, s2, op0, op1), R, W)

    def stt(out, in0, sc, in1, op0, op1, R, W):
        P.op("dve", lambda e: e.scalar_tensor_tensor(out, in0, sc, in1, op0, op1), R, W)

    def cp(out, in_, R, W, eng="dve"):
        if eng == "act":
            P.op("act", lambda e: e.activation(out, in_, AF.Copy), R, W)
        else:
            P.op(eng, lambda e: e.tensor_copy(out, in_), R, W)

    def recip(out, in_, R, W):
        P.op("dve", lambda e: e.reciprocal(out, in_), R, W)

    def memset(ap, val, W, eng="dve"):
        P.op(eng, lambda e: e.memset(ap, val), (), W)

    ws_i = [0]

    def wslab():
        b = WS[ws_i[0] % 2]
        ws_i[0] += 1
        return b

    def wload(slab, off, src_ap, kc, ncols):
        dst = slab.v(off, kc * ncols).rearrange("p (k n) -> p k n", k=kc)
        src = src_ap.rearrange("(k p) n -> p k n", p=128)
        P.dma("pool", dst, src, (), slab.k(off, kc * ncols))

    def wv(slab, off, kc, ncols, k, c0=0, n=128):
        return slab.v(off + k * ncols + c0, n)

    P.dma("sp", CF.v(0, 512), masks_d[:, :], (), CF.k(0, 512))
    P.dma("pool", NLM.v(0, 1792), nlm_d[:, :], (), NLM.k(0, 1792))
    P.dma("sp", FNG.v(0, 16), fng_d[:, :], (), FNG.k(0, 16))
    P.dma("sp", FLG.v(0, 2), flags_d[:, :], (), FLG.k(0, 2))
    for c in range(NCK):
        P.dma("sp", xs(c), x_d[:, c, :], (), xsk(c))
    tt(IDF[0], UI[0], LI[0], ALU.mult, UI[1] + LI[1], IDF[1])
    memset(ONF[0], 1.0, ONF[1])
    cp(IDB[0], IDF[0], IDF[1], IDB[1])
    cp(ONB[0], ONF[0], ONF[1], ONB[1])
    memset(TA.v(0, TW), 0.0, TA.k(0, TW))
    memset(TB.v(0, TW), 0.0, TB.k(0, TW))
    memset(TC.v(0, TW), 0.0, TC.k(0, TW))
    memset(OB.v(0, TW), 0.0, OB.k(0, TW))
    memset(HB.v(0, NCK * HW), 0.0, [k for c in range(NCK) for k in hbk(c) + hbk(c, 1024, 4)])

    psrot = [0]

    def psb(n=1):
        b = psrot[0]
        psrot[0] = (psrot[0] + n) % 6
        if b + n > 6:
            b = 0
            psrot[0] = n % 6
        return b

    def rmsnorm(goff):
        for tb in range(2):
            s0 = tb * 512
            b = psb()
            for c in range(NCK):
                sq = PT.v((c % 2) * 512, 512)
                sqk = PT.k((c % 2) * 512, 512)
                act(sq, xs(c, s0, 512), AF.Square, xsk(c, s0, 512), sqk)
                mm(ps(b), ONB[0], sq, c == 0, c == NCK - 1, ONB[1] + sqk, psk(b))
            r = TA.v(0, 512)
            rk = TA.k(0, 512)
            act(r, ps(b), AF.Ln, psk(b), rk, scale=1.0 / D_MODEL, bias=EPS)
            act(r, r, AF.Exp, rk, rk, scale=-0.5)
            for c in range(NCK):
                stt(hb(c, s0, 512), xs(c, s0, 512), pc(goff + c), r, ALU.mult, ALU.mult,
                    xsk(c, s0, 512) + PCK + rk, hbk(c, s0, 512))

    def halo_exchange():
        src = HX.v(64, 32).rearrange("p (c t) -> p c t", t=2)
        hsrc = HB.t[:, :].rearrange("p (c w) -> p c w", w=HW)[:, :, 1022:1024]
        cp(src, hsrc, [k for c in range(NCK) for k in hbk(c, 512, 512)], HX.k(64, 32))
        P.dma("pool", cc_h_src.ap(), HX.v(64, 32), HX.k(64, 32), [("cc_h_src",)])
        P.op("pool", lambda g: g.collective_compute("AllGather", ALU.bypass, replica_groups=RG,
                                                    ins=[cc_h_src.ap().opt()], outs=[cc_h_dst.ap().opt()]),
             [("cc_h_src",)], [("cc_h_dst",)])
        P.dma("pool", HX.v(0, 32), cc_h_dst.ap()[0:128, :], [("cc_h_dst",)], HX.k(0, 32))
        P.dma("pool", HX.v(32, 32), cc_h_dst.ap()[128:256, :], [("cc_h_dst",)], HX.k(32, 32))
        ts(HX.v(0, 32), HX.v(0, 32), FLG.v(0, 1), None, ALU.mult, None, HX.k(0, 32) + FLG.k(0, 2), HX.k(0, 32))
        stt(HX.v(0, 32), HX.v(32, 32), FLG.v(1, 1), HX.v(0, 32), ALU.mult, ALU.add,
            HX.k(0, 64) + FLG.k(0, 2), HX.k(0, 32))
        pv = HX.v(0, 32).rearrange("p (c t) -> p c t", t=2)
        hdst = HB.t[:, :].rearrange("p (c w) -> p c w", w=HW)
        hk = [k for c in range(NCK) for k in hbk(c, 1024, 4)]
        cp(hdst[:, :, 1024:1025], pv[:, :, 1:2], HX.k(0, 32), hk)
        cp(hdst[:, :, 1025:1026], pv[:, :, 0:1], HX.k(0, 32), hk)

    def proj_fm_g(slab, off, ncols, col0, dst, dk, halo, evac="act", func=None):
        b = psb(2)
        for tb in range(2):
            for k in range(NCK):
                mm(ps(b + tb), wv(slab, off, NCK, ncols, k, col0), hb(k, tb * 512, 512), k == 0, k == NCK - 1,
                   slab.k(off, NCK * ncols) + hbk(k, tb * 512, 512), psk(b + tb))
        base = halo
        for tb in range(2):
            o = dst(base + tb * 512, 512)
            if func is not None:
                act(o[0], ps(b + tb), func, psk(b + tb), o[1])
            elif evac == "act":
                cp(o[0], ps(b + tb), psk(b + tb), o[1], eng="act")
            else:
                cp(o[0], ps(b + tb), psk(b + tb), o[1])
        if halo:
            b2 = psb()
            for k in range(NCK):
                mm(ps(b2, 0, 32), wv(slab, off, NCK, ncols, k, col0), hb(k, 996, 32), k == 0, k == NCK - 1,
                   slab.k(off, NCK * ncols) + hbk(k, 512, 512) + hbk(k, 1024, 4), psk(b2))
            o = dst(base + 1024, halo)
            cp(o[0], ps(b2, 28, halo), psk(b2), o[1])
        yield

    def proj_fm(*a, **kw):
        run(proj_fm_g(*a, **kw))

    def bufdst(buf):
        return lambda s, n: (buf.v(s, n), buf.k(s, n))

    def bigdst(i):
        return lambda s, n: (big(i, s, n), bigk(i, s, n))

    def l2n_g(accb, dsti, norm):
        a, ak = accb.v(0, T), accb.k(0, T)
        if not norm:
            act(big(dsti), a, AF.Silu, ak, bigk(dsti))
            yield
            return
        act(a, a, AF.Silu, ak, ak)
        yield
        b = psb(2)
        for tb in range(2):
            sq = big(dsti, tb * 512, 512)
            sqk = bigk(dsti, tb * 512, 512)
            act(sq, accb.v(tb * 512, 512), AF.Square, ak, sqk)
            mm(ps(b + tb), ONB[0], sq, True, True, ONB[1] + sqk, psk(b + tb))
        r, rk = TA.v(0, T), TA.k(0, T)
        for tb in range(2):
            act(TA.v(tb * 512, 512), ps(b + tb), AF.Ln, psk(b + tb), rk, bias=EPS)
        act(r, r, AF.Exp, rk, rk, scale=-0.5)
        yield
        tt(big(dsti), a, r, ALU.mult, ak + rk, bigk(dsti))
        yield

    def pg(i, n=1):
        return PG.v(i * 128, n * 128), PG.k(i * 128, n * 128)
    G_AB, G_G, G_BETA, G_NB, G_GC, G_EGC, G_EGR, G_DCH = 0, 2, 3, 4, 5, 6, 7, 8
    G_LG, G_DT, G_DTS, G_DTI, G_AT, G_Q, G_T0, G_T1, G_TT0, G_TT1, G_EGB = range(9, 20)
    def pco(n, j):
        s = 19 * T + (n * 5 + j) * 128
        return BIG.v(s, 128), BIG.k(s, 128)
    QKT_S = 0.08838834764831845

    def gdn_gates(l):
        slab = wslab()
        wload(slab, 0, w_ab_d[l], NCK, 32)
        ab, abk = pg(G_AB, 2)
        for n in range(8):
            b = psb()
            for k in range(NCK):
                mm(ps(b, 0, 32), hb(k, n * 128, 128), wv(slab, 0, NCK, 32, k, 0, 32), k == 0, k == NCK - 1,
                   slab.k(0, 512) + hbk(k, n * 128, 128), psk(b))
            cp(PG.v(G_AB * 128 + n * 32, 32), ps(b, 0, 32), psk(b), abk)
        ab3 = ab.rearrange("p (n c) -> p n c", c=32)
        g3 = pg(G_G)[0].rearrange("p (c n) -> p c n", n=8)
        gk = pg(G_G)[1]
        for n in range(8):
            tt(g3[:, :, n], ab3[:, n, 0:16], PBC.v(16, 16), ALU.add, abk + PBC.k(0, 64), gk)
        gfl = pg(G_G)[0]
        act(gfl, gfl, AF.Exp, gk, gk)
        act(gfl, gfl, AF.Ln, gk, gk, bias=1.0)
        for n in range(8):
            tt(g3[:, :, n], g3[:, :, n], PBC.v(32, 16), ALU.mult, gk + PBC.k(0, 64), gk)
        be3 = pg(G_BETA)[0].rearrange("p (c n) -> p c n", n=8)
        bek = pg(G_BETA)[1]
        for n in range(8):
            act(be3[:, :, n], ab3[:, n, 16:32], AF.Sigmoid, abk, bek)
        ts(pg(G_NB)[0], pg(G_BETA)[0], -1.0, None, ALU.mult, None, bek, pg(G_NB)[1])
        b = psb()
        mm(ps(b, 0, 64), UI[0], PG.v(G_G * 128, 64), True, True, UI[1] + gk, psk(b))
        mm(ps(b, 64, 64), LI[0], PG.v(G_G * 128 + 64, 64), True, True, LI[1] + gk, psk(b))
        b2 = psb()
        mm(ps(b2, 0, 128), ONF[0], gfl, True, True, ONF[1] + gk, psk(b2))
        cp(pg(G_GC)[0], ps(b, 0, 128), psk(b), pg(G_GC)[1])
        act(pg(G_EGC)[0], ps(b, 0, 128), AF.Exp, psk(b), pg(G_EGC)[1])
        tt(pg(G_EGR)[0], ps(b2, 0, 128), pg(G_GC)[0], ALU.subtract, psk(b2) + pg(G_GC)[1], pg(G_EGR)[1])
        act(pg(G_EGR)[0], pg(G_EGR)[0], AF.Exp, pg(G_EGR)[1], pg(G_EGR)[1])
        act(pg(G_DCH)[0], ps(b2, 0, 128), AF.Exp, psk(b2), pg(G_DCH)[1])

    def pgb(g0):
        return PG.v(g0 * 128, 256).bitcast(BF16), PG.k(g0 * 128, 256)

    def netb(i):
        return NET.v(i * 256, 256).bitcast(BF16), NET.k(i * 256, 256)

    ATH = [pgb(12), pgb(14)]
    def tcb(i):
        return TC.v(i * 256, 256).bitcast(BF16), TC.k(i * 256, 256)

    def wtb(i):
        return WT.v(i * 512, 512), WT.k(i * 512, 512)

    CSET = [(pgb(16), pgb(18), pgb(22), netb(0), netb(1), netb(2)),
            (tcb(0), tcb(1), tcb(2), tcb(3), wtb(0), wtb(1))]

    def c4(buf, j):
        return buf[0][:, j * 128:(j + 1) * 128]

    FSTOP = 999999999
    fcount = [0]

    def fgate():
        fcount[0] += 1
        return fcount[0] <= FSTOP

    def prep_front(d, h, Q_i, K_i, batch):
        TRI = UI if d == 0 else LI
        MINC, MSTR = (UI, US) if d == 0 else (LI, LS)
        ATh = ATH[batch]
        for j in range(4):
            n = batch * 4 + j
            col = (d * 8 + h) * 8 + n
            c0 = n * 128
            gcol = PG.v(G_G * 128 + col, 1)
            becol = PG.v(G_BETA * 128 + col, 1)
            gccol = PG.v(G_GC * 128 + col, 1)
            Lgk = pg(9)[1]
            Lhl = PG.v(9 * 128, 128).bitcast(BF16)
            Lh, Ll = Lhl[:, 0:128], Lhl[:, 128:256]
            if fgate():
                ts(Lh, TRI[0], gcol, None, ALU.mult, None, TRI[1] + pg(G_G)[1], Lgk)
            if fgate():
                stt(Ll, TRI[0], gcol, Lh, ALU.mult, ALU.subtract, TRI[1] + pg(G_G)[1] + Lgk, Lgk)
            b = psb()
            if fgate():
                mm(ps(b, 0, 128), ONB[0], Lh, True, False, ONB[1] + Lgk, psk(b))
                mm(ps(b, 0, 128), ONB[0], Ll, False, True, ONB[1] + Lgk, psk(b))
            EGB, EGBk = pg(11)
            if fgate():
                act(EGB, ps(b, 0, 128), AF.Exp, psk(b), EGBk)
            DT, DTk = pg(10)
            if fgate():
                ts(DT, ps(b, 0, 128), gccol, 0.0, ALU.subtract, ALU.min, psk(b) + pg(G_GC)[1], DTk)
            if fgate():
                act(DT, DT, AF.Exp, DTk, DTk)
            QgT, QgTk = pco(n, 2)
            if fgate():
                stt(QgT, big(Q_i, c0, 128), QKT_S, EGB, ALU.mult, ALU.mult, bigk(Q_i, c0, 128) + EGBk, QgTk)
            b = psb()
            if fgate():
                mm(ps(b, 0, 128), big(K_i, c0, 128), big(K_i, c0, 128), True, True, bigk(K_i, c0, 128), psk(b))
            if fgate():
                mm(ps(b, 128, 128), big(K_i, c0, 128), big(Q_i, c0, 128), True, True,
                   bigk(K_i, c0, 128) + bigk(Q_i, c0, 128), psk(b))
            if fgate():
                tt(DT, DT, MINC[0], ALU.mult, DTk + MINC[1], DTk)
            QKT, QKTk = pco(n, 3)
            if fgate():
                stt(QKT, ps(b, 128, 128), QKT_S, DT, ALU.mult, ALU.mult, psk(b) + DTk, QKTk)
            if fgate():
                tt(DT, DT, MSTR[0], ALU.mult, DTk + MSTR[1], DTk)
            if fgate():
                stt(c4(ATh, j), ps(b, 0, 128), becol, DT, ALU.mult, ALU.mult, psk(b) + pg(G_BETA)[1] + DTk, ATh[1])
                yield

    def prep_chain(d, batch):
        ATh = ATH[batch]
        NETH, QH, TH, TL, TTH, TTL = CSET[batch]
        r4 = lambda buf: buf[0].rearrange("p (j c) -> p j c", j=4)
        for lev in range(7):
            mask = NLM.v((d * 7 + lev) * 128, 128).unsqueeze(1).broadcast_to([128, 4, 128])
            tt(r4(NETH), r4(ATh), mask, ALU.mult, ATh[1] + NLM.k((d * 7 + lev) * 128, 128), NETH[1], eng=("pool" if d == 0 else "dve"))
            bq = psb()
            for j in range(4):
                o = ps(bq, j * 128, 128)
                if lev == 0:
                    mm(o, IDB[0], IDB[0], True, False, IDB[1], psk(bq))
                    mm(o, c4(NETH, j), IDB[0], False, True, NETH[1] + IDB[1], psk(bq))
                else:
                    mm(o, IDB[0], IDB[0], True, False, IDB[1], psk(bq))
                    mm(o, c4(NETH, j), c4(TH, j), False, False, NETH[1] + TH[1], psk(bq))
                    mm(o, c4(NETH, j), c4(TL, j), False, True, NETH[1] + TL[1], psk(bq))
            if lev == 0:
                cp(TH[0], ps(bq), psk(bq), TH[1], eng="act")
                tt(TL[0], ps(bq), TH[0], ALU.subtract, psk(bq) + TH[1], TL[1])
                for j in range(4):
                    tt(c4(TTH, j), c4(NETH, j), IDB[0], ALU.add, NETH[1] + IDB[1], TTH[1])
                memset(TTL[0], 0.0, TTL[1])
                yield
                continue
            cp(QH[0], ps(bq), psk(bq), QH[1], eng="act")
            yield
            last = lev == 6
            if not last:
                bt = psb()
                for j in range(4):
                    o = ps(bt, j * 128, 128)
                    mm(o, c4(TTH, j), c4(QH, j), True, False, TTH[1] + QH[1], psk(bt))
                    mm(o, c4(TTL, j), c4(QH, j), False, True, TTL[1] + QH[1], psk(bt))
            btt = psb()
            for j in range(4):
                o = ps(btt, j * 128, 128)
                mm(o, c4(QH, j), c4(TTH, j), True, False, TTH[1] + QH[1], psk(btt))
                mm(o, c4(QH, j), c4(TTL, j), False, True, TTL[1] + QH[1], psk(btt))
            if not last:
                cp(TH[0], ps(bt), psk(bt), TH[1], eng="act")
                tt(TL[0], ps(bt), TH[0], ALU.subtract, psk(bt) + TH[1], TL[1])
            cp(TTH[0], ps(btt), psk(btt), TTH[1], eng="act")
            if not last:
                tt(TTL[0], ps(btt), TTH[0], ALU.subtract, psk(btt) + TTH[1], TTL[1])
            yield

    def prep_tail(d, h, K_i, V_i, batch):
        TTH = CSET[batch][4]
        for j in range(4):
            n = batch * 4 + j
            col = (d * 8 + h) * 8 + n
            c0 = n * 128
            becol = PG.v(G_BETA * 128 + col, 1)
            b = psb()
            mm(ps(b, 0, 128), big(K_i, c0, 128), IDB[0], True, True, bigk(K_i, c0, 128) + IDB[1], psk(b))
            mm(ps(b, 128, 128), big(V_i, c0, 128), IDB[0], True, True, bigk(V_i, c0, 128) + IDB[1], psk(b))
            Kg, Kgk = PT.v((j % 2) * 256, 128), PT.k((j % 2) * 256, 128)
            Vt, Vtk = PT.v((j % 2) * 256 + 128, 128), PT.k((j % 2) * 256 + 128, 128)
            Kd, Kdk = pco(n, 4)
            ts(Kg, ps(b, 0, 128), PG.v(G_EGC * 128 + col, 1), None, ALU.mult, None, psk(b) + pg(G_EGC)[1], Kgk)
            ts(Kd, ps(b, 0, 128), PG.v(G_EGR * 128 + col, 1), None, ALU.mult, None, psk(b) + pg(G_EGR)[1], Kdk)
            cp(Vt, ps(b, 128, 128), psk(b), Vtk, eng="act")
            b = psb()
            mm(ps(b, 0, 128), c4(TTH, j), Vt, True, True, TTH[1] + Vtk, psk(b))
            mm(ps(b, 128, 128), Kg, c4(TTH, j), True, True, Kgk + TTH[1], psk(b))
            Ub, Ubk = pco(n, 1)
            ts(Ub, ps(b, 0, 128), becol, None, ALU.mult, None, psk(b) + pg(G_BETA)[1], Ubk)
            WT_, WTk = pco(n, 0)
            cp(WT_, ps(b, 128, 128), psk(b), WTk, eng="act")
            yield

    def run(*gens):
        gens = list(gens)
        while gens:
            for g in list(gens):
                try:
                    next(g)
                except StopIteration:
                    gens.remove(g)

    PSTOP = 99

    def seq(*gens):
        for g in gens:
            yield from g

    def gdn_prep(d, h, Q_i, K_i, V_i, skip_front0=False):
        if not skip_front0:
            run(prep_front(d, h, Q_i, K_i, 0))
        run(prep_chain(d, 0), seq(prep_front(d, h, Q_i, K_i, 1), prep_chain(d, 1)))
        run(prep_tail(d, h, K_i, V_i, 0), prep_tail(d, h, K_i, V_i, 1))

    def gdn_scan_gen(d, h, first_dir):
        S, Sk = SS.v(0, 128), SS.k(0, 128)
        Sb, Sbk = PT.v(384, 128), PT.k(384, 128)
        vn, vnk = PT.v(512, 128), PT.k(512, 128)
        St, Stk = PT.v(640, 128), PT.k(640, 128)
        cp(Sb, S, Sk, Sbk)
        order = range(8) if d == 0 else range(7, -1, -1)
        for n in order:
            col = (d * 8 + h) * 8 + n
            c0 = n * 128
            WT_, WTk = pco(n, 0)
            Ub, Ubk = pco(n, 1)
            QgT, QgTk = pco(n, 2)
            QKT, QKTk = pco(n, 3)
            Kd, Kdk = pco(n, 4)
            b = psb()
            mm(ps(b, 0, 128), WT_, Sb, True, True, WTk + Sbk, psk(b))
            stt(vn, ps(b, 0, 128), PG.v(G_NB * 128 + col, 1), Ub, ALU.mult, ALU.add,
                psk(b) + pg(G_NB)[1] + Ubk, vnk)
            b = psb()
            mm(ps(b, 0, 128), Sb, QgT, True, False, Sbk + QgTk, psk(b))
            mm(ps(b, 0, 128), vn, QKT, False, True, vnk + QKTk, psk(b))
            b3 = psb()
            mm(ps(b3, 0, 128), Kd, vn, True, True, Kdk + vnk, psk(b3))
            if first_dir:
                cp(OB.v(c0, 128), ps(b, 0, 128), psk(b), OB.k(c0, 128), eng="act")
            else:
                tt(OB.v(c0, 128), OB.v(c0, 128), ps(b, 0, 128), ALU.add, psk(b) + OB.k(c0, 128), OB.k(c0, 128))
            stt(S, S, PG.v(G_DCH * 128 + col, 1), ps(b3, 0, 128), ALU.mult, ALU.add,
                Sk + pg(G_DCH)[1] + psk(b3), Sk)
            cp(Sb, S, Sk, Sbk, eng="act")
            yield

    def state_exchange_issue(i):
        S, Sk = SS.v(0, 128), SS.k(0, 128)
        P.dma("pool", cc_s_src[i].ap(), S, Sk, [("cc_s_src", i)])
        P.op("pool", lambda g: g.collective_compute("AllGather", ALU.bypass, replica_groups=RG,
                                                    ins=[cc_s_src[i].ap().opt()], outs=[cc_s_dst[i].ap().opt()]),
             [("cc_s_src", i)], [("cc_s_dst", i)])
        P.dma("pool", SS.v(128, 128), cc_s_dst[i].ap()[0:128, :], [("cc_s_dst", i)], SS.k(128, 128))
        P.dma("pool", SS.v(256, 128), cc_s_dst[i].ap()[128:256, :], [("cc_s_dst", i)], SS.k(256, 128))

    def state_exchange_finish():
        S, Sk = SS.v(0, 128), SS.k(0, 128)
        ts(S, SS.v(128, 128), FLG.v(0, 1), None, ALU.mult, None, SS.k(128, 128) + FLG.k(0, 2), Sk)
        stt(S, SS.v(256, 128), FLG.v(1, 1), S, ALU.mult, ALU.add, SS.k(256, 128) + FLG.k(0, 2) + Sk, Sk)

    def conv_taps_g(pbuf, accb, woff, ntap, boff=None):
        a, ak = accb.v(0, T), accb.k(0, T)
        pk = pbuf.k(0, TW)
        if boff is None:
            ts(a, pbuf.v(0, T), pc(woff), None, ALU.mult, None, pk + PCK, ak)
        else:
            ts(a, pbuf.v(0, T), pc(woff), pc(boff), ALU.mult, ALU.add, pk + PCK, ak)
        yield
        for tap in range(1, ntap):
            stt(a, pbuf.v(tap, T), pc(woff + tap), a, ALU.mult, ALU.add, pk + PCK + ak, ak)
            yield

    def conv_taps(*a, **kw):
        run(conv_taps_g(*a, **kw))

    def sgu(l):
        P.dma("pool", WT.v(0, 1024), sgu_wT_d[l], (), WT.k(0, 1024))
        LNG, LNGk = pg(9, 8)
        LNB, LNBk = pg(17, 8)
        P.dma("sp", LNG, sgu_ln_d[l][:, 0:1024], (), LNGk)
        P.dma("sp", LNB, sgu_ln_d[l][:, 1024:2048], (), LNBk)
        BSB, BSBk = OB.v(0, T), OB.k(0, T)
        P.dma("sp", BSB, sgu_bs_d[l], (), BSBk)

        def gelu(x, xk, t1, t1k, out, outk):
            act(t1, x, AF.Square, xk, t1k)
            ts(t1, t1, 0.044715, 1.0, ALU.mult, ALU.add, t1k, t1k)
            tt(t1, t1, x, ALU.mult, t1k + xk, t1k)
            act(t1, t1, AF.Sigmoid, t1k, t1k, scale=1.5957691216057308)
            tt(out, t1, x, ALU.mult, t1k + xk, outk)

        for g in range(8):
            if g % 2 == 0:
                slab = wslab()
                wload(slab, 0, w_in_d[l][:, OFF_U + g * 128: OFF_U + g * 128 + 256], NCK, 256)
            proj_fm(slab, 0, 256, (g % 2) * 128, bufdst(TA), None, 0)
            gelu(TA.v(0, T), TA.k(0, T), TB.v(0, T), TB.k(0, T), big(g), bigk(g))
        for n in range(8):
            bq = 6
            for half in range(2):
                for q in range(2):
                    cq = half * 512 + q * 256
                    slab = wslab()
                    wload(slab, 0, w_in_d[l][:, OFF_V + cq: OFF_V + cq + 256], NCK, 256)
                    for k in range(NCK):
                        mm(ps(bq + half, q * 256, 256), hb(k, n * 128, 128), wv(slab, 0, NCK, 256, k, 0, 256),
                           k == 0, k == NCK - 1, slab.k(0, 4096) + hbk(k, n * 128, 128), psk(bq + half))
            for half in range(2):
                cp(TA.v(half * 512, 512), ps(bq + half), psk(bq + half), TA.k(0, T), eng="act")
            gelu(TA.v(0, T), TA.k(0, T), TB.v(0, T), TB.k(0, T), TC.v(0, T), TC.k(0, T))
            st, stk = PT.v(0, 8), PT.k(0, 8)
            mu = SS.v(128, 1)
            muk = SS.k(128, 8)
            P.op("dve", lambda e: e.tensor_reduce(SS.v(128, 1), TC.v(0, T), mybir.AxisListType.X, ALU.add),
                 TC.k(0, T), muk)
            ts(SS.v(129, 1), SS.v(128, 1), -1.0 / 1024, None, ALU.mult, None, muk, muk)
            ts(TC.v(0, T), TC.v(0, T), SS.v(129, 1), None, ALU.add, None, TC.k(0, T) + muk, TC.k(0, T))
            act(TB.v(0, T), TC.v(0, T), AF.Square, TC.k(0, T), TB.k(0, T))
            P.op("dve", lambda e: e.tensor_reduce(SS.v(130, 1), TB.v(0, T), mybir.AxisListType.X, ALU.add),
                 TB.k(0, T), muk)
            act(SS.v(131, 1), SS.v(130, 1), AF.Sqrt, muk, muk, scale=1.0 / 1024, bias=EPS)
            recip(SS.v(131, 1), SS.v(131, 1), muk, muk)
            stt(TC.v(0, T), TC.v(0, T), SS.v(131, 1), LNG, ALU.mult, ALU.mult, TC.k(0, T) + muk + LNGk, TC.k(0, T))
            VN, VNk = BIG.v(16 * T, T), BIG.k(16 * T, T)
            tt(VN, TC.v(0, T), LNB, ALU.add, TC.k(0, T) + LNBk, VNk)
            for half in range(2):
                b = psb()
                for gq in range(4):
                    g = half * 4 + gq
                    mm(ps(b, gq * 128, 128), BIG.v(16 * T + g * 128, 128), WT.v(g * 128, 128), True, True,
                       VNk + WT.k(0, 1024), psk(b))
                tmp, tmpk = TA.v(0, 512), TA.k(0, 512)
                tt(tmp, ps(b), OB.v(half * 512, 512), ALU.add, psk(b) + BSBk, tmpk)
                for gq in range(4):
                    g = half * 4 + gq
                    tt(big(g, n * 128, 128), big(g, n * 128, 128), TA.v(gq * 128, 128), ALU.mult,
                       bigk(g, n * 128, 128) + tmpk, bigk(g, n * 128, 128))

    def mixer(l):
        P.dma("sp", PCOL.v(0, NPC), pcol_d[l], (), PCK)
        P.dma("sp", PBC.v(0, 32), pbc_d[l], (), PBC.k(0, 64))
        act(PBC.v(32, 16), PBC.v(0, 16), AF.Exp, PBC.k(0, 64), PBC.k(0, 64))
        ts(PBC.v(32, 16), PBC.v(32, 16), -1.0, None, ALU.mult, None, PBC.k(0, 64), PBC.k(0, 64))
        rmsnorm(PC_NMG)
        if stage < 1:
            return
        halo_exchange()
        if stage < 2:
            return
        sgu(l)
        if stage < 3:
            return
        gdn_gates(l)
        if stage < 4:
            return
        Q_i, K_i, V_i = 16, 17, 18

        def head_pre(h):
            s1 = wslab()
            wload(s1, 0, w_in_d[l][:, h * 128: h * 128 + 128], NCK, 128)
            wload(s1, 2048, w_in_d[l][:, 1024 + h * 128: 1024 + h * 128 + 128], NCK, 128)
            s2 = wslab()
            wload(s2, 0, w_in_d[l][:, 2048 + h * 128: 2048 + h * 128 + 128], NCK, 128)
            wload(s2, 2048, w_in_d[l][:, OFF_Z + h * 128: OFF_Z + h * 128 + 128], NCK, 128)
            yield from proj_fm_g(s2, 2048, 128, 0, bigdst(8 + h), None, 0, func=AF.Silu)
            for j, (slab, off, dsti, norm) in enumerate(((s1, 0, Q_i, True), (s1, 2048, K_i, True), (s2, 0, V_i, False))):
                pdst = lambda s, n: (TA.v(s, n), TA.k(s, n))
                memset(TA.v(0, 2), 0.0, TA.k(0, 2))
                yield from proj_fm_g(slab, off, 128, 0, pdst, None, 2)
                yield from conv_taps_g(TA, TB, PC_QCW + (j * 8 + h) * 5, 5)
                yield from l2n_g(TB, dsti, norm)

        def gnorm_g(h):
            b = psb(2)
            for tb in range(2):
                sq, sqk = PT.v(tb * 512, 512), PT.k(tb * 512, 512)
                act(sq, OB.v(tb * 512, 512), AF.Square, OB.k(0, T), sqk)
                mm(ps(b + tb), ONB[0], sq, True, True, ONB[1] + sqk, psk(b + tb))
            for tb in range(2):
                act(TC.v(tb * 512, 512), ps(b + tb), AF.Ln, psk(b + tb), TC.k(0, T), scale=1.0 / 128, bias=EPS)
            act(TC.v(0, T), TC.v(0, T), AF.Exp, TC.k(0, T), TC.k(0, T), scale=-0.5)
            yield
            tt(TC.v(0, T), TC.v(0, T), OB.v(0, T), ALU.mult, TC.k(0, T) + OB.k(0, T), TC.k(0, T))
            yield
            stt(big(8 + h), TC.v(0, T), pc(PC_GNG), big(8 + h), ALU.mult, ALU.mult,
                TC.k(0, T) + PCK + bigk(8 + h), bigk(8 + h))
            yield

        run(head_pre(0))
        for h in range(nheads):
            memset(SS.v(0, 128), 0.0, SS.k(0, 128))
            gdn_prep(0, h, Q_i, K_i, V_i)
            run(gdn_scan_gen(0, h, True), prep_front(1, h, Q_i, K_i, 0))
            state_exchange_issue(h % 2)
            gdn_prep(1, h, Q_i, K_i, V_i, skip_front0=True)
            state_exchange_finish()
            post = seq(gdn_scan_gen(1, h, False), gnorm_g(h))
            if h + 1 < nheads:
                run(post, head_pre(h + 1))
            else:
                run(post)
        if stage < 5:
            return
        for tb in range(2):
            s0 = tb * 512

            def mg(c):
                s = 16 * T + c * 512
                return BIG.v(s, 512), BIG.k(s, 512)
            for c in range(NCK):
                sa = wslab()
                wload(sa, 0, w_a_d[l][:, c * 128:(c + 1) * 128], 8, 128)
                wload(sa, 1024, w_b_d[l][:, c * 128:(c + 1) * 128], 8, 128)
                sg = wslab()
                wload(sg, 0, w_in_d[l][:, OFF_GA + c * 128: OFF_GA + (c + 1) * 128], NCK, 128)
                wload(sg, 2048, w_in_d[l][:, OFF_GB + c * 128: OFF_GB + (c + 1) * 128], NCK, 128)
                b = psb(4)
                for k in range(8):
                    mm(ps(b), wv(sa, 0, 8, 128, k), big(8 + k, s0, 512), k == 0, k == 7,
                       sa.k(0, 1024) + bigk(8 + k, s0, 512), psk(b))
                for k in range(8):
                    mm(ps(b + 1), wv(sa, 1024, 8, 128, k), big(k, s0, 512), k == 0, k == 7,
                       sa.k(1024, 1024) + bigk(k, s0, 512), psk(b + 1))
                for k in range(NCK):
                    mm(ps(b + 2), wv(sg, 0, NCK, 128, k), hb(k, s0, 512), k == 0, k == NCK - 1,
                       sg.k(0, 2048) + hbk(k, s0, 512), psk(b + 2))
                for k in range(NCK):
                    mm(ps(b + 3), wv(sg, 2048, NCK, 128, k), hb(k, s0, 512), k == 0, k == NCK - 1,
                       sg.k(2048, 2048) + hbk(k, s0, 512), psk(b + 3))
                act(TA.v(0, 512), ps(b + 2), AF.Sigmoid, psk(b + 2), TA.k(0, 512))
                act(TA.v(512, 512), ps(b + 3), AF.Sigmoid, psk(b + 3), TA.k(512, 512))
                tt(TB.v(0, 512), ps(b), TA.v(0, 512), ALU.mult, psk(b) + TA.k(0, 512), TB.k(0, 512))
                tt(TB.v(512, 512), ps(b + 1), TA.v(512, 512), ALU.mult, psk(b + 1) + TA.k(512, 512), TB.k(512, 512))
                m, mk_ = mg(c)
                tt(m, TB.v(0, 512), TB.v(512, 512), ALU.add, TB.k(0, T), mk_)
            for co in range(NCK):
                so = wslab()
                wload(so, 0, w_out_d[l][:, co * 128:(co + 1) * 128], NCK, 128)
                b = psb()
                for k in range(NCK):
                    m, mk_ = mg(k)
                    mm(ps(b), wv(so, 0, NCK, 128, k), m, k == 0, k == NCK - 1, so.k(0, 2048) + mk_, psk(b))
                tt(xs(co, s0, 512), xs(co, s0, 512), ps(b), ALU.add, xsk(co, s0, 512) + psk(b), xsk(co, s0, 512))

    def ffn(l):
        rmsnorm(PC_NFG)
        halo_exchange()
        memset(TA.v(0, 1), 0.0, TA.k(0, 1))
        memset(TC.v(0, 1), 0.0, TC.k(0, 1))
        for qd in range(4):
            for j in range(11):
                cg = qd * 11 + j
                slab = wslab()
                wload(slab, 0, w_up_d[l][:, cg * 128:(cg + 1) * 128], NCK, 128)
                wload(slab, 2048, w_up_d[l][:, D_FF + cg * 128: D_FF + (cg + 1) * 128], NCK, 128)
                proj_fm(slab, 0, 128, 0, lambda s, n: (TA.v(s, n), TA.k(s, n)), None, 1)
                conv_taps(TA, TB, PC_FCW + cg * 3, 3, PC_FCB + cg)
                proj_fm(slab, 2048, 128, 0, lambda s, n: (TC.v(s, n), TC.k(s, n)), None, 1)
                conv_taps(TC, OB, PC_FCW + (44 + cg) * 3, 3, PC_FCB + 44 + cg)
                act(TB.v(0, T), TB.v(0, T), AF.Silu, TB.k(0, T), TB.k(0, T))
                tt(big(j), TB.v(0, T), OB.v(0, T), ALU.mult, TB.k(0, T) + OB.k(0, T), bigk(j))
            for co in range(NCK):
                slab = wslab()
                wload(slab, 0, w_down_d[l][qd * 1408:(qd + 1) * 1408, co * 128:(co + 1) * 128], 11, 128)
                for tb in range(2):
                    b = psb()
                    for k in range(11):
                        mm(ps(b), wv(slab, 0, 11, 128, k), big(k, tb * 512, 512), k == 0, k == 10,
                           slab.k(0, 1408) + bigk(k, tb * 512, 512), psk(b))
                    tt(xs(co, tb * 512, 512), xs(co, tb * 512, 512), ps(b), ALU.add,
                       xsk(co, tb * 512, 512) + psk(b), xsk(co, tb * 512, 512))

    for l in range(depth):
        mixer(l)
        if stage >= 6:
            ffn(l)

    outs = []
    for tb in range(2):
        s0 = tb * 512
        b = psb()
        for c in range(NCK):
            sq = PT.v((c % 2) * 512, 512)
            sqk = PT.k((c % 2) * 512, 512)
            act(sq, xs(c, s0, 512), AF.Square, xsk(c, s0, 512), sqk)
            mm(ps(b), ONB[0], sq, c == 0, c == NCK - 1, ONB[1] + sqk, psk(b))
        r, rk = TA.v(0, 512), TA.k(0, 512)
        act(r, ps(b), AF.Ln, psk(b), rk, scale=1.0 / D_MODEL, bias=EPS)
        act(r, r, AF.Exp, rk, rk, scale=-0.5)
        for c in range(NCK):
            stt(xs(c, s0, 512), xs(c, s0, 512), FNG.v(c, 1), r, ALU.mult, ALU.mult,
                xsk(c, s0, 512) + FNG.k(0, 16) + rk, xsk(c, s0, 512))
    for c in range(NCK):
        outs.append(P.dma("sp", y_d[:, c, :], xs(c), xsk(c), [("y", c)]))
    P.op("sp", lambda e: None, [("y", c) for c in range(NCK)], ())
    P.emit()
    return nc, P


def _masks():
    i = np.arange(128)
    m, c = i[:, None], i[None, :]
    UI = (m <= c).astype(np.float32)
    LS = (m > c).astype(np.float32)
    LI = (m >= c).astype(np.float32)
    US = (m < c).astype(np.float32)
    masks = np.concatenate([UI, LS, LI, US], axis=1)
    nlm = np.zeros((2, 7, 128, 128), np.float32)
    for lev in range(7):
        b = 1 << lev
        same = (m // (2 * b)) == (c // (2 * b))
        s_first = (m % (2 * b)) < b
        c_first = (c % (2 * b)) < b
        nlm[0, lev] = -1.0 * (same & s_first & ~c_first)
        nlm[1, lev] = -1.0 * (same & ~s_first & c_first)
    nlm = nlm.transpose(2, 0, 1, 3).reshape(128, 2 * 7 * 128)
    return np.ascontiguousarray(masks), np.ascontiguousarray(nlm)


def _prep_inputs(inp, depth):
    f = lambda a: np.ascontiguousarray(np.asarray(a, dtype=np.float32))
    x = f(inp["x"])
    w_in = f(inp["w_in"])[:depth]
    shared = {
        "w_in": w_in,
        "w_a": f(inp["w_branch_a"])[:depth], "w_b": f(inp["w_branch_b"])[:depth],
        "w_out": f(inp["w_out"])[:depth], "w_up": f(inp["w_up"])[:depth], "w_down": f(inp["w_down"])[:depth],
        "fng": np.ascontiguousarray(f(inp["final_norm_g"]).reshape(16, 128).T),
    }
    masks, nlm = _masks()
    shared["masks"] = masks
    shared["nlm"] = nlm
    per_par = []
    for par in range(2):
        rev = par == 1
        d = {}
        ab = w_in[:, :, OFF_A:OFF_A + 32].copy()
        alog = f(inp["a_log"])[:depth].copy()
        dtb = f(inp["dt_bias"])[:depth].copy()
        qcw = f(inp["qkv_conv_w"])[:depth].copy()
        fcw = f(inp["ffn_conv_w"])[:depth].copy()
        sw = f(inp["sgu_w"])[:depth].copy()
        sb = f(inp["sgu_b"])[:depth].copy()
        if rev:
            ab = np.concatenate([ab[:, :, 8:16], ab[:, :, 0:8], ab[:, :, 24:32], ab[:, :, 16:24]], axis=2)
            alog = alog[:, ::-1]
            dtb = dtb[:, ::-1]
            qcw = qcw[:, ::-1]
            fcw = fcw[:, ::-1]
            sw = sw[:, :, ::-1, ::-1]
            sb = sb[:, :, ::-1]
        d["w_ab"] = np.ascontiguousarray(ab)
        pcol = np.zeros((depth, 128, NPC), np.float32)
        pcol[:, :, PC_NMG:PC_NMG + 16] = f(inp["norm_mix_g"])[:depth].reshape(depth, 16, 128).transpose(0, 2, 1)
        pcol[:, :, PC_NFG:PC_NFG + 16] = f(inp["norm_ffn_g"])[:depth].reshape(depth, 16, 128).transpose(0, 2, 1)
        pcol[:, :, PC_QCW:PC_QCW + 120] = qcw.reshape(depth, 5, 24, 128).transpose(0, 3, 2, 1).reshape(depth, 128, 120)
        pcol[:, :, PC_FCW:PC_FCW + 264] = fcw.reshape(depth, 3, 88, 128).transpose(0, 3, 2, 1).reshape(depth, 128, 264)
        pcol[:, :, PC_FCB:PC_FCB + 88] = f(inp["ffn_conv_b"])[:depth].reshape(depth, 88, 128).transpose(0, 2, 1)
        pcol[:, :, PC_GNG] = f(inp["gdn_norm_g"])[:depth]
        d["pcol"] = pcol
        pbc = np.zeros((depth, 128, 32), np.float32)
        pbc[:, :, 0:16] = alog.reshape(depth, 1, 16)
        pbc[:, :, 16:32] = dtb.reshape(depth, 1, 16)
        d["pbc"] = pbc
        d["sgu_ln"] = np.ascontiguousarray(np.broadcast_to(
            np.concatenate([f(inp["sgu_ln_g"])[:depth], f(inp["sgu_ln_b"])[:depth]], axis=1)[:, None, :], (depth, 128, 2048)))
        d["sgu_bs"] = np.ascontiguousarray(np.broadcast_to(sb.reshape(depth, 1, 1024), (depth, 128, 1024)))
        d["sgu_wT"] = np.ascontiguousarray(sw.transpose(0, 3, 1, 2).reshape(depth, 128, 1024))
        fl = np.zeros((128, 2), np.float32)
        fl[:, 1 - par] = 1.0
        d["flags"] = fl
        per_par.append(d)
    in_maps = []
    for core in range(8):
        b, par = core // 2, core % 2
        xx = x[b, par * T:(par + 1) * T]
        if par == 1:
            xx = xx[::-1]
        xt = np.ascontiguousarray(xx.T.reshape(16, 128, T).transpose(1, 0, 2))
        m = dict(shared)
        m.update(per_par[par])
        m["x"] = xt
        in_maps.append(m)
    return in_maps


def _assemble(res):
    out = np.zeros((BATCH, SEQ, D_MODEL), np.float32)
    for core in range(8):
        b, par = core // 2, core % 2
        y = np.asarray(res[core]["y"])
        yt = y.transpose(1, 0, 2).reshape(D_MODEL, T).T
        if par == 1:
            yt = yt[::-1]
        out[b, par * T:(par + 1) * T] = yt
    return out


_NC_CACHE = {}


def kernel(**inputs):
    depth = 4
    if depth not in _NC_CACHE:
        _NC_CACHE[depth] = build(depth)[0]
    nc = _NC_CACHE[depth]
    in_maps = _prep_inputs(inputs, depth)
    res = run_bass_kernel_spmd(nc, in_maps, core_ids=list(range(8)))
    return _assemble(res.results)
```

```python
import os
import numpy as np
import concourse.bass as bass
import concourse.mybir as mybir
from concourse.bass_utils import run_bass_kernel_spmd

F32 = mybir.dt.float32
BF16 = mybir.dt.bfloat16
AF = mybir.ActivationFunctionType
ALU = mybir.AluOpType

D_MODEL = 2048
SEQ = 2048
BATCH = 4
T = 1024
NCK = 16
D_FF = 5632
N_IN = 10272
OFF_Z, OFF_A, OFF_B, OFF_U, OFF_V, OFF_GA, OFF_GB = 3072, 4096, 4112, 4128, 5152, 6176, 8224
EPS = 1e-6
NPC = 16 + 16 + 120 + 264 + 88 + 1
PC_NMG, PC_NFG, PC_QCW, PC_FCW, PC_FCB, PC_GNG = 0, 16, 32, 152, 416, 504
NS_DMA = 8


class Op:
    __slots__ = ("eng", "fn", "deps", "signal", "ticket", "is_dma", "idx")


class Prog:
    ENGS = ["pe", "act", "dve", "pool", "sp"]

    def __init__(self, nc):
        self.nc = nc
        self.ops = []
        self.last_w = {}
        self.readers = {}

    def op(self, eng, fn, reads=(), writes=(), is_dma=False):
        o = Op()
        o.eng, o.fn, o.is_dma, o.signal, o.ticket = eng, fn, is_dma, False, None
        o.idx = len(self.ops)
        writes = list(writes) + [r for r in reads if r[0] == "ps" and r not in writes]
        deps = set()
        for r in reads:
            w = self.last_w.get(r)
            if w is not None:
                deps.add(w)
        for r in writes:
            w = self.last_w.get(r)
            if w is not None:
                deps.add(w)
            for rd in self.readers.get(r, ()):
                deps.add(rd)
        for r in reads:
            self.readers.setdefault(r, []).append(o.idx)
        for r in writes:
            self.last_w[r] = o.idx
            self.readers[r] = []
        best = {}
        red = set()
        for d in deps:
            od = self.ops[d]
            if od.is_dma:
                red.add(d)
            else:
                if od.eng == "pe" and eng == "pe" and not is_dma:
                    continue
                if od.eng not in best or best[od.eng] < d:
                    best[od.eng] = d
        red.update(best.values())
        o.deps = red
        self.ops.append(o)
        return o

    def dma(self, q, out, in_, reads=(), writes=()):
        return self.op(q, lambda e: e.dma_start(out=out, in_=in_), reads, writes, is_dma=True)

    def emit(self):
        nc = self.nc
        ops = self.ops
        sem = {e: nc.alloc_semaphore("sem_" + e) for e in self.ENGS}
        dsem = {e: [nc.alloc_semaphore("dsem_%s_%d" % (e, i)) for i in range(NS_DMA)] for e in ("pool", "sp", "act")}
        dcount = {e: 0 for e in dsem}
        dlast = {e: [None] * NS_DMA for e in dsem}
        for o in ops:
            for d in o.deps:
                ops[d].signal = True
        cnt = {e: 0 for e in self.ENGS}
        for o in ops:
            if o.is_dma:
                k = dcount[o.eng]
                dcount[o.eng] += 1
                slot = k % NS_DMA
                o.ticket = (dsem[o.eng][slot], 16 * (k // NS_DMA + 1))
                if dlast[o.eng][slot] is not None:
                    o.deps.add(dlast[o.eng][slot])
                dlast[o.eng][slot] = o.idx
            elif o.signal:
                cnt[o.eng] += 1
                o.ticket = (sem[o.eng], cnt[o.eng])
        per = {e: [] for e in self.ENGS}
        for o in ops:
            per[o.eng].append(o)
        self.stats = {e: len(per[e]) for e in per}
        self.stats["sig"] = dict(cnt)

        def mk(ename):
            def body(e):
                waited = {}
                for o in per[ename]:
                    for d in sorted(o.deps):
                        s, v = ops[d].ticket
                        key = id(s)
                        if waited.get(key, 0) < v:
                            e.wait_ge(s, v)
                            waited[key] = v
                    inst = o.fn(e)
                    if inst is None:
                        continue
                    if o.is_dma:
                        inst.then_inc(o.ticket[0], 16)
                    elif o.signal:
                        inst.then_inc(o.ticket[0], 1)
            return body

        with nc.Block() as block:
            block.tensor(mk("pe"))
            block.scalar(mk("act"))
            block.vector(mk("dve"))
            block.gpsimd(mk("pool"))
            block.sync(mk("sp"))


class Buf:
    GR = 128

    def __init__(self, nc, name, n, dt):
        self.name = name
        self.n = n
        self.t = nc.alloc_sbuf_tensor(name, [128, n], dt)

    def k(self, s, n):
        return [(self.name, g) for g in range(s // self.GR, (s + n - 1) // self.GR + 1)]

    def v(self, s, n):
        return self.t[:, s:s + n]


def build(depth=4, dbg=None, rg=None, stage=99, nheads=8, noex=False, hstage=9):
    nc = bass.Bass("TRN2", target_bir_lowering=False)
    P = Prog(nc)
    dt_in = lambda name, shape: nc.dram_tensor(name, list(shape), F32, kind="ExternalInput").ap()
    x_d = dt_in("x", [128, NCK, T])
    w_in_d = dt_in("w_in", [depth, D_MODEL, N_IN])
    w_ab_d = dt_in("w_ab", [depth, D_MODEL, 32])
    w_a_d = dt_in("w_a", [depth, 1024, D_MODEL])
    w_b_d = dt_in("w_b", [depth, 1024, D_MODEL])
    w_out_d = dt_in("w_out", [depth, D_MODEL, D_MODEL])
    w_up_d = dt_in("w_up", [depth, D_MODEL, 2 * D_FF])
    w_down_d = dt_in("w_down", [depth, D_FF, D_MODEL])
    pcol_d = dt_in("pcol", [depth, 128, NPC])
    pbc_d = dt_in("pbc", [depth, 128, 32])
    sgu_ln_d = dt_in("sgu_ln", [depth, 128, 2048])
    sgu_bs_d = dt_in("sgu_bs", [depth, 128, 1024])
    sgu_wT_d = dt_in("sgu_wT", [depth, 128, 1024])
    fng_d = dt_in("fng", [128, NCK])
    flags_d = dt_in("flags", [128, 2])
    masks_d = dt_in("masks", [128, 4 * 128])
    nlm_d = dt_in("nlm", [128, 2 * 7 * 128])
    y_d = nc.dram_tensor("y", [128, NCK, T], F32, kind="ExternalOutput").ap()
    dbg_d = None
    if dbg:
        dbg_d = nc.dram_tensor("dbg", [128, dbg], F32, kind="ExternalOutput").ap()
    cc_h_src = nc.dram_tensor("cc_h_src", [128, 32], BF16)
    cc_h_dst = nc.dram_tensor("cc_h_dst", [256, 32], BF16)
    cc_s_src = [nc.dram_tensor("cc_s_src%d" % i, [128, 128], F32) for i in range(2)]
    cc_s_dst = [nc.dram_tensor("cc_s_dst%d" % i, [256, 128], F32) for i in range(2)]
    RG = rg or [[0, 1], [2, 3], [4, 5], [6, 7]]

    XS = Buf(nc, "XS", NCK * T, F32)
    HW = 1028
    HB = Buf(nc, "HB", NCK * HW, BF16)
    BIG = Buf(nc, "BIG", 24 * T, BF16)
    WS = [Buf(nc, "WS%d" % i, 4096, BF16) for i in range(2)]
    TW = 1032
    TA = Buf(nc, "TA", TW, F32)
    TB = Buf(nc, "TB", TW, F32)
    TC = Buf(nc, "TC", TW, F32)
    OB = Buf(nc, "OB", TW, F32)
    PG = Buf(nc, "PG", 25 * 128, F32)
    NET = Buf(nc, "NET", 7 * 128, F32)
    NLM = Buf(nc, "NLM", 2 * 7 * 128, BF16)
    PT = Buf(nc, "PT", 8 * 128, BF16)
    SS = Buf(nc, "SS", 3 * 128, F32)
    CF = Buf(nc, "CF", 6 * 128, F32)
    CB = Buf(nc, "CB", 2 * 128, BF16)
    PCOL = Buf(nc, "PCOL", NPC + 7, F32)
    PBC = Buf(nc, "PBC", 64, F32)
    FNG = Buf(nc, "FNG", 16, F32)
    FLG = Buf(nc, "FLG", 2, F32)
    WT = Buf(nc, "WT", 1024, BF16)
    HX = Buf(nc, "HX", 2 * 32 + 32, BF16)
    PS = nc.alloc_psum_tensor("ps", [128, 8, 512], F32)

    def ps(b, s=0, n=512):
        return PS[:, b, s:s + n]

    def psk(b):
        return [("ps", b)]

    def xs(c, s=0, n=T):
        return XS.v(c * T + s, n)

    def xsk(c, s=0, n=T):
        return XS.k(c * T + s, n)

    def hb(c, s=0, n=T):
        return HB.v(c * HW + s, n)

    def hbk(c, s=0, n=T):
        return [("HB", c, (s + i) // 512) for i in range(0, n, 512)] if s < 1024 else [("HBh", c)]

    def big(i, s=0, n=T):
        return BIG.v(i * T + s, n)

    def bigk(i, s=0, n=T):
        return BIG.k(i * T + s, n)

    UI, LS, LI, US, IDF, ONF = [(CF.v(i * 128, 128), CF.k(i * 128, 128)) for i in range(6)]
    IDB, ONB = [(CB.v(i * 128, 128), CB.k(i * 128, 128)) for i in range(2)]

    def pc(off, n=1):
        return PCOL.v(off, n)

    PCK = [("PCOL", 0)]

    def mm(out, lhsT, rhs, start, stop, R, W):
        P.op("pe", lambda e: e.matmul(out, lhsT, rhs, start=start, stop=stop), R, W)

    def act(out, in_, func, R, W, scale=None, bias=None):
        kw = {}
        if scale is not None:
            kw["scale"] = scale
        if bias is not None:
            kw["bias"] = bias
        P.op("act", lambda e: e.activation(out, in_, func, **kw), R, W)

    def tt(out, a, b, op, R, W, eng="dve"):
        P.op(eng, lambda e: e.tensor_tensor(out, a, b, op), R, W)

    def ts(out, in0, s1, s2, op0, op1, R, W, eng="dve"):
        if op1 is None:
            P.op(eng, lambda e: e.tensor_scalar(out, in0, s1, None, op0), R, W)
        else:
            P.op(eng, lambda e: e.tensor_scalar(out, in0, s1, s2, op0, op1), R, W)

    def stt(out, in0, sc, in1, op0, op1, R, W):
        P.op("dve", lambda e: e.scalar_tensor_tensor(out, in0, sc, in1, op0, op1), R, W)

    def cp(out, in_, R, W, eng="dve"):
        if eng == "act":
            P.op("act", lambda e: e.activation(out, in_, AF.Copy), R, W)
        else:
            P.op(eng, lambda e: e.tensor_copy(out, in_), R, W)

    def recip(out, in_, R, W):
        P.op("dve", lambda e: e.reciprocal(out, in_), R, W)

    def memset(ap, val, W, eng="dve"):
        P.op(eng, lambda e: e.memset(ap, val), (), W)

    ws_i = [0]

    def wslab():
        b = WS[ws_i[0] % 2]
        ws_i[0] += 1
        return b

    def wload(slab, off, src_ap, kc, ncols):
        dst = slab.v(off, kc * ncols).rearrange("p (k n) -> p k n", k=kc)
        src = src_ap.rearrange("(k p) n -> p k n", p=128)
        P.dma("pool", dst, src, (), slab.k(off, kc * ncols))

    def wv(slab, off, kc, ncols, k, c0=0, n=128):
        return slab.v(off + k * ncols + c0, n)

    P.dma("sp", CF.v(0, 512), masks_d[:, :], (), CF.k(0, 512))
    P.dma("pool", NLM.v(0, 1792), nlm_d[:, :], (), NLM.k(0, 1792))
    P.dma("sp", FNG.v(0, 16), fng_d[:, :], (), FNG.k(0, 16))
    P.dma("sp", FLG.v(0, 2), flags_d[:, :], (), FLG.k(0, 2))
    for c in range(NCK):
        P.dma("sp", xs(c), x_d[:, c, :], (), xsk(c))
    tt(IDF[0], UI[0], LI[0], ALU.mult, UI[1] + LI[1], IDF[1])
    memset(ONF[0], 1.0, ONF[1])
    cp(IDB[0], IDF[0], IDF[1], IDB[1])
    cp(ONB[0], ONF[0], ONF[1], ONB[1])
    memset(TA.v(0, TW), 0.0, TA.k(0, TW))
    memset(TB.v(0, TW), 0.0, TB.k(0, TW))
    memset(TC.v(0, TW), 0.0, TC.k(0, TW))
    memset(OB.v(0, TW), 0.0, OB.k(0, TW))
    memset(HB.v(0, NCK * HW), 0.0, [k for c in range(NCK) for k in hbk(c) + hbk(c, 1024, 4)])

    psrot = [0]

    def psb(n=1):
        b = psrot[0]
        psrot[0] = (psrot[0] + n) % 6
        if b + n > 6:
            b = 0
            psrot[0] = n % 6
        return b

    def rmsnorm(goff):
        for tb in range(2):
            s0 = tb * 512
            b = psb()
            for c in range(NCK):
                sq = PT.v((c % 2) * 512, 512)
                sqk = PT.k((c % 2) * 512, 512)
                act(sq, xs(c, s0, 512), AF.Square, xsk(c, s0, 512), sqk)
                mm(ps(b), ONB[0], sq, c == 0, c == NCK - 1, ONB[1] + sqk, psk(b))
            r = TA.v(0, 512)
            rk = TA.k(0, 512)
            act(r, ps(b), AF.Ln, psk(b), rk, scale=1.0 / D_MODEL, bias=EPS)
            act(r, r, AF.Exp, rk, rk, scale=-0.5)
            for c in range(NCK):
                stt(hb(c, s0, 512), xs(c, s0, 512), pc(goff + c), r, ALU.mult, ALU.mult,
                    xsk(c, s0, 512) + PCK + rk, hbk(c, s0, 512))

    def halo_exchange():
        src = HX.v(64, 32).rearrange("p (c t) -> p c t", t=2)
        hsrc = HB.t[:, :].rearrange("p (c w) -> p c w", w=HW)[:, :, 1022:1024]
        cp(src, hsrc, [k for c in range(NCK) for k in hbk(c, 512, 512)], HX.k(64, 32))
        P.dma("pool", cc_h_src.ap(), HX.v(64, 32), HX.k(64, 32), [("cc_h_src",)])
        P.op("pool", lambda g: g.collective_compute("AllGather", ALU.bypass, replica_groups=RG,
                                                    ins=[cc_h_src.ap().opt()], outs=[cc_h_dst.ap().opt()]),
             [("cc_h_src",)], [("cc_h_dst",)])
        P.dma("pool", HX.v(0, 32), cc_h_dst.ap()[0:128, :], [("cc_h_dst",)], HX.k(0, 32))
        P.dma("pool", HX.v(32, 32), cc_h_dst.ap()[128:256, :], [("cc_h_dst",)], HX.k(32, 32))
        ts(HX.v(0, 32), HX.v(0, 32), FLG.v(0, 1), None, ALU.mult, None, HX.k(0, 32) + FLG.k(0, 2), HX.k(0, 32))
        stt(HX.v(0, 32), HX.v(32, 32), FLG.v(1, 1), HX.v(0, 32), ALU.mult, ALU.add,
            HX.k(0, 64) + FLG.k(0, 2), HX.k(0, 32))
        pv = HX.v(0, 32).rearrange("p (c t) -> p c t", t=2)
        hdst = HB.t[:, :].rearrange("p (c w) -> p c w", w=HW)
        hk = [k for c in range(NCK) for k in hbk(c, 1024, 4)]
        cp(hdst[:, :, 1024:1025], pv[:, :, 1:2], HX.k(0, 32), hk)
        cp(hdst[:, :, 1025:1026], pv[:, :, 0:1], HX.k(0, 32), hk)

    def proj_fm_g(slab, off, ncols, col0, dst, dk, halo, evac="act", func=None):
        b = psb(2)
        for tb in range(2):
            for k in range(NCK):
                mm(ps(b + tb), wv(slab, off, NCK, ncols, k, col0), hb(k, tb * 512, 512), k == 0, k == NCK - 1,
                   slab.k(off, NCK * ncols) + hbk(k, tb * 512, 512), psk(b + tb))
        base = halo
        for tb in range(2):
            o = dst(base + tb * 512, 512)
            if func is not None:
                act(o[0], ps(b + tb), func, psk(b + tb), o[1])
            elif evac == "act":
                cp(o[0], ps(b + tb), psk(b + tb), o[1], eng="act")
            else:
                cp(o[0], ps(b + tb), psk(b + tb), o[1])
        if halo:
            b2 = psb()
            for k in range(NCK):
                mm(ps(b2, 0, 32), wv(slab, off, NCK, ncols, k, col0), hb(k, 996, 32), k == 0, k == NCK - 1,
                   slab.k(off, NCK * ncols) + hbk(k, 512, 512) + hbk(k, 1024, 4), psk(b2))
            o = dst(base + 1024, halo)
            cp(o[0], ps(b2, 28, halo), psk(b2), o[1])
        yield

    def proj_fm(*a, **kw):
        run(proj_fm_g(*a, **kw))

    def bufdst(buf):
        return lambda s, n: (buf.v(s, n), buf.k(s, n))

    def bigdst(i):
        return lambda s, n: (big(i, s, n), bigk(i, s, n))

    def l2n_g(accb, dsti, norm):
        a, ak = accb.v(0, T), accb.k(0, T)
        if not norm:
            act(big(dsti), a, AF.Silu, ak, bigk(dsti))
            yield
            return
        act(a, a, AF.Silu, ak, ak)
        yield
        b = psb(2)
        for tb in range(2):
            sq = big(dsti, tb * 512, 512)
            sqk = bigk(dsti, tb * 512, 512)
            act(sq, accb.v(tb * 512, 512), AF.Square, ak, sqk)
            mm(ps(b + tb), ONB[0], sq, True, True, ONB[1] + sqk, psk(b + tb))
        r, rk = TA.v(0, T), TA.k(0, T)
        for tb in range(2):
            act(TA.v(tb * 512, 512), ps(b + tb), AF.Ln, psk(b + tb), rk, bias=EPS)
        act(r, r, AF.Exp, rk, rk, scale=-0.5)
        yield
        tt(big(dsti), a, r, ALU.mult, ak + rk, bigk(dsti))
        yield

    def pg(i, n=1):
        return PG.v(i * 128, n * 128), PG.k(i * 128, n * 128)
    G_AB, G_G, G_BETA, G_NB, G_GC, G_EGC, G_EGR, G_DCH = 0, 2, 3, 4, 5, 6, 7, 8
    G_LG, G_DT, G_DTS, G_DTI, G_AT, G_Q, G_T0, G_T1, G_TT0, G_TT1, G_EGB = range(9, 20)
    def pco(n, j):
        s = 19 * T + (n * 5 + j) * 128
        return BIG.v(s, 128), BIG.k(s, 128)
    QKT_S = 0.08838834764831845

    def gdn_gates(l):
        slab = wslab()
        wload(slab, 0, w_ab_d[l], NCK, 32)
        ab, abk = pg(G_AB, 2)
        for n in range(8):
            b = psb()
            for k in range(NCK):
                mm(ps(b, 0, 32), hb(k, n * 128, 128), wv(slab, 0, NCK, 32, k, 0, 32), k == 0, k == NCK - 1,
                   slab.k(0, 512) + hbk(k, n * 128, 128), psk(b))
            cp(PG.v(G_AB * 128 + n * 32, 32), ps(b, 0, 32), psk(b), abk)
        ab3 = ab.rearrange("p (n c) -> p n c", c=32)
        g3 = pg(G_G)[0].rearrange("p (c n) -> p c n", n=8)
        gk = pg(G_G)[1]
        for n in range(8):
            tt(g3[:, :, n], ab3[:, n, 0:16], PBC.v(16, 16), ALU.add, abk + PBC.k(0, 64), gk)
        gfl = pg(G_G)[0]
        act(gfl, gfl, AF.Exp, gk, gk)
        act(gfl, gfl, AF.Ln, gk, gk, bias=1.0)
        for n in range(8):
            tt(g3[:, :, n], g3[:, :, n], PBC.v(32, 16), ALU.mult, gk + PBC.k(0, 64), gk)
        be3 = pg(G_BETA)[0].rearrange("p (c n) -> p c n", n=8)
        bek = pg(G_BETA)[1]
        for n in range(8):
            act(be3[:, :, n], ab3[:, n, 16:32], AF.Sigmoid, abk, bek)
        ts(pg(G_NB)[0], pg(G_BETA)[0], -1.0, None, ALU.mult, None, bek, pg(G_NB)[1])
        b = psb()
        mm(ps(b, 0, 64), UI[0], PG.v(G_G * 128, 64), True, True, UI[1] + gk, psk(b))
        mm(ps(b, 64, 64), LI[0], PG.v(G_G * 128 + 64, 64), True, True, LI[1] + gk, psk(b))
        b2 = psb()
        mm(ps(b2, 0, 128), ONF[0], gfl, True, True, ONF[1] + gk, psk(b2))
        cp(pg(G_GC)[0], ps(b, 0, 128), psk(b), pg(G_GC)[1])
        act(pg(G_EGC)[0], ps(b, 0, 128), AF.Exp, psk(b), pg(G_EGC)[1])
        tt(pg(G_EGR)[0], ps(b2, 0, 128), pg(G_GC)[0], ALU.subtract, psk(b2) + pg(G_GC)[1], pg(G_EGR)[1])
        act(pg(G_EGR)[0], pg(G_EGR)[0], AF.Exp, pg(G_EGR)[1], pg(G_EGR)[1])
        act(pg(G_DCH)[0], ps(b2, 0, 128), AF.Exp, psk(b2), pg(G_DCH)[1])

    def pgb(g0):
        return PG.v(g0 * 128, 256).bitcast(BF16), PG.k(g0 * 128, 256)

    def netb(i):
        return NET.v(i * 256, 256).bitcast(BF16), NET.k(i * 256, 256)

    ATH = [pgb(12), pgb(14)]
    def tcb(i):
        return TC.v(i * 256, 256).bitcast(BF16), TC.k(i * 256, 256)

    def wtb(i):
        return WT.v(i * 512, 512), WT.k(i * 512, 512)

    CSET = [(pgb(16), pgb(18), pgb(22), netb(0), netb(1), netb(2)),
            (tcb(0), tcb(1), tcb(2), tcb(3), wtb(0), wtb(1))]

    def c4(buf, j):
        return buf[0][:, j * 128:(j + 1) * 128]

    FSTOP = 999999999
    fcount = [0]

    def fgate():
        fcount[0] += 1
        return fcount[0] <= FSTOP

    def prep_front(d, h, Q_i, K_i, batch):
        TRI = UI if d == 0 else LI
        MINC, MSTR = (UI, US) if d == 0 else (LI, LS)
        ATh = ATH[batch]
        for j in range(4):
            n = batch * 4 + j
            col = (d * 8 + h) * 8 + n
            c0 = n * 128
            gcol = PG.v(G_G * 128 + col, 1)
            becol = PG.v(G_BETA * 128 + col, 1)
            gccol = PG.v(G_GC * 128 + col, 1)
            Lgk = pg(9)[1]
            Lhl = PG.v(9 * 128, 128).bitcast(BF16)
            Lh, Ll = Lhl[:, 0:128], Lhl[:, 128:256]
            if fgate():
                ts(Lh, TRI[0], gcol, None, ALU.mult, None, TRI[1] + pg(G_G)[1], Lgk)
            if fgate():
                stt(Ll, TRI[0], gcol, Lh, ALU.mult, ALU.subtract, TRI[1] + pg(G_G)[1] + Lgk, Lgk)
            b = psb()
            if fgate():
                mm(ps(b, 0, 128), ONB[0], Lh, True, False, ONB[1] + Lgk, psk(b))
                mm(ps(b, 0, 128), ONB[0], Ll, False, True, ONB[1] + Lgk, psk(b))
            EGB, EGBk = pg(11)
            if fgate():
                act(EGB, ps(b, 0, 128), AF.Exp, psk(b), EGBk)
            DT, DTk = pg(10)
            if fgate():
                ts(DT, ps(b, 0, 128), gccol, 0.0, ALU.subtract, ALU.min, psk(b) + pg(G_GC)[1], DTk)
            if fgate():
                act(DT, DT, AF.Exp, DTk, DTk)
            QgT, QgTk = pco(n, 2)
            if fgate():
                stt(QgT, big(Q_i, c0, 128), QKT_S, EGB, ALU.mult, ALU.mult, bigk(Q_i, c0, 128) + EGBk, QgTk)
            b = psb()
            if fgate():
                mm(ps(b, 0, 128), big(K_i, c0, 128), big(K_i, c0, 128), True, True, bigk(K_i, c0, 128), psk(b))
            if fgate():
                mm(ps(b, 128, 128), big(K_i, c0, 128), big(Q_i, c0, 128), True, True,
                   bigk(K_i, c0, 128) + bigk(Q_i, c0, 128), psk(b))
            if fgate():
                tt(DT, DT, MINC[0], ALU.mult, DTk + MINC[1], DTk)
            QKT, QKTk = pco(n, 3)
            if fgate():
                stt(QKT, ps(b, 128, 128), QKT_S, DT, ALU.mult, ALU.mult, psk(b) + DTk, QKTk)
            if fgate():
                tt(DT, DT, MSTR[0], ALU.mult, DTk + MSTR[1], DTk)
            if fgate():
                stt(c4(ATh, j), ps(b, 0, 128), becol, DT, ALU.mult, ALU.mult, psk(b) + pg(G_BETA)[1] + DTk, ATh[1])
                yield

    def prep_chain(d, batch):
        ATh = ATH[batch]
        NETH, QH, TH, TL, TTH, TTL = CSET[batch]
        r4 = lambda buf: buf[0].rearrange("p (j c) -> p j c", j=4)
        for lev in range(7):
            mask = NLM.v((d * 7 + lev) * 128, 128).unsqueeze(1).broadcast_to([128, 4, 128])
            tt(r4(NETH), r4(ATh), mask, ALU.mult, ATh[1] + NLM.k((d * 7 + lev) * 128, 128), NETH[1], eng=("pool" if d == 0 else "dve"))
            bq = psb()
            for j in range(4):
                o = ps(bq, j * 128, 128)
                if lev == 0:
                    mm(o, IDB[0], IDB[0], True, False, IDB[1], psk(bq))
                    mm(o, c4(NETH, j), IDB[0], False, True, NETH[1] + IDB[1], psk(bq))
                else:
                    mm(o, IDB[0], IDB[0], True, False, IDB[1], psk(bq))
                    mm(o, c4(NETH, j), c4(TH, j), False, False, NETH[1] + TH[1], psk(bq))
                    mm(o, c4(NETH, j), c4(TL, j), False, True, NETH[1] + TL[1], psk(bq))
            if lev == 0:
                cp(TH[0], ps(bq), psk(bq), TH[1], eng="act")
                tt(TL[0], ps(bq), TH[0], ALU.subtract, psk(bq) + TH[1], TL[1])
                for j in range(4):
                    tt(c4(TTH, j), c4(NETH, j), IDB[0], ALU.add, NETH[1] + IDB[1], TTH[1])
                memset(TTL[0], 0.0, TTL[1])
                yield
                continue
            cp(QH[0], ps(bq), psk(bq), QH[1], eng="act")
            yield
            last = lev == 6
            if not last:
                bt = psb()
                for j in range(4):
                    o = ps(bt, j * 128, 128)
                    mm(o, c4(TTH, j), c4(QH, j), True, False, TTH[1] + QH[1], psk(bt))
                    mm(o, c4(TTL, j), c4(QH, j), False, True, TTL[1] + QH[1], psk(bt))
            btt = psb()
            for j in range(4):
                o = ps(btt, j * 128, 128)
                mm(o, c4(QH, j), c4(TTH, j), True, False, TTH[1] + QH[1], psk(btt))
                mm(o, c4(QH, j), c4(TTL, j), False, True, TTL[1] + QH[1], psk(btt))
            if not last:
                cp(TH[0], ps(bt), psk(bt), TH[1], eng="act")
                tt(TL[0], ps(bt), TH[0], ALU.subtract, psk(bt) + TH[1], TL[1])
            cp(TTH[0], ps(btt), psk(btt), TTH[1], eng="act")
            if not last:
                tt(TTL[0], ps(btt), TTH[0], ALU.subtract, psk(btt) + TTH[1], TTL[1])
            yield

    def prep_tail(d, h, K_i, V_i, batch):
        TTH = CSET[batch][4]
        for j in range(4):
            n = batch * 4 + j
            col = (d * 8 + h) * 8 + n
            c0 = n * 128
            becol = PG.v(G_BETA * 128 + col, 1)
            b = psb()
            mm(ps(b, 0, 128), big(K_i, c0, 128), IDB[0], True, True, bigk(K_i, c0, 128) + IDB[1], psk(b))
            mm(ps(b, 128, 128), big(V_i, c0, 128), IDB[0], True, True, bigk(V_i, c0, 128) + IDB[1], psk(b))
            Kg, Kgk = PT.v((j % 2) * 256, 128), PT.k((j % 2) * 256, 128)
            Vt, Vtk = PT.v((j % 2) * 256 + 128, 128), PT.k((j % 2) * 256 + 128, 128)
            Kd, Kdk = pco(n, 4)
            ts(Kg, ps(b, 0, 128), PG.v(G_EGC * 128 + col, 1), None, ALU.mult, None, psk(b) + pg(G_EGC)[1], Kgk)
            ts(Kd, ps(b, 0, 128), PG.v(G_EGR * 128 + col, 1), None, ALU.mult, None, psk(b) + pg(G_EGR)[1], Kdk)
            cp(Vt, ps(b, 128, 128), psk(b), Vtk, eng="act")
            b = psb()
            mm(ps(b, 0, 128), c4(TTH, j), Vt, True, True, TTH[1] + Vtk, psk(b))
            mm(ps(b, 128, 128), Kg, c4(TTH, j), True, True, Kgk + TTH[1], psk(b))
            Ub, Ubk = pco(n, 1)
            ts(Ub, ps(b, 0, 128), becol, None, ALU.mult, None, psk(b) + pg(G_BETA)[1], Ubk)
            WT_, WTk = pco(n, 0)
            cp(WT_, ps(b, 128, 128), psk(b), WTk, eng="act")
            yield

    def run(*gens):
        gens = list(gens)
        while gens:
            for g in list(gens):
                try:
                    next(g)
                except StopIteration:
                    gens.remove(g)

    PSTOP = 99

    def seq(*gens):
        for g in gens:
            yield from g

    def gdn_prep(d, h, Q_i, K_i, V_i, skip_front0=False):
        if not skip_front0:
            run(prep_front(d, h, Q_i, K_i, 0))
        run(prep_chain(d, 0), seq(prep_front(d, h, Q_i, K_i, 1), prep_chain(d, 1)))
        run(prep_tail(d, h, K_i, V_i, 0), prep_tail(d, h, K_i, V_i, 1))

    def gdn_scan_gen(d, h, first_dir):
        S, Sk = SS.v(0, 128), SS.k(0, 128)
        Sb, Sbk = PT.v(384, 128), PT.k(384, 128)
        vn, vnk = PT.v(512, 128), PT.k(512, 128)
        St, Stk = PT.v(640, 128), PT.k(640, 128)
        cp(Sb, S, Sk, Sbk)
        order = range(8) if d == 0 else range(7, -1, -1)
        for n in order:
            col = (d * 8 + h) * 8 + n
            c0 = n * 128
            WT_, WTk = pco(n, 0)
            Ub, Ubk = pco(n, 1)
            QgT, QgTk = pco(n, 2)
            QKT, QKTk = pco(n, 3)
            Kd, Kdk = pco(n, 4)
            b = psb()
            mm(ps(b, 0, 128), WT_, Sb, True, True, WTk + Sbk, psk(b))
            stt(vn, ps(b, 0, 128), PG.v(G_NB * 128 + col, 1), Ub, ALU.mult, ALU.add,
                psk(b) + pg(G_NB)[1] + Ubk, vnk)
            b = psb()
            mm(ps(b, 0, 128), Sb, QgT, True, False, Sbk + QgTk, psk(b))
            mm(ps(b, 0, 128), vn, QKT, False, True, vnk + QKTk, psk(b))
            b3 = psb()
            mm(ps(b3, 0, 128), Kd, vn, True, True, Kdk + vnk, psk(b3))
            if first_dir:
                cp(OB.v(c0, 128), ps(b, 0, 128), psk(b), OB.k(c0, 128), eng="act")
            else:
                tt(OB.v(c0, 128), OB.v(c0, 128), ps(b, 0, 128), ALU.add, psk(b) + OB.k(c0, 128), OB.k(c0, 128))
            stt(S, S, PG.v(G_DCH * 128 + col, 1), ps(b3, 0, 128), ALU.mult, ALU.add,
                Sk + pg(G_DCH)[1] + psk(b3), Sk)
            cp(Sb, S, Sk, Sbk, eng="act")
            yield

    def state_exchange_issue(i):
        S, Sk = SS.v(0, 128), SS.k(0, 128)
        P.dma("pool", cc_s_src[i].ap(), S, Sk, [("cc_s_src", i)])
        P.op("pool", lambda g: g.collective_compute("AllGather", ALU.bypass, replica_groups=RG,
                                                    ins=[cc_s_src[i].ap().opt()], outs=[cc_s_dst[i].ap().opt()]),
             [("cc_s_src", i)], [("cc_s_dst", i)])
        P.dma("pool", SS.v(128, 128), cc_s_dst[i].ap()[0:128, :], [("cc_s_dst", i)], SS.k(128, 128))
        P.dma("pool", SS.v(256, 128), cc_s_dst[i].ap()[128:256, :], [("cc_s_dst", i)], SS.k(256, 128))

    def state_exchange_finish():
        S, Sk = SS.v(0, 128), SS.k(0, 128)
        ts(S, SS.v(128, 128), FLG.v(0, 1), None, ALU.mult, None, SS.k(128, 128) + FLG.k(0, 2), Sk)
        stt(S, SS.v(256, 128), FLG.v(1, 1), S, ALU.mult, ALU.add, SS.k(256, 128) + FLG.k(0, 2) + Sk, Sk)

    def conv_taps_g(pbuf, accb, woff, ntap, boff=None):
        a, ak = accb.v(0, T), accb.k(0, T)
        pk = pbuf.k(0, TW)
        if boff is None:
            ts(a, pbuf.v(0, T), pc(woff), None, ALU.mult, None, pk + PCK, ak)
        else:
            ts(a, pbuf.v(0, T), pc(woff), pc(boff), ALU.mult, ALU.add, pk + PCK, ak)
        yield
        for tap in range(1, ntap):
            stt(a, pbuf.v(tap, T), pc(woff + tap), a, ALU.mult, ALU.add, pk + PCK + ak, ak)
            yield

    def conv_taps(*a, **kw):
        run(conv_taps_g(*a, **kw))

    def sgu(l):
        P.dma("pool", WT.v(0, 1024), sgu_wT_d[l], (), WT.k(0, 1024))
        LNG, LNGk = pg(9, 8)
        LNB, LNBk = pg(17, 8)
        P.dma("sp", LNG, sgu_ln_d[l][:, 0:1024], (), LNGk)
        P.dma("sp", LNB, sgu_ln_d[l][:, 1024:2048], (), LNBk)
        BSB, BSBk = OB.v(0, T), OB.k(0, T)
        P.dma("sp", BSB, sgu_bs_d[l], (), BSBk)

        def gelu(x, xk, t1, t1k, out, outk):
            act(t1, x, AF.Square, xk, t1k)
            ts(t1, t1, 0.044715, 1.0, ALU.mult, ALU.add, t1k, t1k)
            tt(t1, t1, x, ALU.mult, t1k + xk, t1k)
            act(t1, t1, AF.Sigmoid, t1k, t1k, scale=1.5957691216057308)
            tt(out, t1, x, ALU.mult, t1k + xk, outk)

        for g in range(8):
            if g % 2 == 0:
                slab = wslab()
                wload(slab, 0, w_in_d[l][:, OFF_U + g * 128: OFF_U + g * 128 + 256], NCK, 256)
            proj_fm(slab, 0, 256, (g % 2) * 128, bufdst(TA), None, 0)
            gelu(TA.v(0, T), TA.k(0, T), TB.v(0, T), TB.k(0, T), big(g), bigk(g))
        WV0 = 8 * T
        wvk = BIG.k(WV0, 16 * T)
        wv3 = BIG.v(WV0, 16 * T).rearrange("p (k n) -> p k n", k=NCK)
        for hf in range(2):
            P.dma("pool", wv3[:, :, hf * 512:(hf + 1) * 512],
                  w_in_d[l][:, OFF_V + hf * 512: OFF_V + (hf + 1) * 512].rearrange("(k p) n -> p k n", p=128),
                  (), wvk)
        for n in range(8):
            bq = 6
            for half in range(2):
                for q in range(2):
                    cq = half * 512 + q * 256
                    for k in range(NCK):
                        mm(ps(bq + half, q * 256, 256), hb(k, n * 128, 128), BIG.v(WV0 + k * T + cq, 256),
                           k == 0, k == NCK - 1, wvk + hbk(k, n * 128, 128), psk(bq + half))
            for half in range(2):
                cp(TA.v(half * 512, 512), ps(bq + half), psk(bq + half), TA.k(0, T), eng="act")
            gelu(TA.v(0, T), TA.k(0, T), TB.v(0, T), TB.k(0, T), TC.v(0, T), TC.k(0, T))
            st, stk = PT.v(0, 8), PT.k(0, 8)
            mu = SS.v(128, 1)
            muk = SS.k(128, 8)
            P.op("dve", lambda e: e.tensor_reduce(SS.v(128, 1), TC.v(0, T), mybir.AxisListType.X, ALU.add),
                 TC.k(0, T), muk)
            ts(SS.v(129, 1), SS.v(128, 1), -1.0 / 1024, None, ALU.mult, None, muk, muk)
            ts(TC.v(0, T), TC.v(0, T), SS.v(129, 1), None, ALU.add, None, TC.k(0, T) + muk, TC.k(0, T))
            act(TB.v(0, T), TC.v(0, T), AF.Square, TC.k(0, T), TB.k(0, T))
            P.op("dve", lambda e: e.tensor_reduce(SS.v(130, 1), TB.v(0, T), mybir.AxisListType.X, ALU.add),
                 TB.k(0, T), muk)
            act(SS.v(131, 1), SS.v(130, 1), AF.Sqrt, muk, muk, scale=1.0 / 1024, bias=EPS)
            recip(SS.v(131, 1), SS.v(131, 1), muk, muk)
            stt(TC.v(0, T), TC.v(0, T), SS.v(131, 1), LNG, ALU.mult, ALU.mult, TC.k(0, T) + muk + LNGk, TC.k(0, T))
            VN, VNk = PT.v(0, T), PT.k(0, T)
            tt(VN, TC.v(0, T), LNB, ALU.add, TC.k(0, T) + LNBk, VNk)
            for half in range(2):
                b = psb()
                for gq in range(4):
                    g = half * 4 + gq
                    mm(ps(b, gq * 128, 128), PT.v(g * 128, 128), WT.v(g * 128, 128), True, True,
                       VNk + WT.k(0, 1024), psk(b))
                tmp, tmpk = TA.v(0, 512), TA.k(0, 512)
                tt(tmp, ps(b), OB.v(half * 512, 512), ALU.add, psk(b) + BSBk, tmpk)
                for gq in range(4):
                    g = half * 4 + gq
                    tt(big(g, n * 128, 128), big(g, n * 128, 128), TA.v(gq * 128, 128), ALU.mult,
                       bigk(g, n * 128, 128) + tmpk, bigk(g, n * 128, 128))

    def mixer(l):
        P.dma("sp", PCOL.v(0, NPC), pcol_d[l], (), PCK)
        P.dma("sp", PBC.v(0, 32), pbc_d[l], (), PBC.k(0, 64))
        act(PBC.v(32, 16), PBC.v(0, 16), AF.Exp, PBC.k(0, 64), PBC.k(0, 64))
        ts(PBC.v(32, 16), PBC.v(32, 16), -1.0, None, ALU.mult, None, PBC.k(0, 64), PBC.k(0, 64))
        rmsnorm(PC_NMG)
        if stage < 1:
            return
        halo_exchange()
        if stage < 2:
            return
        sgu(l)
        if stage < 3:
            return
        gdn_gates(l)
        if stage < 4:
            return
        Q_i, K_i, V_i = 16, 17, 18

        def head_pre(h):
            s1 = wslab()
            wload(s1, 0, w_in_d[l][:, h * 128: h * 128 + 128], NCK, 128)
            wload(s1, 2048, w_in_d[l][:, 1024 + h * 128: 1024 + h * 128 + 128], NCK, 128)
            s2 = wslab()
            wload(s2, 0, w_in_d[l][:, 2048 + h * 128: 2048 + h * 128 + 128], NCK, 128)
            wload(s2, 2048, w_in_d[l][:, OFF_Z + h * 128: OFF_Z + h * 128 + 128], NCK, 128)
            yield from proj_fm_g(s2, 2048, 128, 0, bigdst(8 + h), None, 0, func=AF.Silu)
            for j, (slab, off, dsti, norm) in enumerate(((s1, 0, Q_i, True), (s1, 2048, K_i, True), (s2, 0, V_i, False))):
                pdst = lambda s, n: (TA.v(s, n), TA.k(s, n))
                memset(TA.v(0, 2), 0.0, TA.k(0, 2))
                yield from proj_fm_g(slab, off, 128, 0, pdst, None, 2)
                yield from conv_taps_g(TA, TB, PC_QCW + (j * 8 + h) * 5, 5)
                yield from l2n_g(TB, dsti, norm)

        def gnorm_g(h):
            b = psb(2)
            for tb in range(2):
                sq, sqk = PT.v(tb * 512, 512), PT.k(tb * 512, 512)
                act(sq, OB.v(tb * 512, 512), AF.Square, OB.k(0, T), sqk)
                mm(ps(b + tb), ONB[0], sq, True, True, ONB[1] + sqk, psk(b + tb))
            for tb in range(2):
                act(TC.v(tb * 512, 512), ps(b + tb), AF.Ln, psk(b + tb), TC.k(0, T), scale=1.0 / 128, bias=EPS)
            act(TC.v(0, T), TC.v(0, T), AF.Exp, TC.k(0, T), TC.k(0, T), scale=-0.5)
            yield
            tt(TC.v(0, T), TC.v(0, T), OB.v(0, T), ALU.mult, TC.k(0, T) + OB.k(0, T), TC.k(0, T))
            yield
            stt(big(8 + h), TC.v(0, T), pc(PC_GNG), big(8 + h), ALU.mult, ALU.mult,
                TC.k(0, T) + PCK + bigk(8 + h), bigk(8 + h))
            yield

        run(head_pre(0))
        for h in range(nheads):
            memset(SS.v(0, 128), 0.0, SS.k(0, 128))
            gdn_prep(0, h, Q_i, K_i, V_i)
            run(gdn_scan_gen(0, h, True), prep_front(1, h, Q_i, K_i, 0))
            state_exchange_issue(h % 2)
            gdn_prep(1, h, Q_i, K_i, V_i, skip_front0=True)
            state_exchange_finish()
            post = seq(gdn_scan_gen(1, h, False), gnorm_g(h))
            if h + 1 < nheads:
                run(post, head_pre(h + 1))
            else:
                run(post)
        if stage < 5:
            return
        for tb in range(2):
            s0 = tb * 512

            def mg(c):
                s = 16 * T + c * 512
                return BIG.v(s, 512), BIG.k(s, 512)
            for c in range(NCK):
                sa = wslab()
                wload(sa, 0, w_a_d[l][:, c * 128:(c + 1) * 128], 8, 128)
                wload(sa, 1024, w_b_d[l][:, c * 128:(c + 1) * 128], 8, 128)
                sg = wslab()
                wload(sg, 0, w_in_d[l][:, OFF_GA + c * 128: OFF_GA + (c + 1) * 128], NCK, 128)
                wload(sg, 2048, w_in_d[l][:, OFF_GB + c * 128: OFF_GB + (c + 1) * 128], NCK, 128)
                b = psb(4)
                for k in range(8):
                    mm(ps(b), wv(sa, 0, 8, 128, k), big(8 + k, s0, 512), k == 0, k == 7,
                       sa.k(0, 1024) + bigk(8 + k, s0, 512), psk(b))
                for k in range(8):
                    mm(ps(b + 1), wv(sa, 1024, 8, 128, k), big(k, s0, 512), k == 0, k == 7,
                       sa.k(1024, 1024) + bigk(k, s0, 512), psk(b + 1))
                for k in range(NCK):
                    mm(ps(b + 2), wv(sg, 0, NCK, 128, k), hb(k, s0, 512), k == 0, k == NCK - 1,
                       sg.k(0, 2048) + hbk(k, s0, 512), psk(b + 2))
                for k in range(NCK):
                    mm(ps(b + 3), wv(sg, 2048, NCK, 128, k), hb(k, s0, 512), k == 0, k == NCK - 1,
                       sg.k(2048, 2048) + hbk(k, s0, 512), psk(b + 3))
                act(TA.v(0, 512), ps(b + 2), AF.Sigmoid, psk(b + 2), TA.k(0, 512))
                act(TA.v(512, 512), ps(b + 3), AF.Sigmoid, psk(b + 3), TA.k(512, 512))
                tt(TB.v(0, 512), ps(b), TA.v(0, 512), ALU.mult, psk(b) + TA.k(0, 512), TB.k(0, 512))
                tt(TB.v(512, 512), ps(b + 1), TA.v(512, 512), ALU.mult, psk(b + 1) + TA.k(512, 512), TB.k(512, 512))
                m, mk_ = mg(c)
                tt(m, TB.v(0, 512), TB.v(512, 512), ALU.add, TB.k(0, T), mk_)
            for co in range(NCK):
                so = wslab()
                wload(so, 0, w_out_d[l][:, co * 128:(co + 1) * 128], NCK, 128)
                b = psb()
                for k in range(NCK):
                    m, mk_ = mg(k)
                    mm(ps(b), wv(so, 0, NCK, 128, k), m, k == 0, k == NCK - 1, so.k(0, 2048) + mk_, psk(b))
                tt(xs(co, s0, 512), xs(co, s0, 512), ps(b), ALU.add, xsk(co, s0, 512) + psk(b), xsk(co, s0, 512))

    def ffn(l):
        rmsnorm(PC_NFG)
        halo_exchange()
        memset(TA.v(0, 1), 0.0, TA.k(0, 1))
        memset(TC.v(0, 1), 0.0, TC.k(0, 1))
        for qd in range(4):
            for j in range(11):
                cg = qd * 11 + j
                slab = wslab()
                wload(slab, 0, w_up_d[l][:, cg * 128:(cg + 1) * 128], NCK, 128)
                wload(slab, 2048, w_up_d[l][:, D_FF + cg * 128: D_FF + (cg + 1) * 128], NCK, 128)
                proj_fm(slab, 0, 128, 0, lambda s, n: (TA.v(s, n), TA.k(s, n)), None, 1)
                conv_taps(TA, TB, PC_FCW + cg * 3, 3, PC_FCB + cg)
                proj_fm(slab, 2048, 128, 0, lambda s, n: (TC.v(s, n), TC.k(s, n)), None, 1)
                conv_taps(TC, OB, PC_FCW + (44 + cg) * 3, 3, PC_FCB + 44 + cg)
                act(TB.v(0, T), TB.v(0, T), AF.Silu, TB.k(0, T), TB.k(0, T))
                tt(big(j), TB.v(0, T), OB.v(0, T), ALU.mult, TB.k(0, T) + OB.k(0, T), bigk(j))
            for co in range(NCK):
                slab = wslab()
                wload(slab, 0, w_down_d[l][qd * 1408:(qd + 1) * 1408, co * 128:(co + 1) * 128], 11, 128)
                for tb in range(2):
                    b = psb()
                    for k in range(11):
                        mm(ps(b), wv(slab, 0, 11, 128, k), big(k, tb * 512, 512), k == 0, k == 10,
                           slab.k(0, 1408) + bigk(k, tb * 512, 512), psk(b))
                    tt(xs(co, tb * 512, 512), xs(co, tb * 512, 512), ps(b), ALU.add,
                       xsk(co, tb * 512, 512) + psk(b), xsk(co, tb * 512, 512))

    for l in range(depth):
        mixer(l)
        if stage >= 6:
            ffn(l)

    outs = []
    for tb in range(2):
        s0 = tb * 512
        b = psb()
        for c in range(NCK):
            sq = PT.v((c % 2) * 512, 512)
            sqk = PT.k((c % 2) * 512, 512)
            act(sq, xs(c, s0, 512), AF.Square, xsk(c, s0, 512), sqk)
            mm(ps(b), ONB[0], sq, c == 0, c == NCK - 1, ONB[1] + sqk, psk(b))
        r, rk = TA.v(0, 512), TA.k(0, 512)
        act(r, ps(b), AF.Ln, psk(b), rk, scale=1.0 / D_MODEL, bias=EPS)
        act(r, r, AF.Exp, rk, rk, scale=-0.5)
        for c in range(NCK):
            stt(xs(c, s0, 512), xs(c, s0, 512), FNG.v(c, 1), r, ALU.mult, ALU.mult,
                xsk(c, s0, 512) + FNG.k(0, 16) + rk, xsk(c, s0, 512))
    for c in range(NCK):
        outs.append(P.dma("sp", y_d[:, c, :], xs(c), xsk(c), [("y", c)]))
    P.op("sp", lambda e: None, [("y", c) for c in range(NCK)], ())
    P.emit()
    return nc, P


def _masks():
    i = np.arange(128)
    m, c = i[:, None], i[None, :]
    UI = (m <= c).astype(np.float32)
    LS = (m > c).astype(np.float32)
    LI = (m >= c).astype(np.float32)
    US = (m < c).astype(np.float32)
    masks = np.concatenate([UI, LS, LI, US], axis=1)
    nlm = np.zeros((2, 7, 128, 128), np.float32)
    for lev in range(7):
        b = 1 << lev
        same = (m // (2 * b)) == (c // (2 * b))
        s_first = (m % (2 * b)) < b
        c_first = (c % (2 * b)) < b
        nlm[0, lev] = -1.0 * (same & s_first & ~c_first)
        nlm[1, lev] = -1.0 * (same & ~s_first & c_first)
    nlm = nlm.transpose(2, 0, 1, 3).reshape(128, 2 * 7 * 128)
    return np.ascontiguousarray(masks), np.ascontiguousarray(nlm)


def _prep_inputs(inp, depth):
    f = lambda a: np.ascontiguousarray(np.asarray(a, dtype=np.float32))
    x = f(inp["x"])
    w_in = f(inp["w_in"])[:depth]
    shared = {
        "w_in": w_in,
        "w_a": f(inp["w_branch_a"])[:depth], "w_b": f(inp["w_branch_b"])[:depth],
        "w_out": f(inp["w_out"])[:depth], "w_up": f(inp["w_up"])[:depth], "w_down": f(inp["w_down"])[:depth],
        "fng": np.ascontiguousarray(f(inp["final_norm_g"]).reshape(16, 128).T),
    }
    masks, nlm = _masks()
    shared["masks"] = masks
    shared["nlm"] = nlm
    per_par = []
    for par in range(2):
        rev = par == 1
        d = {}
        ab = w_in[:, :, OFF_A:OFF_A + 32].copy()
        alog = f(inp["a_log"])[:depth].copy()
        dtb = f(inp["dt_bias"])[:depth].copy()
        qcw = f(inp["qkv_conv_w"])[:depth].copy()
        fcw = f(inp["ffn_conv_w"])[:depth].copy()
        sw = f(inp["sgu_w"])[:depth].copy()
        sb = f(inp["sgu_b"])[:depth].copy()
        if rev:
            ab = np.concatenate([ab[:, :, 8:16], ab[:, :, 0:8], ab[:, :, 24:32], ab[:, :, 16:24]], axis=2)
            alog = alog[:, ::-1]
            dtb = dtb[:, ::-1]
            qcw = qcw[:, ::-1]
            fcw = fcw[:, ::-1]
            sw = sw[:, :, ::-1, ::-1]
            sb = sb[:, :, ::-1]
        d["w_ab"] = np.ascontiguousarray(ab)
        pcol = np.zeros((depth, 128, NPC), np.float32)
        pcol[:, :, PC_NMG:PC_NMG + 16] = f(inp["norm_mix_g"])[:depth].reshape(depth, 16, 128).transpose(0, 2, 1)
        pcol[:, :, PC_NFG:PC_NFG + 16] = f(inp["norm_ffn_g"])[:depth].reshape(depth, 16, 128).transpose(0, 2, 1)
        pcol[:, :, PC_QCW:PC_QCW + 120] = qcw.reshape(depth, 5, 24, 128).transpose(0, 3, 2, 1).reshape(depth, 128, 120)
        pcol[:, :, PC_FCW:PC_FCW + 264] = fcw.reshape(depth, 3, 88, 128).transpose(0, 3, 2, 1).reshape(depth, 128, 264)
        pcol[:, :, PC_FCB:PC_FCB + 88] = f(inp["ffn_conv_b"])[:depth].reshape(depth, 88, 128).transpose(0, 2, 1)
        pcol[:, :, PC_GNG] = f(inp["gdn_norm_g"])[:depth]
        d["pcol"] = pcol
        pbc = np.zeros((depth, 128, 32), np.float32)
        pbc[:, :, 0:16] = alog.reshape(depth, 1, 16)
        pbc[:, :, 16:32] = dtb.reshape(depth, 1, 16)
        d["pbc"] = pbc
        d["sgu_ln"] = np.ascontiguousarray(np.broadcast_to(
            np.concatenate([f(inp["sgu_ln_g"])[:depth], f(inp["sgu_ln_b"])[:depth]], axis=1)[:, None, :], (depth, 128, 2048)))
        d["sgu_bs"] = np.ascontiguousarray(np.broadcast_to(sb.reshape(depth, 1, 1024), (depth, 128, 1024)))
        d["sgu_wT"] = np.ascontiguousarray(sw.transpose(0, 3, 1, 2).reshape(depth, 128, 1024))
        fl = np.zeros((128, 2), np.float32)
        fl[:, 1 - par] = 1.0
        d["flags"] = fl
        per_par.append(d)
    in_maps = []
    for core in range(8):
        b, par = core // 2, core % 2
        xx = x[b, par * T:(par + 1) * T]
        if par == 1:
            xx = xx[::-1]
        xt = np.ascontiguousarray(xx.T.reshape(16, 128, T).transpose(1, 0, 2))
        m = dict(shared)
        m.update(per_par[par])
        m["x"] = xt
        in_maps.append(m)
    return in_maps


def _assemble(res):
    out = np.zeros((BATCH, SEQ, D_MODEL), np.float32)
    for core in range(8):
        b, par = core // 2, core % 2
        y = np.asarray(res[core]["y"])
        yt = y.transpose(1, 0, 2).reshape(D_MODEL, T).T
        if par == 1:
            yt = yt[::-1]
        out[b, par * T:(par + 1) * T] = yt
    return out


_NC_CACHE = {}


def kernel(**inputs):
    depth = 4
    if depth not in _NC_CACHE:
        _NC_CACHE[depth] = build(depth)[0]
    nc = _NC_CACHE[depth]
    in_maps = _prep_inputs(inputs, depth)
    res = run_bass_kernel_spmd(nc, in_maps, core_ids=list(range(8)))
    return _assemble(res.results)
```

```python
import os
import numpy as np
import concourse.bass as bass
import concourse.mybir as mybir
from concourse.bass_utils import run_bass_kernel_spmd

F32 = mybir.dt.float32
BF16 = mybir.dt.bfloat16
AF = mybir.ActivationFunctionType
ALU = mybir.AluOpType

D_MODEL = 2048
SEQ = 2048
BATCH = 4
T = 1024
NCK = 16
D_FF = 5632
N_IN = 10272
OFF_Z, OFF_A, OFF_B, OFF_U, OFF_V, OFF_GA, OFF_GB = 3072, 4096, 4112, 4128, 5152, 6176, 8224
EPS = 1e-6
NPC = 16 + 16 + 120 + 264 + 88 + 1
PC_NMG, PC_NFG, PC_QCW, PC_FCW, PC_FCB, PC_GNG = 0, 16, 32, 152, 416, 504
NS_DMA = 8


class Op:
    __slots__ = ("eng", "fn", "deps", "signal", "ticket", "is_dma", "idx")


class Prog:
    ENGS = ["pe", "act", "dve", "pool", "sp"]

    def __init__(self, nc):
        self.nc = nc
        self.ops = []
        self.last_w = {}
        self.readers = {}

    def op(self, eng, fn, reads=(), writes=(), is_dma=False):
        o = Op()
        o.eng, o.fn, o.is_dma, o.signal, o.ticket = eng, fn, is_dma, False, None
        o.idx = len(self.ops)
        deps = set()
        for r in reads:
            w = self.last_w.get(r)
            if w is not None:
                deps.add(w)
            if r[0] == "ps":
                for rd in self.readers.get(r, ()):
                    if self.ops[rd].eng != eng:
                        deps.add(rd)
        for r in writes:
            w = self.last_w.get(r)
            if w is not None:
                deps.add(w)
            for rd in self.readers.get(r, ()):
                deps.add(rd)
        for r in reads:
            self.readers.setdefault(r, []).append(o.idx)
        for r in writes:
            self.last_w[r] = o.idx
            self.readers[r] = []
        best = {}
        red = set()
        for d in deps:
            od = self.ops[d]
            if od.is_dma:
                red.add(d)
            else:
                if od.eng == "pe" and eng == "pe" and not is_dma:
                    continue
                if od.eng not in best or best[od.eng] < d:
                    best[od.eng] = d
        red.update(best.values())
        o.deps = red
        self.ops.append(o)
        return o

    def dma(self, q, out, in_, reads=(), writes=()):
        return self.op(q, lambda e: e.dma_start(out=out, in_=in_), reads, writes, is_dma=True)

    def emit(self):
        nc = self.nc
        ops = self.ops
        sem = {e: nc.alloc_semaphore("sem_" + e) for e in self.ENGS}
        dsem = {e: [nc.alloc_semaphore("dsem_%s_%d" % (e, i)) for i in range(NS_DMA)] for e in ("pool", "sp", "act")}
        dcount = {e: 0 for e in dsem}
        dlast = {e: [None] * NS_DMA for e in dsem}
        for o in ops:
            for d in o.deps:
                ops[d].signal = True
        cnt = {e: 0 for e in self.ENGS}
        for o in ops:
            if o.is_dma:
                k = dcount[o.eng]
                dcount[o.eng] += 1
                slot = k % NS_DMA
                o.ticket = (dsem[o.eng][slot], 16 * (k // NS_DMA + 1))
                if dlast[o.eng][slot] is not None:
                    o.deps.add(dlast[o.eng][slot])
                dlast[o.eng][slot] = o.idx
            elif o.signal:
                cnt[o.eng] += 1
                o.ticket = (sem[o.eng], cnt[o.eng])
        per = {e: [] for e in self.ENGS}
        for o in ops:
            per[o.eng].append(o)
        self.stats = {e: len(per[e]) for e in per}
        self.stats["sig"] = dict(cnt)

        def mk(ename):
            def body(e):
                waited = {}
                for o in per[ename]:
                    for d in sorted(o.deps):
                        s, v = ops[d].ticket
                        key = id(s)
                        if waited.get(key, 0) < v:
                            e.wait_ge(s, v)
                            waited[key] = v
                    inst = o.fn(e)
                    if inst is None:
                        continue
                    if o.is_dma:
                        inst.then_inc(o.ticket[0], 16)
                    elif o.signal:
                        inst.then_inc(o.ticket[0], 1)
            return body

        with nc.Block() as block:
            block.tensor(mk("pe"))
            block.scalar(mk("act"))
            block.vector(mk("dve"))
            block.gpsimd(mk("pool"))
            block.sync(mk("sp"))


class Buf:
    GR = 128

    def __init__(self, nc, name, n, dt):
        self.name = name
        self.n = n
        self.t = nc.alloc_sbuf_tensor(name, [128, n], dt)

    def k(self, s, n):
        return [(self.name, g) for g in range(s // self.GR, (s + n - 1) // self.GR + 1)]

    def v(self, s, n):
        return self.t[:, s:s + n]


def build(depth=4, dbg=None, rg=None, stage=99, nheads=8, noex=False, hstage=9):
    nc = bass.Bass("TRN2", target_bir_lowering=False)
    P = Prog(nc)
    dt_in = lambda name, shape: nc.dram_tensor(name, list(shape), F32, kind="ExternalInput").ap()
    x_d = dt_in("x", [128, NCK, T])
    w_in_d = dt_in("w_in", [depth, D_MODEL, N_IN])
    w_ab_d = dt_in("w_ab", [depth, D_MODEL, 32])
    w_a_d = dt_in("w_a", [depth, 1024, D_MODEL])
    w_b_d = dt_in("w_b", [depth, 1024, D_MODEL])
    w_out_d = dt_in("w_out", [depth, D_MODEL, D_MODEL])
    w_up_d = dt_in("w_up", [depth, D_MODEL, 2 * D_FF])
    w_down_d = dt_in("w_down", [depth, D_FF, D_MODEL])
    pcol_d = dt_in("pcol", [depth, 128, NPC])
    pbc_d = dt_in("pbc", [depth, 128, 32])
    sgu_ln_d = dt_in("sgu_ln", [depth, 128, 2048])
    sgu_bs_d = dt_in("sgu_bs", [depth, 128, 1024])
    sgu_wT_d = dt_in("sgu_wT", [depth, 128, 1024])
    fng_d = dt_in("fng", [128, NCK])
    flags_d = dt_in("flags", [128, 2])
    masks_d = dt_in("masks", [128, 4 * 128])
    nlm_d = dt_in("nlm", [128, 2 * 7 * 128])
    y_d = nc.dram_tensor("y", [128, NCK, T], F32, kind="ExternalOutput").ap()
    dbg_d = None
    if dbg:
        dbg_d = nc.dram_tensor("dbg", [128, dbg], F32, kind="ExternalOutput").ap()
    cc_h_src = nc.dram_tensor("cc_h_src", [128, 32], BF16)
    cc_h_dst = nc.dram_tensor("cc_h_dst", [256, 32], BF16)
    cc_s_src = [nc.dram_tensor("cc_s_src%d" % i, [128, 128], F32) for i in range(2)]
    cc_s_dst = [nc.dram_tensor("cc_s_dst%d" % i, [256, 128], F32) for i in range(2)]
    RG = rg or [[0, 1], [2, 3], [4, 5], [6, 7]]

    XS = Buf(nc, "XS", NCK * T, F32)
    HW = 1028
    HB = Buf(nc, "HB", NCK * HW, BF16)
    BIG = Buf(nc, "BIG", 24 * T, BF16)
    WS = [Buf(nc, "WS%d" % i, 4096, BF16) for i in range(2)]
    TW = 1032
    TA = Buf(nc, "TA", TW, F32)
    TB = Buf(nc, "TB", TW, F32)
    TC = Buf(nc, "TC", TW, F32)
    OB = Buf(nc, "OB", TW, F32)
    PG = Buf(nc, "PG", 25 * 128, F32)
    NET = Buf(nc, "NET", 7 * 128, F32)
    NLM = Buf(nc, "NLM", 2 * 7 * 128, BF16)
    PT = Buf(nc, "PT", 8 * 128, BF16)
    SS = Buf(nc, "SS", 3 * 128, F32)
    CF = Buf(nc, "CF", 6 * 128, F32)
    CB = Buf(nc, "CB", 2 * 128, BF16)
    PCOL = Buf(nc, "PCOL", NPC + 7, F32)
    PBC = Buf(nc, "PBC", 64, F32)
    FNG = Buf(nc, "FNG", 16, F32)
    FLG = Buf(nc, "FLG", 2, F32)
    WT = Buf(nc, "WT", 1024, BF16)
    HX = Buf(nc, "HX", 2 * 32 + 32, BF16)
    PS = nc.alloc_psum_tensor("ps", [128, 8, 512], F32)

    def ps(b, s=0, n=512):
        return PS[:, b, s:s + n]

    def psk(b):
        return [("ps", b)]

    def xs(c, s=0, n=T):
        return XS.v(c * T + s, n)

    def xsk(c, s=0, n=T):
        return XS.k(c * T + s, n)

    def hb(c, s=0, n=T):
        return HB.v(c * HW + s, n)

    def hbk(c, s=0, n=T):
        return [("HB", c, (s + i) // 512) for i in range(0, n, 512)] if s < 1024 else [("HBh", c)]

    def big(i, s=0, n=T):
        return BIG.v(i * T + s, n)

    def bigk(i, s=0, n=T):
        return BIG.k(i * T + s, n)

    UI, LS, LI, US, IDF, ONF = [(CF.v(i * 128, 128), CF.k(i * 128, 128)) for i in range(6)]
    IDB, ONB = [(CB.v(i * 128, 128), CB.k(i * 128, 128)) for i in range(2)]

    def pc(off, n=1):
        return PCOL.v(off, n)

    PCK = [("PCOL", 0)]

    def mm(out, lhsT, rhs, start, stop, R, W):
        P.op("pe", lambda e: e.matmul(out, lhsT, rhs, start=start, stop=stop), R, W)

    def act(out, in_, func, R, W, scale=None, bias=None):
        kw = {}
        if scale is not None:
            kw["scale"] = scale
        if bias is not None:
            kw["bias"] = bias
        P.op("act", lambda e: e.activation(out, in_, func, **kw), R, W)

    def tt(out, a, b, op, R, W, eng="dve"):
        P.op(eng, lambda e: e.tensor_tensor(out, a, b, op), R, W)

    def ts(out, in0, s1, s2, op0, op1, R, W, eng="dve"):
        if op1 is None:
            P.op(eng, lambda e: e.tensor_scalar(out, in0, s1, None, op0), R, W)
        else:
            P.op(eng, lambda e: e.tensor_scalar(out, in0, s1, s2, op0, op1), R, W)

    def stt(out, in0, sc, in1, op0, op1, R, W):
        P.op("dve", lambda e: e.scalar_tensor_tensor(out, in0, sc, in1, op0, op1), R, W)

    def cp(out, in_, R, W, eng="dve"):
        if eng == "act":
            P.op("act", lambda e: e.activation(out, in_, AF.Copy), R, W)
        else:
            P.op(eng, lambda e: e.tensor_copy(out, in_), R, W)

    def recip(out, in_, R, W):
        P.op("dve", lambda e: e.reciprocal(out, in_), R, W)

    def memset(ap, val, W, eng="dve"):
        P.op(eng, lambda e: e.memset(ap, val), (), W)

    ws_i = [0]

    def wslab():
        b = WS[ws_i[0] % 2]
        ws_i[0] += 1
        return b

    def wload(slab, off, src_ap, kc, ncols):
        dst = slab.v(off, kc * ncols).rearrange("p (k n) -> p k n", k=kc)
        src = src_ap.rearrange("(k p) n -> p k n", p=128)
        P.dma("pool", dst, src, (), slab.k(off, kc * ncols))

    def wv(slab, off, kc, ncols, k, c0=0, n=128):
        return slab.v(off + k * ncols + c0, n)

    P.dma("sp", CF.v(0, 512), masks_d[:, :], (), CF.k(0, 512))
    P.dma("pool", NLM.v(0, 1792), nlm_d[:, :], (), NLM.k(0, 1792))
    P.dma("sp", FNG.v(0, 16), fng_d[:, :], (), FNG.k(0, 16))
    P.dma("sp", FLG.v(0, 2), flags_d[:, :], (), FLG.k(0, 2))
    for c in range(NCK):
        P.dma("sp", xs(c), x_d[:, c, :], (), xsk(c))
    tt(IDF[0], UI[0], LI[0], ALU.mult, UI[1] + LI[1], IDF[1])
    memset(ONF[0], 1.0, ONF[1])
    cp(IDB[0], IDF[0], IDF[1], IDB[1])
    cp(ONB[0], ONF[0], ONF[1], ONB[1])
    memset(TA.v(0, TW), 0.0, TA.k(0, TW))
    memset(TB.v(0, TW), 0.0, TB.k(0, TW))
    memset(TC.v(0, TW), 0.0, TC.k(0, TW))
    memset(OB.v(0, TW), 0.0, OB.k(0, TW))
    memset(HB.v(0, NCK * HW), 0.0, [k for c in range(NCK) for k in hbk(c) + hbk(c, 1024, 4)])

    psrot = [0]

    def psb(n=1):
        b = psrot[0]
        psrot[0] = (psrot[0] + n) % 6
        if b + n > 6:
            b = 0
            psrot[0] = n % 6
        return b

    def rmsnorm(goff):
        for tb in range(2):
            s0 = tb * 512
            b = psb()
            for c in range(NCK):
                sq = PT.v((c % 2) * 512, 512)
                sqk = PT.k((c % 2) * 512, 512)
                act(sq, xs(c, s0, 512), AF.Square, xsk(c, s0, 512), sqk)
                mm(ps(b), ONB[0], sq, c == 0, c == NCK - 1, ONB[1] + sqk, psk(b))
            r = TA.v(0, 512)
            rk = TA.k(0, 512)
            act(r, ps(b), AF.Ln, psk(b), rk, scale=1.0 / D_MODEL, bias=EPS)
            act(r, r, AF.Exp, rk, rk, scale=-0.5)
            for c in range(NCK):
                stt(hb(c, s0, 512), xs(c, s0, 512), pc(goff + c), r, ALU.mult, ALU.mult,
                    xsk(c, s0, 512) + PCK + rk, hbk(c, s0, 512))

    def halo_exchange():
        src = HX.v(64, 32).rearrange("p (c t) -> p c t", t=2)
        hsrc = HB.t[:, :].rearrange("p (c w) -> p c w", w=HW)[:, :, 1022:1024]
        cp(src, hsrc, [k for c in range(NCK) for k in hbk(c, 512, 512)], HX.k(64, 32))
        P.dma("pool", cc_h_src.ap(), HX.v(64, 32), HX.k(64, 32), [("cc_h_src",)])
        P.op("pool", lambda g: g.collective_compute("AllGather", ALU.bypass, replica_groups=RG,
                                                    ins=[cc_h_src.ap().opt()], outs=[cc_h_dst.ap().opt()]),
             [("cc_h_src",)], [("cc_h_dst",)])
        P.dma("pool", HX.v(0, 32), cc_h_dst.ap()[0:128, :], [("cc_h_dst",)], HX.k(0, 32))
        P.dma("pool", HX.v(32, 32), cc_h_dst.ap()[128:256, :], [("cc_h_dst",)], HX.k(32, 32))
        ts(HX.v(0, 32), HX.v(0, 32), FLG.v(0, 1), None, ALU.mult, None, HX.k(0, 32) + FLG.k(0, 2), HX.k(0, 32))
        stt(HX.v(0, 32), HX.v(32, 32), FLG.v(1, 1), HX.v(0, 32), ALU.mult, ALU.add,
            HX.k(0, 64) + FLG.k(0, 2), HX.k(0, 32))
        pv = HX.v(0, 32).rearrange("p (c t) -> p c t", t=2)
        hdst = HB.t[:, :].rearrange("p (c w) -> p c w", w=HW)
        hk = [k for c in range(NCK) for k in hbk(c, 1024, 4)]
        cp(hdst[:, :, 1024:1025], pv[:, :, 1:2], HX.k(0, 32), hk)
        cp(hdst[:, :, 1025:1026], pv[:, :, 0:1], HX.k(0, 32), hk)

    def proj_fm_g(slab, off, ncols, col0, dst, dk, halo, evac="act", func=None):
        b = psb(2)
        for tb in range(2):
            for k in range(NCK):
                mm(ps(b + tb), wv(slab, off, NCK, ncols, k, col0), hb(k, tb * 512, 512), k == 0, k == NCK - 1,
                   slab.k(off, NCK * ncols) + hbk(k, tb * 512, 512), psk(b + tb))
        base = halo
        for tb in range(2):
            o = dst(base + tb * 512, 512)
            if func is not None:
                act(o[0], ps(b + tb), func, psk(b + tb), o[1])
            elif evac == "act":
                cp(o[0], ps(b + tb), psk(b + tb), o[1], eng="act")
            else:
                cp(o[0], ps(b + tb), psk(b + tb), o[1])
        if halo:
            b2 = psb()
            for k in range(NCK):
                mm(ps(b2, 0, 32), wv(slab, off, NCK, ncols, k, col0), hb(k, 996, 32), k == 0, k == NCK - 1,
                   slab.k(off, NCK * ncols) + hbk(k, 512, 512) + hbk(k, 1024, 4), psk(b2))
            o = dst(base + 1024, halo)
            cp(o[0], ps(b2, 28, halo), psk(b2), o[1])
        yield

    def proj_fm(*a, **kw):
        run(proj_fm_g(*a, **kw))

    def bufdst(buf):
        return lambda s, n: (buf.v(s, n), buf.k(s, n))

    def bigdst(i):
        return lambda s, n: (big(i, s, n), bigk(i, s, n))

    def l2n_g(accb, dsti, norm):
        a, ak = accb.v(0, T), accb.k(0, T)
        if not norm:
            act(big(dsti), a, AF.Silu, ak, bigk(dsti))
            yield
            return
        act(a, a, AF.Silu, ak, ak)
        yield
        b = psb(2)
        for tb in range(2):
            sq = big(dsti, tb * 512, 512)
            sqk = bigk(dsti, tb * 512, 512)
            act(sq, accb.v(tb * 512, 512), AF.Square, ak, sqk)
            mm(ps(b + tb), ONB[0], sq, True, True, ONB[1] + sqk, psk(b + tb))
        r, rk = TA.v(0, T), TA.k(0, T)
        for tb in range(2):
            act(TA.v(tb * 512, 512), ps(b + tb), AF.Ln, psk(b + tb), rk, bias=EPS)
        act(r, r, AF.Exp, rk, rk, scale=-0.5)
        yield
        tt(big(dsti), a, r, ALU.mult, ak + rk, bigk(dsti))
        yield

    def pg(i, n=1):
        return PG.v(i * 128, n * 128), PG.k(i * 128, n * 128)
    G_AB, G_G, G_BETA, G_NB, G_GC, G_EGC, G_EGR, G_DCH = 0, 2, 3, 4, 5, 6, 7, 8
    G_LG, G_DT, G_DTS, G_DTI, G_AT, G_Q, G_T0, G_T1, G_TT0, G_TT1, G_EGB = range(9, 20)
    def pco(n, j):
        s = 19 * T + (n * 5 + j) * 128
        return BIG.v(s, 128), BIG.k(s, 128)
    QKT_S = 0.08838834764831845

    def gdn_gates(l):
        slab = wslab()
        wload(slab, 0, w_ab_d[l], NCK, 32)
        ab, abk = pg(G_AB, 2)
        for n in range(8):
            b = psb()
            for k in range(NCK):
                mm(ps(b, 0, 32), hb(k, n * 128, 128), wv(slab, 0, NCK, 32, k, 0, 32), k == 0, k == NCK - 1,
                   slab.k(0, 512) + hbk(k, n * 128, 128), psk(b))
            cp(PG.v(G_AB * 128 + n * 32, 32), ps(b, 0, 32), psk(b), abk)
        ab3 = ab.rearrange("p (n c) -> p n c", c=32)
        g3 = pg(G_G)[0].rearrange("p (c n) -> p c n", n=8)
        gk = pg(G_G)[1]
        for n in range(8):
            tt(g3[:, :, n], ab3[:, n, 0:16], PBC.v(16, 16), ALU.add, abk + PBC.k(0, 64), gk)
        gfl = pg(G_G)[0]
        act(gfl, gfl, AF.Exp, gk, gk)
        act(gfl, gfl, AF.Ln, gk, gk, bias=1.0)
        for n in range(8):
            tt(g3[:, :, n], g3[:, :, n], PBC.v(32, 16), ALU.mult, gk + PBC.k(0, 64), gk)
        be3 = pg(G_BETA)[0].rearrange("p (c n) -> p c n", n=8)
        bek = pg(G_BETA)[1]
        for n in range(8):
            act(be3[:, :, n], ab3[:, n, 16:32], AF.Sigmoid, abk, bek)
        ts(pg(G_NB)[0], pg(G_BETA)[0], -1.0, None, ALU.mult, None, bek, pg(G_NB)[1])
        b = psb()
        mm(ps(b, 0, 64), UI[0], PG.v(G_G * 128, 64), True, True, UI[1] + gk, psk(b))
        mm(ps(b, 64, 64), LI[0], PG.v(G_G * 128 + 64, 64), True, True, LI[1] + gk, psk(b))
        b2 = psb()
        mm(ps(b2, 0, 128), ONF[0], gfl, True, True, ONF[1] + gk, psk(b2))
        cp(pg(G_GC)[0], ps(b, 0, 128), psk(b), pg(G_GC)[1])
        act(pg(G_EGC)[0], ps(b, 0, 128), AF.Exp, psk(b), pg(G_EGC)[1])
        tt(pg(G_EGR)[0], ps(b2, 0, 128), pg(G_GC)[0], ALU.subtract, psk(b2) + pg(G_GC)[1], pg(G_EGR)[1])
        act(pg(G_EGR)[0], pg(G_EGR)[0], AF.Exp, pg(G_EGR)[1], pg(G_EGR)[1])
        act(pg(G_DCH)[0], ps(b2, 0, 128), AF.Exp, psk(b2), pg(G_DCH)[1])

    def pgb(g0):
        return PG.v(g0 * 128, 256).bitcast(BF16), PG.k(g0 * 128, 256)

    def netb(i):
        return NET.v(i * 256, 256).bitcast(BF16), NET.k(i * 256, 256)

    ATH = [pgb(12), pgb(14)]
    def tcb(i):
        return TC.v(i * 256, 256).bitcast(BF16), TC.k(i * 256, 256)

    def wtb(i):
        return WT.v(i * 512, 512), WT.k(i * 512, 512)

    CSET = [(pgb(16), pgb(18), pgb(22), netb(0), netb(1), netb(2)),
            (tcb(0), tcb(1), tcb(2), tcb(3), wtb(0), wtb(1))]

    def c4(buf, j):
        return buf[0][:, j * 128:(j + 1) * 128]

    FSTOP = 999999999
    fcount = [0]

    def fgate():
        fcount[0] += 1
        return fcount[0] <= FSTOP

    def prep_front(d, h, Q_i, K_i, batch):
        TRI = UI if d == 0 else LI
        MINC, MSTR = (UI, US) if d == 0 else (LI, LS)
        ATh = ATH[batch]
        for j in range(4):
            n = batch * 4 + j
            col = (d * 8 + h) * 8 + n
            c0 = n * 128
            gcol = PG.v(G_G * 128 + col, 1)
            becol = PG.v(G_BETA * 128 + col, 1)
            gccol = PG.v(G_GC * 128 + col, 1)
            Lgk = pg(9)[1]
            Lhl = PG.v(9 * 128, 128).bitcast(BF16)
            Lh, Ll = Lhl[:, 0:128], Lhl[:, 128:256]
            if fgate():
                ts(Lh, TRI[0], gcol, None, ALU.mult, None, TRI[1] + pg(G_G)[1], Lgk)
            if fgate():
                stt(Ll, TRI[0], gcol, Lh, ALU.mult, ALU.subtract, TRI[1] + pg(G_G)[1] + Lgk, Lgk)
            b = psb()
            if fgate():
                mm(ps(b, 0, 128), ONB[0], Lh, True, False, ONB[1] + Lgk, psk(b))
                mm(ps(b, 0, 128), ONB[0], Ll, False, True, ONB[1] + Lgk, psk(b))
            EGB, EGBk = pg(11)
            if fgate():
                act(EGB, ps(b, 0, 128), AF.Exp, psk(b), EGBk)
            DT, DTk = pg(10)
            if fgate():
                ts(DT, ps(b, 0, 128), gccol, 0.0, ALU.subtract, ALU.min, psk(b) + pg(G_GC)[1], DTk)
            if fgate():
                act(DT, DT, AF.Exp, DTk, DTk)
            QgT, QgTk = pco(n, 2)
            if fgate():
                stt(QgT, big(Q_i, c0, 128), QKT_S, EGB, ALU.mult, ALU.mult, bigk(Q_i, c0, 128) + EGBk, QgTk)
            b = psb()
            if fgate():
                mm(ps(b, 0, 128), big(K_i, c0, 128), big(K_i, c0, 128), True, True, bigk(K_i, c0, 128), psk(b))
            if fgate():
                mm(ps(b, 128, 128), big(K_i, c0, 128), big(Q_i, c0, 128), True, True,
                   bigk(K_i, c0, 128) + bigk(Q_i, c0, 128), psk(b))
            if fgate():
                tt(DT, DT, MINC[0], ALU.mult, DTk + MINC[1], DTk)
            QKT, QKTk = pco(n, 3)
            if fgate():
                stt(QKT, ps(b, 128, 128), QKT_S, DT, ALU.mult, ALU.mult, psk(b) + DTk, QKTk)
            if fgate():
                tt(DT, DT, MSTR[0], ALU.mult, DTk + MSTR[1], DTk)
            if fgate():
                stt(c4(ATh, j), ps(b, 0, 128), becol, DT, ALU.mult, ALU.mult, psk(b) + pg(G_BETA)[1] + DTk, ATh[1])
                yield

    def prep_chain(d, batch):
        ATh = ATH[batch]
        NETH, QH, TH, TL, TTH, TTL = CSET[batch]
        r4 = lambda buf: buf[0].rearrange("p (j c) -> p j c", j=4)
        for lev in range(7):
            mask = NLM.v((d * 7 + lev) * 128, 128).unsqueeze(1).broadcast_to([128, 4, 128])
            tt(r4(NETH), r4(ATh), mask, ALU.mult, ATh[1] + NLM.k((d * 7 + lev) * 128, 128), NETH[1], eng=("pool" if d == 0 else "dve"))
            bq = psb()
            for j in range(4):
                o = ps(bq, j * 128, 128)
                if lev == 0:
                    mm(o, IDB[0], IDB[0], True, False, IDB[1], psk(bq))
                    mm(o, c4(NETH, j), IDB[0], False, True, NETH[1] + IDB[1], psk(bq))
                else:
                    mm(o, IDB[0], IDB[0], True, False, IDB[1], psk(bq))
                    mm(o, c4(NETH, j), c4(TH, j), False, False, NETH[1] + TH[1], psk(bq))
                    mm(o, c4(NETH, j), c4(TL, j), False, True, NETH[1] + TL[1], psk(bq))
            if lev == 0:
                cp(TH[0], ps(bq), psk(bq), TH[1], eng="act")
                tt(TL[0], ps(bq), TH[0], ALU.subtract, psk(bq) + TH[1], TL[1])
                for j in range(4):
                    tt(c4(TTH, j), c4(NETH, j), IDB[0], ALU.add, NETH[1] + IDB[1], TTH[1])
                memset(TTL[0], 0.0, TTL[1])
                yield
                continue
            cp(QH[0], ps(bq), psk(bq), QH[1], eng="act")
            yield
            last = lev == 6
            if not last:
                bt = psb()
                for j in range(4):
                    o = ps(bt, j * 128, 128)
                    mm(o, c4(TTH, j), c4(QH, j), True, False, TTH[1] + QH[1], psk(bt))
                    mm(o, c4(TTL, j), c4(QH, j), False, True, TTL[1] + QH[1], psk(bt))
            btt = psb()
            for j in range(4):
                o = ps(btt, j * 128, 128)
                mm(o, c4(QH, j), c4(TTH, j), True, False, TTH[1] + QH[1], psk(btt))
                mm(o, c4(QH, j), c4(TTL, j), False, True, TTL[1] + QH[1], psk(btt))
            if not last:
                cp(TH[0], ps(bt), psk(bt), TH[1], eng="act")
                tt(TL[0], ps(bt), TH[0], ALU.subtract, psk(bt) + TH[1], TL[1])
            cp(TTH[0], ps(btt), psk(btt), TTH[1], eng="act")
            if not last:
                tt(TTL[0], ps(btt), TTH[0], ALU.subtract, psk(btt) + TTH[1], TTL[1])
            yield

    def prep_tail(d, h, K_i, V_i, batch):
        TTH = CSET[batch][4]
        for j in range(4):
            n = batch * 4 + j
            col = (d * 8 + h) * 8 + n
            c0 = n * 128
            becol = PG.v(G_BETA * 128 + col, 1)
            b = psb()
            mm(ps(b, 0, 128), big(K_i, c0, 128), IDB[0], True, True, bigk(K_i, c0, 128) + IDB[1], psk(b))
            mm(ps(b, 128, 128), big(V_i, c0, 128), IDB[0], True, True, bigk(V_i, c0, 128) + IDB[1], psk(b))
            Kg, Kgk = PT.v((j % 2) * 256, 128), PT.k((j % 2) * 256, 128)
            Vt, Vtk = PT.v((j % 2) * 256 + 128, 128), PT.k((j % 2) * 256 + 128, 128)
            Kd, Kdk = pco(n, 4)
            ts(Kg, ps(b, 0, 128), PG.v(G_EGC * 128 + col, 1), None, ALU.mult, None, psk(b) + pg(G_EGC)[1], Kgk)
            ts(Kd, ps(b, 0, 128), PG.v(G_EGR * 128 + col, 1), None, ALU.mult, None, psk(b) + pg(G_EGR)[1], Kdk)
            cp(Vt, ps(b, 128, 128), psk(b), Vtk, eng="act")
            b = psb()
            mm(ps(b, 0, 128), c4(TTH, j), Vt, True, True, TTH[1] + Vtk, psk(b))
            mm(ps(b, 128, 128), Kg, c4(TTH, j), True, True, Kgk + TTH[1], psk(b))
            Ub, Ubk = pco(n, 1)
            ts(Ub, ps(b, 0, 128), becol, None, ALU.mult, None, psk(b) + pg(G_BETA)[1], Ubk)
            WT_, WTk = pco(n, 0)
            cp(WT_, ps(b, 128, 128), psk(b), WTk, eng="act")
            yield

    def run(*gens):
        gens = list(gens)
        while gens:
            for g in list(gens):
                try:
                    next(g)
                except StopIteration:
                    gens.remove(g)

    PSTOP = 99

    def seq(*gens):
        for g in gens:
            yield from g

    def gdn_prep(d, h, Q_i, K_i, V_i, skip_front0=False):
        if not skip_front0:
            run(prep_front(d, h, Q_i, K_i, 0))
        run(prep_chain(d, 0), seq(prep_front(d, h, Q_i, K_i, 1), prep_chain(d, 1)))
        run(prep_tail(d, h, K_i, V_i, 0), prep_tail(d, h, K_i, V_i, 1))

    def gdn_scan_gen(d, h, first_dir):
        S, Sk = SS.v(0, 128), SS.k(0, 128)
        Sb, Sbk = PT.v(384, 128), PT.k(384, 128)
        vn, vnk = PT.v(512, 128), PT.k(512, 128)
        St, Stk = PT.v(640, 128), PT.k(640, 128)
        cp(Sb, S, Sk, Sbk)
        order = range(8) if d == 0 else range(7, -1, -1)
        for n in order:
            col = (d * 8 + h) * 8 + n
            c0 = n * 128
            WT_, WTk = pco(n, 0)
            Ub, Ubk = pco(n, 1)
            QgT, QgTk = pco(n, 2)
            QKT, QKTk = pco(n, 3)
            Kd, Kdk = pco(n, 4)
            b = psb()
            mm(ps(b, 0, 128), WT_, Sb, True, True, WTk + Sbk, psk(b))
            stt(vn, ps(b, 0, 128), PG.v(G_NB * 128 + col, 1), Ub, ALU.mult, ALU.add,
                psk(b) + pg(G_NB)[1] + Ubk, vnk)
            b = psb()
            mm(ps(b, 0, 128), Sb, QgT, True, False, Sbk + QgTk, psk(b))
            mm(ps(b, 0, 128), vn, QKT, False, True, vnk + QKTk, psk(b))
            b3 = psb()
            mm(ps(b3, 0, 128), Kd, vn, True, True, Kdk + vnk, psk(b3))
            if first_dir:
                cp(OB.v(c0, 128), ps(b, 0, 128), psk(b), OB.k(c0, 128), eng="act")
            else:
                tt(OB.v(c0, 128), OB.v(c0, 128), ps(b, 0, 128), ALU.add, psk(b) + OB.k(c0, 128), OB.k(c0, 128))
            stt(S, S, PG.v(G_DCH * 128 + col, 1), ps(b3, 0, 128), ALU.mult, ALU.add,
                Sk + pg(G_DCH)[1] + psk(b3), Sk)
            cp(Sb, S, Sk, Sbk, eng="act")
            yield

    def state_exchange_issue(i):
        S, Sk = SS.v(0, 128), SS.k(0, 128)
        P.dma("pool", cc_s_src[i].ap(), S, Sk, [("cc_s_src", i)])
        P.op("pool", lambda g: g.collective_compute("AllGather", ALU.bypass, replica_groups=RG,
                                                    ins=[cc_s_src[i].ap().opt()], outs=[cc_s_dst[i].ap().opt()]),
             [("cc_s_src", i)], [("cc_s_dst", i)])
        P.dma("pool", SS.v(128, 128), cc_s_dst[i].ap()[0:128, :], [("cc_s_dst", i)], SS.k(128, 128))
        P.dma("pool", SS.v(256, 128), cc_s_dst[i].ap()[128:256, :], [("cc_s_dst", i)], SS.k(256, 128))

    def state_exchange_finish():
        S, Sk = SS.v(0, 128), SS.k(0, 128)
        ts(S, SS.v(128, 128), FLG.v(0, 1), None, ALU.mult, None, SS.k(128, 128) + FLG.k(0, 2), Sk)
        stt(S, SS.v(256, 128), FLG.v(1, 1), S, ALU.mult, ALU.add, SS.k(256, 128) + FLG.k(0, 2) + Sk, Sk)

    def conv_taps_g(pbuf, accb, woff, ntap, boff=None):
        a, ak = accb.v(0, T), accb.k(0, T)
        pk = pbuf.k(0, TW)
        if boff is None:
            ts(a, pbuf.v(0, T), pc(woff), None, ALU.mult, None, pk + PCK, ak)
        else:
            ts(a, pbuf.v(0, T), pc(woff), pc(boff), ALU.mult, ALU.add, pk + PCK, ak)
        yield
        for tap in range(1, ntap):
            stt(a, pbuf.v(tap, T), pc(woff + tap), a, ALU.mult, ALU.add, pk + PCK + ak, ak)
            yield

    def conv_taps(*a, **kw):
        run(conv_taps_g(*a, **kw))

    def sgu(l):
        P.dma("pool", WT.v(0, 1024), sgu_wT_d[l], (), WT.k(0, 1024))
        LNG, LNGk = pg(9, 8)
        LNB, LNBk = pg(17, 8)
        P.dma("sp", LNG, sgu_ln_d[l][:, 0:1024], (), LNGk)
        P.dma("sp", LNB, sgu_ln_d[l][:, 1024:2048], (), LNBk)
        BSB, BSBk = OB.v(0, T), OB.k(0, T)
        P.dma("sp", BSB, sgu_bs_d[l], (), BSBk)

        def gelu(x, xk, t1, t1k, out, outk):
            act(t1, x, AF.Square, xk, t1k)
            ts(t1, t1, 0.044715, 1.0, ALU.mult, ALU.add, t1k, t1k)
            tt(t1, t1, x, ALU.mult, t1k + xk, t1k)
            act(t1, t1, AF.Sigmoid, t1k, t1k, scale=1.5957691216057308)
            tt(out, t1, x, ALU.mult, t1k + xk, outk)

        for g in range(8):
            if g % 2 == 0:
                slab = wslab()
                wload(slab, 0, w_in_d[l][:, OFF_U + g * 128: OFF_U + g * 128 + 256], NCK, 256)
            proj_fm(slab, 0, 256, (g % 2) * 128, bufdst(TA), None, 0)
            gelu(TA.v(0, T), TA.k(0, T), TB.v(0, T), TB.k(0, T), big(g), bigk(g))
        WV0 = 8 * T
        wvk = BIG.k(WV0, 16 * T)
        wv3 = BIG.v(WV0, 16 * T).rearrange("p (k n) -> p k n", k=NCK)
        for hf in range(2):
            P.dma("pool", wv3[:, :, hf * 512:(hf + 1) * 512],
                  w_in_d[l][:, OFF_V + hf * 512: OFF_V + (hf + 1) * 512].rearrange("(k p) n -> p k n", p=128),
                  (), wvk)
        for n in range(8):
            bq = 6
            for half in range(2):
                for q in range(2):
                    cq = half * 512 + q * 256
                    for k in range(NCK):
                        mm(ps(bq + half, q * 256, 256), hb(k, n * 128, 128), BIG.v(WV0 + k * T + cq, 256),
                           k == 0, k == NCK - 1, wvk + hbk(k, n * 128, 128), psk(bq + half))
            for half in range(2):
                cp(TA.v(half * 512, 512), ps(bq + half), psk(bq + half), TA.k(0, T), eng="act")
            gelu(TA.v(0, T), TA.k(0, T), TB.v(0, T), TB.k(0, T), TC.v(0, T), TC.k(0, T))
            st, stk = PT.v(0, 8), PT.k(0, 8)
            mu = SS.v(128, 1)
            muk = SS.k(128, 8)
            P.op("dve", lambda e: e.tensor_reduce(SS.v(128, 1), TC.v(0, T), mybir.AxisListType.X, ALU.add),
                 TC.k(0, T), muk)
            ts(SS.v(129, 1), SS.v(128, 1), -1.0 / 1024, None, ALU.mult, None, muk, muk)
            ts(TC.v(0, T), TC.v(0, T), SS.v(129, 1), None, ALU.add, None, TC.k(0, T) + muk, TC.k(0, T))
            act(TB.v(0, T), TC.v(0, T), AF.Square, TC.k(0, T), TB.k(0, T))
            P.op("dve", lambda e: e.tensor_reduce(SS.v(130, 1), TB.v(0, T), mybir.AxisListType.X, ALU.add),
                 TB.k(0, T), muk)
            act(SS.v(131, 1), SS.v(130, 1), AF.Sqrt, muk, muk, scale=1.0 / 1024, bias=EPS)
            recip(SS.v(131, 1), SS.v(131, 1), muk, muk)
            stt(TC.v(0, T), TC.v(0, T), SS.v(131, 1), LNG, ALU.mult, ALU.mult, TC.k(0, T) + muk + LNGk, TC.k(0, T))
            VN, VNk = PT.v(0, T), PT.k(0, T)
            tt(VN, TC.v(0, T), LNB, ALU.add, TC.k(0, T) + LNBk, VNk)
            for half in range(2):
                b = psb()
                for gq in range(4):
                    g = half * 4 + gq
                    mm(ps(b, gq * 128, 128), PT.v(g * 128, 128), WT.v(g * 128, 128), True, True,
                       VNk + WT.k(0, 1024), psk(b))
                tmp, tmpk = TA.v(0, 512), TA.k(0, 512)
                tt(tmp, ps(b), OB.v(half * 512, 512), ALU.add, psk(b) + BSBk, tmpk)
                for gq in range(4):
                    g = half * 4 + gq
                    tt(big(g, n * 128, 128), big(g, n * 128, 128), TA.v(gq * 128, 128), ALU.mult,
                       bigk(g, n * 128, 128) + tmpk, bigk(g, n * 128, 128))

    def mixer(l):
        P.dma("sp", PCOL.v(0, NPC), pcol_d[l], (), PCK)
        P.dma("sp", PBC.v(0, 32), pbc_d[l], (), PBC.k(0, 64))
        act(PBC.v(32, 16), PBC.v(0, 16), AF.Exp, PBC.k(0, 64), PBC.k(0, 64))
        ts(PBC.v(32, 16), PBC.v(32, 16), -1.0, None, ALU.mult, None, PBC.k(0, 64), PBC.k(0, 64))
        rmsnorm(PC_NMG)
        if stage < 1:
            return
        halo_exchange()
        if stage < 2:
            return
        sgu(l)
        if stage < 3:
            return
        gdn_gates(l)
        if stage < 4:
            return
        Q_i, K_i, V_i = 16, 17, 18

        def head_pre(h):
            s1 = wslab()
            wload(s1, 0, w_in_d[l][:, h * 128: h * 128 + 128], NCK, 128)
            wload(s1, 2048, w_in_d[l][:, 1024 + h * 128: 1024 + h * 128 + 128], NCK, 128)
            s2 = wslab()
            wload(s2, 0, w_in_d[l][:, 2048 + h * 128: 2048 + h * 128 + 128], NCK, 128)
            wload(s2, 2048, w_in_d[l][:, OFF_Z + h * 128: OFF_Z + h * 128 + 128], NCK, 128)
            yield from proj_fm_g(s2, 2048, 128, 0, bigdst(8 + h), None, 0, func=AF.Silu)
            for j, (slab, off, dsti, norm) in enumerate(((s1, 0, Q_i, True), (s1, 2048, K_i, True), (s2, 0, V_i, False))):
                pdst = lambda s, n: (TA.v(s, n), TA.k(s, n))
                memset(TA.v(0, 2), 0.0, TA.k(0, 2))
                yield from proj_fm_g(slab, off, 128, 0, pdst, None, 2)
                yield from conv_taps_g(TA, TB, PC_QCW + (j * 8 + h) * 5, 5)
                yield from l2n_g(TB, dsti, norm)

        def gnorm_g(h):
            b = psb(2)
            for tb in range(2):
                sq, sqk = PT.v(tb * 512, 512), PT.k(tb * 512, 512)
                act(sq, OB.v(tb * 512, 512), AF.Square, OB.k(0, T), sqk)
                mm(ps(b + tb), ONB[0], sq, True, True, ONB[1] + sqk, psk(b + tb))
            for tb in range(2):
                act(TC.v(tb * 512, 512), ps(b + tb), AF.Ln, psk(b + tb), TC.k(0, T), scale=1.0 / 128, bias=EPS)
            act(TC.v(0, T), TC.v(0, T), AF.Exp, TC.k(0, T), TC.k(0, T), scale=-0.5)
            yield
            tt(TC.v(0, T), TC.v(0, T), OB.v(0, T), ALU.mult, TC.k(0, T) + OB.k(0, T), TC.k(0, T))
            yield
            stt(big(8 + h), TC.v(0, T), pc(PC_GNG), big(8 + h), ALU.mult, ALU.mult,
                TC.k(0, T) + PCK + bigk(8 + h), bigk(8 + h))
            yield

        run(head_pre(0))
        for h in range(nheads):
            memset(SS.v(0, 128), 0.0, SS.k(0, 128))
            gdn_prep(0, h, Q_i, K_i, V_i)
            run(gdn_scan_gen(0, h, True), prep_front(1, h, Q_i, K_i, 0))
            state_exchange_issue(h % 2)
            gdn_prep(1, h, Q_i, K_i, V_i, skip_front0=True)
            state_exchange_finish()
            post = seq(gdn_scan_gen(1, h, False), gnorm_g(h))
            if h + 1 < nheads:
                run(post, head_pre(h + 1))
            else:
                run(post)
        if stage < 5:
            return
        def mg(c, tb):
            if tb == 0:
                s_ = 16 * T + c * 512
                return BIG.v(s_, 512), BIG.k(s_, 512)
            if c < 12:
                return PG.v(c * 256, 256).bitcast(BF16), PG.k(c * 256, 256)
            return OB.v((c - 12) * 256, 256).bitcast(BF16), OB.k((c - 12) * 256, 256)

        for c in range(NCK):
            sa = wslab()
            wload(sa, 0, w_a_d[l][:, c * 128:(c + 1) * 128], 8, 128)
            wload(sa, 1024, w_b_d[l][:, c * 128:(c + 1) * 128], 8, 128)
            sg = wslab()
            wload(sg, 0, w_in_d[l][:, OFF_GA + c * 128: OFF_GA + (c + 1) * 128], NCK, 128)
            wload(sg, 2048, w_in_d[l][:, OFF_GB + c * 128: OFF_GB + (c + 1) * 128], NCK, 128)
            for tb in range(2):
                s0, b = tb * 512, 4 * tb
                for k in range(8):
                    mm(ps(b), wv(sa, 0, 8, 128, k), big(8 + k, s0, 512), k == 0, k == 7,
                       sa.k(0, 1024) + bigk(8 + k, s0, 512), psk(b))
            for tb in range(2):
                s0, b = tb * 512, 4 * tb
                for k in range(8):
                    mm(ps(b + 1), wv(sa, 1024, 8, 128, k), big(k, s0, 512), k == 0, k == 7,
                       sa.k(1024, 1024) + bigk(k, s0, 512), psk(b + 1))
            for tb in range(2):
                s0, b = tb * 512, 4 * tb
                for k in range(NCK):
                    mm(ps(b + 2), wv(sg, 0, NCK, 128, k), hb(k, s0, 512), k == 0, k == NCK - 1,
                       sg.k(0, 2048) + hbk(k, s0, 512), psk(b + 2))
            for tb in range(2):
                s0, b = tb * 512, 4 * tb
                for k in range(NCK):
                    mm(ps(b + 3), wv(sg, 2048, NCK, 128, k), hb(k, s0, 512), k == 0, k == NCK - 1,
                       sg.k(2048, 2048) + hbk(k, s0, 512), psk(b + 3))
            for tb in range(2):
                b = 4 * tb
                SG = TA if tb == 0 else TC
                act(SG.v(0, 512), ps(b + 2), AF.Sigmoid, psk(b + 2), SG.k(0, 512))
                act(SG.v(512, 512), ps(b + 3), AF.Sigmoid, psk(b + 3), SG.k(512, 512))
                tt(TB.v(tb * 512, 512), ps(b), SG.v(0, 512), ALU.mult, psk(b) + SG.k(0, 512), TB.k(tb * 512, 512))
                tt(SG.v(512, 512), ps(b + 1), SG.v(512, 512), ALU.mult, psk(b + 1) + SG.k(512, 512), SG.k(512, 512))
                m, mk_ = mg(c, tb)
                tt(m, TB.v(tb * 512, 512), SG.v(512, 512), ALU.add, TB.k(tb * 512, 512) + SG.k(512, 512), mk_)
        for co in range(NCK):
            so = wslab()
            wload(so, 0, w_out_d[l][:, co * 128:(co + 1) * 128], NCK, 128)
            for tb in range(2):
                s0 = tb * 512
                b = psb()
                for k in range(NCK):
                    m, mk_ = mg(k, tb)
                    mm(ps(b), wv(so, 0, NCK, 128, k), m, k == 0, k == NCK - 1, so.k(0, 2048) + mk_, psk(b))
                tt(xs(co, s0, 512), xs(co, s0, 512), ps(b), ALU.add, xsk(co, s0, 512) + psk(b), xsk(co, s0, 512))

    def ffn(l):
        rmsnorm(PC_NFG)
        halo_exchange()
        memset(TA.v(0, 1), 0.0, TA.k(0, 1))
        memset(TC.v(0, 1), 0.0, TC.k(0, 1))
        for qd in range(4):
            for j in range(11):
                cg = qd * 11 + j
                slab = wslab()
                wload(slab, 0, w_up_d[l][:, cg * 128:(cg + 1) * 128], NCK, 128)
                wload(slab, 2048, w_up_d[l][:, D_FF + cg * 128: D_FF + (cg + 1) * 128], NCK, 128)
                proj_fm(slab, 0, 128, 0, lambda s, n: (TA.v(s, n), TA.k(s, n)), None, 1)
                conv_taps(TA, TB, PC_FCW + cg * 3, 3, PC_FCB + cg)
                proj_fm(slab, 2048, 128, 0, lambda s, n: (TC.v(s, n), TC.k(s, n)), None, 1)
                conv_taps(TC, OB, PC_FCW + (44 + cg) * 3, 3, PC_FCB + 44 + cg)
                act(TB.v(0, T), TB.v(0, T), AF.Silu, TB.k(0, T), TB.k(0, T))
                tt(big(j), TB.v(0, T), OB.v(0, T), ALU.mult, TB.k(0, T) + OB.k(0, T), bigk(j))
            for co in range(NCK):
                slab = wslab()
                wload(slab, 0, w_down_d[l][qd * 1408:(qd + 1) * 1408, co * 128:(co + 1) * 128], 11, 128)
                for tb in range(2):
                    b = psb()
                    for k in range(11):
                        mm(ps(b), wv(slab, 0, 11, 128, k), big(k, tb * 512, 512), k == 0, k == 10,
                           slab.k(0, 1408) + bigk(k, tb * 512, 512), psk(b))
                    tt(xs(co, tb * 512, 512), xs(co, tb * 512, 512), ps(b), ALU.add,
                       xsk(co, tb * 512, 512) + psk(b), xsk(co, tb * 512, 512))

    for l in range(depth):
        mixer(l)
        if stage >= 6:
            ffn(l)

    outs = []
    for tb in range(2):
        s0 = tb * 512
        b = psb()
        for c in range(NCK):
            sq = PT.v((c % 2) * 512, 512)
            sqk = PT.k((c % 2) * 512, 512)
            act(sq, xs(c, s0, 512), AF.Square, xsk(c, s0, 512), sqk)
            mm(ps(b), ONB[0], sq, c == 0, c == NCK - 1, ONB[1] + sqk, psk(b))
        r, rk = TA.v(0, 512), TA.k(0, 512)
        act(r, ps(b), AF.Ln, psk(b), rk, scale=1.0 / D_MODEL, bias=EPS)
        act(r, r, AF.Exp, rk, rk, scale=-0.5)
        for c in range(NCK):
            stt(xs(c, s0, 512), xs(c, s0, 512), FNG.v(c, 1), r, ALU.mult, ALU.mult,
                xsk(c, s0, 512) + FNG.k(0, 16) + rk, xsk(c, s0, 512))
    for c in range(NCK):
        outs.append(P.dma("sp", y_d[:, c, :], xs(c), xsk(c), [("y", c)]))
    P.op("sp", lambda e: None, [("y", c) for c in range(NCK)], ())
    P.emit()
    return nc, P


def _masks():
    i = np.arange(128)
    m, c = i[:, None], i[None, :]
    UI = (m <= c).astype(np.float32)
    LS = (m > c).astype(np.float32)
    LI = (m >= c).astype(np.float32)
    US = (m < c).astype(np.float32)
    masks = np.concatenate([UI, LS, LI, US], axis=1)
    nlm = np.zeros((2, 7, 128, 128), np.float32)
    for lev in range(7):
        b = 1 << lev
        same = (m // (2 * b)) == (c // (2 * b))
        s_first = (m % (2 * b)) < b
        c_first = (c % (2 * b)) < b
        nlm[0, lev] = -1.0 * (same & s_first & ~c_first)
        nlm[1, lev] = -1.0 * (same & ~s_first & c_first)
    nlm = nlm.transpose(2, 0, 1, 3).reshape(128, 2 * 7 * 128)
    return np.ascontiguousarray(masks), np.ascontiguousarray(nlm)


def _prep_inputs(inp, depth):
    f = lambda a: np.ascontiguousarray(np.asarray(a, dtype=np.float32))
    x = f(inp["x"])
    w_in = f(inp["w_in"])[:depth]
    shared = {
        "w_in": w_in,
        "w_a": f(inp["w_branch_a"])[:depth], "w_b": f(inp["w_branch_b"])[:depth],
        "w_out": f(inp["w_out"])[:depth], "w_up": f(inp["w_up"])[:depth], "w_down": f(inp["w_down"])[:depth],
        "fng": np.ascontiguousarray(f(inp["final_norm_g"]).reshape(16, 128).T),
    }
    masks, nlm = _masks()
    shared["masks"] = masks
    shared["nlm"] = nlm
    per_par = []
    for par in range(2):
        rev = par == 1
        d = {}
        ab = w_in[:, :, OFF_A:OFF_A + 32].copy()
        alog = f(inp["a_log"])[:depth].copy()
        dtb = f(inp["dt_bias"])[:depth].copy()
        qcw = f(inp["qkv_conv_w"])[:depth].copy()
        fcw = f(inp["ffn_conv_w"])[:depth].copy()
        sw = f(inp["sgu_w"])[:depth].copy()
        sb = f(inp["sgu_b"])[:depth].copy()
        if rev:
            ab = np.concatenate([ab[:, :, 8:16], ab[:, :, 0:8], ab[:, :, 24:32], ab[:, :, 16:24]], axis=2)
            alog = alog[:, ::-1]
            dtb = dtb[:, ::-1]
            qcw = qcw[:, ::-1]
            fcw = fcw[:, ::-1]
            sw = sw[:, :, ::-1, ::-1]
            sb = sb[:, :, ::-1]
        d["w_ab"] = np.ascontiguousarray(ab)
        pcol = np.zeros((depth, 128, NPC), np.float32)
        pcol[:, :, PC_NMG:PC_NMG + 16] = f(inp["norm_mix_g"])[:depth].reshape(depth, 16, 128).transpose(0, 2, 1)
        pcol[:, :, PC_NFG:PC_NFG + 16] = f(inp["norm_ffn_g"])[:depth].reshape(depth, 16, 128).transpose(0, 2, 1)
        pcol[:, :, PC_QCW:PC_QCW + 120] = qcw.reshape(depth, 5, 24, 128).transpose(0, 3, 2, 1).reshape(depth, 128, 120)
        pcol[:, :, PC_FCW:PC_FCW + 264] = fcw.reshape(depth, 3, 88, 128).transpose(0, 3, 2, 1).reshape(depth, 128, 264)
        pcol[:, :, PC_FCB:PC_FCB + 88] = f(inp["ffn_conv_b"])[:depth].reshape(depth, 88, 128).transpose(0, 2, 1)
        pcol[:, :, PC_GNG] = f(inp["gdn_norm_g"])[:depth]
        d["pcol"] = pcol
        pbc = np.zeros((depth, 128, 32), np.float32)
        pbc[:, :, 0:16] = alog.reshape(depth, 1, 16)
        pbc[:, :, 16:32] = dtb.reshape(depth, 1, 16)
        d["pbc"] = pbc
        d["sgu_ln"] = np.ascontiguousarray(np.broadcast_to(
            np.concatenate([f(inp["sgu_ln_g"])[:depth], f(inp["sgu_ln_b"])[:depth]], axis=1)[:, None, :], (depth, 128, 2048)))
        d["sgu_bs"] = np.ascontiguousarray(np.broadcast_to(sb.reshape(depth, 1, 1024), (depth, 128, 1024)))
        d["sgu_wT"] = np.ascontiguousarray(sw.transpose(0, 3, 1, 2).reshape(depth, 128, 1024))
        fl = np.zeros((128, 2), np.float32)
        fl[:, 1 - par] = 1.0
        d["flags"] = fl
        per_par.append(d)
    in_maps = []
    for core in range(8):
        b, par = core // 2, core % 2
        xx = x[b, par * T:(par + 1) * T]
        if par == 1:
            xx = xx[::-1]
        xt = np.ascontiguousarray(xx.T.reshape(16, 128, T).transpose(1, 0, 2))
        m = dict(shared)
        m.update(per_par[par])
        m["x"] = xt
        in_maps.append(m)
    return in_maps


def _assemble(res):
    out = np.zeros((BATCH, SEQ, D_MODEL), np.float32)
    for core in range(8):
        b, par = core // 2, core % 2
        y = np.asarray(res[core]["y"])
        yt = y.transpose(1, 0, 2).reshape(D_MODEL, T).T
        if par == 1:
            yt = yt[::-1]
        out[b, par * T:(par + 1) * T] = yt
    return out


_NC_CACHE = {}


def kernel(**inputs):
    depth = 4
    if depth not in _NC_CACHE:
        _NC_CACHE[depth] = build(depth)[0]
    nc = _NC_CACHE[depth]
    in_maps = _prep_inputs(inputs, depth)
    res = run_bass_kernel_spmd(nc, in_maps, core_ids=list(range(8)))
    return _assemble(res.results)
```

```python
import os
import numpy as np
import concourse.bass as bass
import concourse.mybir as mybir
from concourse.bass_utils import run_bass_kernel_spmd

F32 = mybir.dt.float32
BF16 = mybir.dt.bfloat16
AF = mybir.ActivationFunctionType
ALU = mybir.AluOpType

D_MODEL = 2048
SEQ = 2048
BATCH = 4
T = 1024
NCK = 16
D_FF = 5632
N_IN = 10272
OFF_Z, OFF_A, OFF_B, OFF_U, OFF_V, OFF_GA, OFF_GB = 3072, 4096, 4112, 4128, 5152, 6176, 8224
EPS = 1e-6
NPC = 16 + 16 + 120 + 264 + 88 + 1
PC_NMG, PC_NFG, PC_QCW, PC_FCW, PC_FCB, PC_GNG = 0, 16, 32, 152, 416, 504
NS_DMA = 8


class Op:
    __slots__ = ("eng", "fn", "deps", "signal", "ticket", "is_dma", "idx")


class Prog:
    ENGS = ["pe", "act", "dve", "pool", "sp"]

    def __init__(self, nc):
        self.nc = nc
        self.ops = []
        self.last_w = {}
        self.readers = {}

    def op(self, eng, fn, reads=(), writes=(), is_dma=False):
        o = Op()
        o.eng, o.fn, o.is_dma, o.signal, o.ticket = eng, fn, is_dma, False, None
        o.idx = len(self.ops)
        writes = list(writes) + [r for r in reads if r[0] == "ps" and r not in writes]
        deps = set()
        for r in reads:
            w = self.last_w.get(r)
            if w is not None:
                deps.add(w)
        for r in writes:
            w = self.last_w.get(r)
            if w is not None:
                deps.add(w)
            for rd in self.readers.get(r, ()):
                deps.add(rd)
        for r in reads:
            self.readers.setdefault(r, []).append(o.idx)
        for r in writes:
            self.last_w[r] = o.idx
            self.readers[r] = []
        best = {}
        red = set()
        for d in deps:
            od = self.ops[d]
            if od.is_dma:
                red.add(d)
            else:
                if od.eng == "pe" and eng == "pe" and not is_dma:
                    continue
                if od.eng not in best or best[od.eng] < d:
                    best[od.eng] = d
        red.update(best.values())
        o.deps = red
        self.ops.append(o)
        return o

    def dma(self, q, out, in_, reads=(), writes=()):
        return self.op(q, lambda e: e.dma_start(out=out, in_=in_), reads, writes, is_dma=True)

    def emit(self):
        nc = self.nc
        ops = self.ops
        sem = {e: nc.alloc_semaphore("sem_" + e) for e in self.ENGS}
        dsem = {e: [nc.alloc_semaphore("dsem_%s_%d" % (e, i)) for i in range(NS_DMA)] for e in ("pool", "sp", "act")}
        dcount = {e: 0 for e in dsem}
        dlast = {e: [None] * NS_DMA for e in dsem}
        for o in ops:
            for d in o.deps:
                ops[d].signal = True
        cnt = {e: 0 for e in self.ENGS}
        for o in ops:
            if o.is_dma:
                k = dcount[o.eng]
                dcount[o.eng] += 1
                slot = k % NS_DMA
                o.ticket = (dsem[o.eng][slot], 16 * (k // NS_DMA + 1))
                if dlast[o.eng][slot] is not None:
                    o.deps.add(dlast[o.eng][slot])
                dlast[o.eng][slot] = o.idx
            elif o.signal:
                cnt[o.eng] += 1
                o.ticket = (sem[o.eng], cnt[o.eng])
        per = {e: [] for e in self.ENGS}
        for o in ops:
            per[o.eng].append(o)
        self.stats = {e: len(per[e]) for e in per}
        self.stats["sig"] = dict(cnt)

        def mk(ename):
            def body(e):
                waited = {}
                for o in per[ename]:
                    for d in sorted(o.deps):
                        s, v = ops[d].ticket
                        key = id(s)
                        if waited.get(key, 0) < v:
                            e.wait_ge(s, v)
                            waited[key] = v
                    inst = o.fn(e)
                    if inst is None:
                        continue
                    if o.is_dma:
                        inst.then_inc(o.ticket[0], 16)
                    elif o.signal:
                        inst.then_inc(o.ticket[0], 1)
            return body

        with nc.Block() as block:
            block.tensor(mk("pe"))
            block.scalar(mk("act"))
            block.vector(mk("dve"))
            block.gpsimd(mk("pool"))
            block.sync(mk("sp"))


class Buf:
    GR = 128

    def __init__(self, nc, name, n, dt):
        self.name = name
        self.n = n
        self.t = nc.alloc_sbuf_tensor(name, [128, n], dt)

    def k(self, s, n):
        return [(self.name, g) for g in range(s // self.GR, (s + n - 1) // self.GR + 1)]

    def v(self, s, n):
        return self.t[:, s:s + n]


def build(depth=4, dbg=None, rg=None, stage=99, nheads=8, noex=False, hstage=9):
    nc = bass.Bass("TRN2", target_bir_lowering=False)
    P = Prog(nc)
    dt_in = lambda name, shape: nc.dram_tensor(name, list(shape), F32, kind="ExternalInput").ap()
    x_d = dt_in("x", [128, NCK, T])
    w_in_d = dt_in("w_in", [depth, D_MODEL, N_IN])
    w_ab_d = dt_in("w_ab", [depth, D_MODEL, 32])
    w_a_d = dt_in("w_a", [depth, 1024, D_MODEL])
    w_b_d = dt_in("w_b", [depth, 1024, D_MODEL])
    w_out_d = dt_in("w_out", [depth, D_MODEL, D_MODEL])
    w_up_d = dt_in("w_up", [depth, D_MODEL, 2 * D_FF])
    w_down_d = dt_in("w_down", [depth, D_FF, D_MODEL])
    pcol_d = dt_in("pcol", [depth, 128, NPC])
    pbc_d = dt_in("pbc", [depth, 128, 32])
    sgu_ln_d = dt_in("sgu_ln", [depth, 128, 2048])
    sgu_bs_d = dt_in("sgu_bs", [depth, 128, 1024])
    sgu_wT_d = dt_in("sgu_wT", [depth, 128, 1024])
    fng_d = dt_in("fng", [128, NCK])
    flags_d = dt_in("flags", [128, 2])
    masks_d = dt_in("masks", [128, 4 * 128])
    nlm_d = dt_in("nlm", [128, 2 * 7 * 128])
    y_d = nc.dram_tensor("y", [128, NCK, T], F32, kind="ExternalOutput").ap()
    dbg_d = None
    if dbg:
        dbg_d = nc.dram_tensor("dbg", [128, dbg], F32, kind="ExternalOutput").ap()
    cc_h_src = nc.dram_tensor("cc_h_src", [128, 32], BF16)
    cc_h_dst = nc.dram_tensor("cc_h_dst", [256, 32], BF16)
    cc_s_src = [nc.dram_tensor("cc_s_src%d" % i, [128, 128], F32) for i in range(2)]
    cc_s_dst = [nc.dram_tensor("cc_s_dst%d" % i, [256, 128], F32) for i in range(2)]
    RG = rg or [[0, 1], [2, 3], [4, 5], [6, 7]]

    XS = Buf(nc, "XS", NCK * T, F32)
    HW = 1028
    HB = Buf(nc, "HB", NCK * HW, BF16)
    BIG = Buf(nc, "BIG", 24 * T, BF16)
    WS = [Buf(nc, "WS%d" % i, 4096, BF16) for i in range(2)]
    TW = 1032
    TA = Buf(nc, "TA", TW, F32)
    TB = Buf(nc, "TB", TW, F32)
    TC = Buf(nc, "TC", TW, F32)
    OB = Buf(nc, "OB", TW, F32)
    PG = Buf(nc, "PG", 25 * 128, F32)
    NET = Buf(nc, "NET", 7 * 128, F32)
    NLM = Buf(nc, "NLM", 2 * 7 * 128, BF16)
    PT = Buf(nc, "PT", 8 * 128, BF16)
    SS = Buf(nc, "SS", 3 * 128, F32)
    CF = Buf(nc, "CF", 6 * 128, F32)
    CB = Buf(nc, "CB", 2 * 128, BF16)
    PCOL = Buf(nc, "PCOL", NPC + 7, F32)
    PBC = Buf(nc, "PBC", 64, F32)
    FNG = Buf(nc, "FNG", 16, F32)
    FLG = Buf(nc, "FLG", 2, F32)
    WT = Buf(nc, "WT", 1024, BF16)
    HX = Buf(nc, "HX", 2 * 32 + 32, BF16)
    PS = nc.alloc_psum_tensor("ps", [128, 8, 512], F32)

    def ps(b, s=0, n=512):
        return PS[:, b, s:s + n]

    def psk(b):
        return [("ps", b)]

    def xs(c, s=0, n=T):
        return XS.v(c * T + s, n)

    def xsk(c, s=0, n=T):
        return XS.k(c * T + s, n)

    def hb(c, s=0, n=T):
        return HB.v(c * HW + s, n)

    def hbk(c, s=0, n=T):
        return [("HB", c, (s + i) // 512) for i in range(0, n, 512)] if s < 1024 else [("HBh", c)]

    def big(i, s=0, n=T):
        return BIG.v(i * T + s, n)

    def bigk(i, s=0, n=T):
        return BIG.k(i * T + s, n)

    UI, LS, LI, US, IDF, ONF = [(CF.v(i * 128, 128), CF.k(i * 128, 128)) for i in range(6)]
    IDB, ONB = [(CB.v(i * 128, 128), CB.k(i * 128, 128)) for i in range(2)]

    def pc(off, n=1):
        return PCOL.v(off, n)

    PCK = [("PCOL", 0)]

    def mm(out, lhsT, rhs, start, stop, R, W):
        P.op("pe", lambda e: e.matmul(out, lhsT, rhs, start=start, stop=stop), R, W)

    def act(out, in_, func, R, W, scale=None, bias=None):
        kw = {}
        if scale is not None:
            kw["scale"] = scale
        if bias is not None:
            kw["bias"] = bias
        P.op("act", lambda e: e.activation(out, in_, func, **kw), R, W)

    def tt(out, a, b, op, R, W, eng="dve"):
        P.op(eng, lambda e: e.tensor_tensor(out, a, b, op), R, W)

    def ts(out, in0, s1, s2, op0, op1, R, W, eng="dve"):
        if op1 is None:
            P.op(eng, lambda e: e.tensor_scalar(out, in0, s1, None, op0), R, W)
        else:
            P.op(eng, lambda e: e.tensor_scalar(out, in0, s1, s2, op0, op1), R, W)

    def stt(out, in0, sc, in1, op0, op1, R, W):
        P.op("dve", lambda e: e.scalar_tensor_tensor(out, in0, sc, in1, op0, op1), R, W)

    def cp(out, in_, R, W, eng="dve"):
        if eng == "act":
            P.op("act", lambda e: e.activation(out, in_, AF.Copy), R, W)
        else:
            P.op(eng, lambda e: e.tensor_copy(out, in_), R, W)

    def recip(out, in_, R, W):
        P.op("dve", lambda e: e.reciprocal(out, in_), R, W)

    def memset(ap, val, W, eng="dve"):
        P.op(eng, lambda e: e.memset(ap, val), (), W)

    ws_i = [0]

    def wslab():
        b = WS[ws_i[0] % 2]
        ws_i[0] += 1
        return b

    def wload(slab, off, src_ap, kc, ncols):
        dst = slab.v(off, kc * ncols).rearrange("p (k n) -> p k n", k=kc)
        src = src_ap.rearrange("(k p) n -> p k n", p=128)
        P.dma("pool", dst, src, (), slab.k(off, kc * ncols))

    def wv(slab, off, kc, ncols, k, c0=0, n=128):
        return slab.v(off + k * ncols + c0, n)

    P.dma("sp", CF.v(0, 512), masks_d[:, :], (), CF.k(0, 512))
    P.dma("pool", NLM.v(0, 1792), nlm_d[:, :], (), NLM.k(0, 1792))
    P.dma("sp", FNG.v(0, 16), fng_d[:, :], (), FNG.k(0, 16))
    P.dma("sp", FLG.v(0, 2), flags_d[:, :], (), FLG.k(0, 2))
    for c in range(NCK):
        P.dma("sp", xs(c), x_d[:, c, :], (), xsk(c))
    tt(IDF[0], UI[0], LI[0], ALU.mult, UI[1] + LI[1], IDF[1])
    memset(ONF[0], 1.0, ONF[1])
    cp(IDB[0], IDF[0], IDF[1], IDB[1])
    cp(ONB[0], ONF[0], ONF[1], ONB[1])
    memset(TA.v(0, TW), 0.0, TA.k(0, TW))
    memset(TB.v(0, TW), 0.0, TB.k(0, TW))
    memset(TC.v(0, TW), 0.0, TC.k(0, TW))
    memset(OB.v(0, TW), 0.0, OB.k(0, TW))
    memset(HB.v(0, NCK * HW), 0.0, [k for c in range(NCK) for k in hbk(c) + hbk(c, 1024, 4)])

    psrot = [0]

    def psb(n=1):
        b = psrot[0]
        psrot[0] = (psrot[0] + n) % 6
        if b + n > 6:
            b = 0
            psrot[0] = n % 6
        return b

    def rmsnorm(goff):
        for tb in range(2):
            s0 = tb * 512
            b = psb()
            for c in range(NCK):
                sq = PT.v((c % 2) * 512, 512)
                sqk = PT.k((c % 2) * 512, 512)
                act(sq, xs(c, s0, 512), AF.Square, xsk(c, s0, 512), sqk)
                mm(ps(b), ONB[0], sq, c == 0, c == NCK - 1, ONB[1] + sqk, psk(b))
            r = TA.v(0, 512)
            rk = TA.k(0, 512)
            act(r, ps(b), AF.Ln, psk(b), rk, scale=1.0 / D_MODEL, bias=EPS)
            act(r, r, AF.Exp, rk, rk, scale=-0.5)
            for c in range(NCK):
                stt(hb(c, s0, 512), xs(c, s0, 512), pc(goff + c), r, ALU.mult, ALU.mult,
                    xsk(c, s0, 512) + PCK + rk, hbk(c, s0, 512))

    def halo_exchange():
        src = HX.v(64, 32).rearrange("p (c t) -> p c t", t=2)
        hsrc = HB.t[:, :].rearrange("p (c w) -> p c w", w=HW)[:, :, 1022:1024]
        cp(src, hsrc, [k for c in range(NCK) for k in hbk(c, 512, 512)], HX.k(64, 32))
        P.dma("pool", cc_h_src.ap(), HX.v(64, 32), HX.k(64, 32), [("cc_h_src",)])
        P.op("pool", lambda g: g.collective_compute("AllGather", ALU.bypass, replica_groups=RG,
                                                    ins=[cc_h_src.ap().opt()], outs=[cc_h_dst.ap().opt()]),
             [("cc_h_src",)], [("cc_h_dst",)])
        P.dma("pool", HX.v(0, 32), cc_h_dst.ap()[0:128, :], [("cc_h_dst",)], HX.k(0, 32))
        P.dma("pool", HX.v(32, 32), cc_h_dst.ap()[128:256, :], [("cc_h_dst",)], HX.k(32, 32))
        ts(HX.v(0, 32), HX.v(0, 32), FLG.v(0, 1), None, ALU.mult, None, HX.k(0, 32) + FLG.k(0, 2), HX.k(0, 32))
        stt(HX.v(0, 32), HX.v(32, 32), FLG.v(1, 1), HX.v(0, 32), ALU.mult, ALU.add,
            HX.k(0, 64) + FLG.k(0, 2), HX.k(0, 32))
        pv = HX.v(0, 32).rearrange("p (c t) -> p c t", t=2)
        hdst = HB.t[:, :].rearrange("p (c w) -> p c w", w=HW)
        hk = [k for c in range(NCK) for k in hbk(c, 1024, 4)]
        cp(hdst[:, :, 1024:1025], pv[:, :, 1:2], HX.k(0, 32), hk)
        cp(hdst[:, :, 1025:1026], pv[:, :, 0:1], HX.k(0, 32), hk)

    def proj_fm_g(slab, off, ncols, col0, dst, dk, halo, evac="act", func=None):
        b = psb(2)
        for tb in range(2):
            for k in range(NCK):
                mm(ps(b + tb), wv(slab, off, NCK, ncols, k, col0), hb(k, tb * 512, 512), k == 0, k == NCK - 1,
                   slab.k(off, NCK * ncols) + hbk(k, tb * 512, 512), psk(b + tb))
        base = halo
        for tb in range(2):
            o = dst(base + tb * 512, 512)
            if func is not None:
                act(o[0], ps(b + tb), func, psk(b + tb), o[1])
            elif evac == "act":
                cp(o[0], ps(b + tb), psk(b + tb), o[1], eng="act")
            else:
                cp(o[0], ps(b + tb), psk(b + tb), o[1])
        if halo:
            b2 = psb()
            for k in range(NCK):
                mm(ps(b2, 0, 32), wv(slab, off, NCK, ncols, k, col0), hb(k, 996, 32), k == 0, k == NCK - 1,
                   slab.k(off, NCK * ncols) + hbk(k, 512, 512) + hbk(k, 1024, 4), psk(b2))
            o = dst(base + 1024, halo)
            cp(o[0], ps(b2, 28, halo), psk(b2), o[1])
        yield

    def proj_fm(*a, **kw):
        run(proj_fm_g(*a, **kw))

    def bufdst(buf):
        return lambda s, n: (buf.v(s, n), buf.k(s, n))

    def bigdst(i):
        return lambda s, n: (big(i, s, n), bigk(i, s, n))

    def l2n_g(accb, dsti, norm):
        a, ak = accb.v(0, T), accb.k(0, T)
        if not norm:
            act(big(dsti), a, AF.Silu, ak, bigk(dsti))
            yield
            return
        act(a, a, AF.Silu, ak, ak)
        yield
        b = psb(2)
        for tb in range(2):
            sq = big(dsti, tb * 512, 512)
            sqk = bigk(dsti, tb * 512, 512)
            act(sq, accb.v(tb * 512, 512), AF.Square, ak, sqk)
            mm(ps(b + tb), ONB[0], sq, True, True, ONB[1] + sqk, psk(b + tb))
        r, rk = TA.v(0, T), TA.k(0, T)
        for tb in range(2):
            act(TA.v(tb * 512, 512), ps(b + tb), AF.Ln, psk(b + tb), rk, bias=EPS)
        act(r, r, AF.Exp, rk, rk, scale=-0.5)
        yield
        tt(big(dsti), a, r, ALU.mult, ak + rk, bigk(dsti))
        yield

    def pg(i, n=1):
        return PG.v(i * 128, n * 128), PG.k(i * 128, n * 128)
    G_AB, G_G, G_BETA, G_NB, G_GC, G_EGC, G_EGR, G_DCH = 0, 2, 3, 4, 5, 6, 7, 8
    G_LG, G_DT, G_DTS, G_DTI, G_AT, G_Q, G_T0, G_T1, G_TT0, G_TT1, G_EGB = range(9, 20)
    def pco(n, j):
        s = 19 * T + (n * 5 + j) * 128
        return BIG.v(s, 128), BIG.k(s, 128)
    QKT_S = 0.08838834764831845

    def gdn_gates(l):
        slab = wslab()
        wload(slab, 0, w_ab_d[l], NCK, 32)
        ab, abk = pg(G_AB, 2)
        for n in range(8):
            b = psb()
            for k in range(NCK):
                mm(ps(b, 0, 32), hb(k, n * 128, 128), wv(slab, 0, NCK, 32, k, 0, 32), k == 0, k == NCK - 1,
                   slab.k(0, 512) + hbk(k, n * 128, 128), psk(b))
            cp(PG.v(G_AB * 128 + n * 32, 32), ps(b, 0, 32), psk(b), abk)
        ab3 = ab.rearrange("p (n c) -> p n c", c=32)
        g3 = pg(G_G)[0].rearrange("p (c n) -> p c n", n=8)
        gk = pg(G_G)[1]
        for n in range(8):
            tt(g3[:, :, n], ab3[:, n, 0:16], PBC.v(16, 16), ALU.add, abk + PBC.k(0, 64), gk)
        gfl = pg(G_G)[0]
        act(gfl, gfl, AF.Exp, gk, gk)
        act(gfl, gfl, AF.Ln, gk, gk, bias=1.0)
        for n in range(8):
            tt(g3[:, :, n], g3[:, :, n], PBC.v(32, 16), ALU.mult, gk + PBC.k(0, 64), gk)
        be3 = pg(G_BETA)[0].rearrange("p (c n) -> p c n", n=8)
        bek = pg(G_BETA)[1]
        for n in range(8):
            act(be3[:, :, n], ab3[:, n, 16:32], AF.Sigmoid, abk, bek)
        ts(pg(G_NB)[0], pg(G_BETA)[0], -1.0, None, ALU.mult, None, bek, pg(G_NB)[1])
        b = psb()
        mm(ps(b, 0, 64), UI[0], PG.v(G_G * 128, 64), True, True, UI[1] + gk, psk(b))
        mm(ps(b, 64, 64), LI[0], PG.v(G_G * 128 + 64, 64), True, True, LI[1] + gk, psk(b))
        b2 = psb()
        mm(ps(b2, 0, 128), ONF[0], gfl, True, True, ONF[1] + gk, psk(b2))
        cp(pg(G_GC)[0], ps(b, 0, 128), psk(b), pg(G_GC)[1])
        act(pg(G_EGC)[0], ps(b, 0, 128), AF.Exp, psk(b), pg(G_EGC)[1])
        tt(pg(G_EGR)[0], ps(b2, 0, 128), pg(G_GC)[0], ALU.subtract, psk(b2) + pg(G_GC)[1], pg(G_EGR)[1])
        act(pg(G_EGR)[0], pg(G_EGR)[0], AF.Exp, pg(G_EGR)[1], pg(G_EGR)[1])
        act(pg(G_DCH)[0], ps(b2, 0, 128), AF.Exp, psk(b2), pg(G_DCH)[1])

    def pgb(g0):
        return PG.v(g0 * 128, 256).bitcast(BF16), PG.k(g0 * 128, 256)

    def netb(i):
        return NET.v(i * 256, 256).bitcast(BF16), NET.k(i * 256, 256)

    ATH = [pgb(12), pgb(14)]
    def tcb(i):
        return TC.v(i * 256, 256).bitcast(BF16), TC.k(i * 256, 256)

    def wtb(i):
        return WT.v(i * 512, 512), WT.k(i * 512, 512)

    CSET = [(pgb(16), pgb(18), pgb(22), netb(0), netb(1), netb(2)),
            (tcb(0), tcb(1), tcb(2), tcb(3), wtb(0), wtb(1))]

    def c4(buf, j):
        return buf[0][:, j * 128:(j + 1) * 128]

    FSTOP = 999999999
    fcount = [0]

    def fgate():
        fcount[0] += 1
        return fcount[0] <= FSTOP

    def prep_front(d, h, Q_i, K_i, batch):
        TRI = UI if d == 0 else LI
        MINC, MSTR = (UI, US) if d == 0 else (LI, LS)
        ATh = ATH[batch]
        for j in range(4):
            n = batch * 4 + j
            col = (d * 8 + h) * 8 + n
            c0 = n * 128
            gcol = PG.v(G_G * 128 + col, 1)
            becol = PG.v(G_BETA * 128 + col, 1)
            gccol = PG.v(G_GC * 128 + col, 1)
            Lgk = pg(9)[1]
            Lhl = PG.v(9 * 128, 128).bitcast(BF16)
            Lh, Ll = Lhl[:, 0:128], Lhl[:, 128:256]
            if fgate():
                ts(Lh, TRI[0], gcol, None, ALU.mult, None, TRI[1] + pg(G_G)[1], Lgk)
            if fgate():
                stt(Ll, TRI[0], gcol, Lh, ALU.mult, ALU.subtract, TRI[1] + pg(G_G)[1] + Lgk, Lgk)
            b = psb()
            if fgate():
                mm(ps(b, 0, 128), ONB[0], Lh, True, False, ONB[1] + Lgk, psk(b))
                mm(ps(b, 0, 128), ONB[0], Ll, False, True, ONB[1] + Lgk, psk(b))
            EGB, EGBk = pg(11)
            if fgate():
                act(EGB, ps(b, 0, 128), AF.Exp, psk(b), EGBk)
            DT, DTk = pg(10)
            if fgate():
                ts(DT, ps(b, 0, 128), gccol, 0.0, ALU.subtract, ALU.min, psk(b) + pg(G_GC)[1], DTk)
            if fgate():
                act(DT, DT, AF.Exp, DTk, DTk)
            QgT, QgTk = pco(n, 2)
            if fgate():
                stt(QgT, big(Q_i, c0, 128), QKT_S, EGB, ALU.mult, ALU.mult, bigk(Q_i, c0, 128) + EGBk, QgTk)
            b = psb()
            if fgate():
                mm(ps(b, 0, 128), big(K_i, c0, 128), big(K_i, c0, 128), True, True, bigk(K_i, c0, 128), psk(b))
            if fgate():
                mm(ps(b, 128, 128), big(K_i, c0, 128), big(Q_i, c0, 128), True, True,
                   bigk(K_i, c0, 128) + bigk(Q_i, c0, 128), psk(b))
            if fgate():
                tt(DT, DT, MINC[0], ALU.mult, DTk + MINC[1], DTk)
            QKT, QKTk = pco(n, 3)
            if fgate():
                stt(QKT, ps(b, 128, 128), QKT_S, DT, ALU.mult, ALU.mult, psk(b) + DTk, QKTk)
            if fgate():
                stt(c4(ATh, j), ps(b, 0, 128), becol, DT, ALU.mult, ALU.mult, psk(b) + pg(G_BETA)[1] + DTk, ATh[1])
                yield

    def prep_chain(d, batch):
        ATh = ATH[batch]
        NETH, QH, TH, TL, TTH, TTL = CSET[batch]
        r4 = lambda buf: buf[0].rearrange("p (j c) -> p j c", j=4)
        for lev in range(7):
            mask = NLM.v((d * 7 + lev) * 128, 128).unsqueeze(1).broadcast_to([128, 4, 128])
            tt(r4(NETH), r4(ATh), mask, ALU.mult, ATh[1] + NLM.k((d * 7 + lev) * 128, 128), NETH[1], eng=("pool" if d == 0 else "dve"))
            bq = psb()
            for j in range(4):
                o = ps(bq, j * 128, 128)
                if lev == 0:
                    mm(o, IDB[0], IDB[0], True, False, IDB[1], psk(bq))
                    mm(o, c4(NETH, j), IDB[0], False, True, NETH[1] + IDB[1], psk(bq))
                else:
                    mm(o, IDB[0], IDB[0], True, False, IDB[1], psk(bq))
                    mm(o, c4(NETH, j), c4(TH, j), False, False, NETH[1] + TH[1], psk(bq))
                    mm(o, c4(NETH, j), c4(TL, j), False, True, NETH[1] + TL[1], psk(bq))
            if lev == 0:
                cp(TH[0], ps(bq), psk(bq), TH[1], eng="act")
                memset(TL[0], 0.0, TL[1])
                tt(r4(TTH), r4(NETH), IDB[0].unsqueeze(1).broadcast_to([128, 4, 128]), ALU.add,
                   NETH[1] + IDB[1], TTH[1])
                memset(TTL[0], 0.0, TTL[1])
                yield
                continue
            cp(QH[0], ps(bq), psk(bq), QH[1], eng="act")
            yield
            last = lev == 6
            if not last:
                bt = psb()
                for j in range(4):
                    o = ps(bt, j * 128, 128)
                    mm(o, c4(TTH, j), c4(QH, j), True, False, TTH[1] + QH[1], psk(bt))
                    mm(o, c4(TTL, j), c4(QH, j), False, True, TTL[1] + QH[1], psk(bt))
            btt = psb()
            for j in range(4):
                o = ps(btt, j * 128, 128)
                mm(o, c4(QH, j), c4(TTH, j), True, False, TTH[1] + QH[1], psk(btt))
                mm(o, c4(QH, j), c4(TTL, j), False, True, TTL[1] + QH[1], psk(btt))
            if not last:
                cp(TH[0], ps(bt), psk(bt), TH[1], eng="act")
                tt(TL[0], ps(bt), TH[0], ALU.subtract, psk(bt) + TH[1], TL[1])
            cp(TTH[0], ps(btt), psk(btt), TTH[1], eng="act")
            if not last:
                tt(TTL[0], ps(btt), TTH[0], ALU.subtract, psk(btt) + TTH[1], TTL[1])
            yield

    def prep_tail(d, h, K_i, V_i, batch):
        TTH = CSET[batch][4]
        for j in range(4):
            n = batch * 4 + j
            col = (d * 8 + h) * 8 + n
            c0 = n * 128
            becol = PG.v(G_BETA * 128 + col, 1)
            b = psb()
            mm(ps(b, 0, 128), big(K_i, c0, 128), IDB[0], True, True, bigk(K_i, c0, 128) + IDB[1], psk(b))
            mm(ps(b, 128, 128), big(V_i, c0, 128), IDB[0], True, True, bigk(V_i, c0, 128) + IDB[1], psk(b))
            Kg, Kgk = PT.v((j % 2) * 256, 128), PT.k((j % 2) * 256, 128)
            Vt, Vtk = PT.v((j % 2) * 256 + 128, 128), PT.k((j % 2) * 256 + 128, 128)
            Kd, Kdk = pco(n, 4)
            ts(Kg, ps(b, 0, 128), PG.v(G_EGC * 128 + col, 1), None, ALU.mult, None, psk(b) + pg(G_EGC)[1], Kgk)
            ts(Kd, ps(b, 0, 128), PG.v(G_EGR * 128 + col, 1), None, ALU.mult, None, psk(b) + pg(G_EGR)[1], Kdk)
            cp(Vt, ps(b, 128, 128), psk(b), Vtk, eng="act")
            b = psb()
            mm(ps(b, 0, 128), c4(TTH, j), Vt, True, True, TTH[1] + Vtk, psk(b))
            mm(ps(b, 128, 128), Kg, c4(TTH, j), True, True, Kgk + TTH[1], psk(b))
            Ub, Ubk = pco(n, 1)
            ts(Ub, ps(b, 0, 128), becol, None, ALU.mult, None, psk(b) + pg(G_BETA)[1], Ubk)
            WT_, WTk = pco(n, 0)
            cp(WT_, ps(b, 128, 128), psk(b), WTk, eng="act")
            yield

    def run(*gens):
        gens = list(gens)
        while gens:
            for g in list(gens):
                try:
                    next(g)
                except StopIteration:
                    gens.remove(g)

    PSTOP = 99

    def seq(*gens):
        for g in gens:
            yield from g

    def gdn_prep(d, h, Q_i, K_i, V_i, skip_front0=False):
        if not skip_front0:
            run(prep_front(d, h, Q_i, K_i, 0))
        run(prep_chain(d, 0), seq(prep_front(d, h, Q_i, K_i, 1), prep_chain(d, 1)))
        run(prep_tail(d, h, K_i, V_i, 0), prep_tail(d, h, K_i, V_i, 1))

    def gdn_scan_gen(d, h, first_dir):
        S, Sk = SS.v(0, 128), SS.k(0, 128)
        Sb, Sbk = PT.v(384, 128), PT.k(384, 128)
        vn, vnk = PT.v(512, 128), PT.k(512, 128)
        St, Stk = PT.v(640, 128), PT.k(640, 128)
        cp(Sb, S, Sk, Sbk)
        order = range(8) if d == 0 else range(7, -1, -1)
        for n in order:
            col = (d * 8 + h) * 8 + n
            c0 = n * 128
            WT_, WTk = pco(n, 0)
            Ub, Ubk = pco(n, 1)
            QgT, QgTk = pco(n, 2)
            QKT, QKTk = pco(n, 3)
            Kd, Kdk = pco(n, 4)
            b = psb()
            mm(ps(b, 0, 128), WT_, Sb, True, True, WTk + Sbk, psk(b))
            stt(vn, ps(b, 0, 128), PG.v(G_NB * 128 + col, 1), Ub, ALU.mult, ALU.add,
                psk(b) + pg(G_NB)[1] + Ubk, vnk)
            b = psb()
            mm(ps(b, 0, 128), Sb, QgT, True, False, Sbk + QgTk, psk(b))
            mm(ps(b, 0, 128), vn, QKT, False, True, vnk + QKTk, psk(b))
            b3 = psb()
            mm(ps(b3, 0, 128), Kd, vn, True, True, Kdk + vnk, psk(b3))
            if first_dir:
                cp(OB.v(c0, 128), ps(b, 0, 128), psk(b), OB.k(c0, 128), eng="act")
            else:
                tt(OB.v(c0, 128), OB.v(c0, 128), ps(b, 0, 128), ALU.add, psk(b) + OB.k(c0, 128), OB.k(c0, 128))
            stt(S, S, PG.v(G_DCH * 128 + col, 1), ps(b3, 0, 128), ALU.mult, ALU.add,
                Sk + pg(G_DCH)[1] + psk(b3), Sk)
            cp(Sb, S, Sk, Sbk, eng="act")
            yield

    def state_exchange_issue(i):
        S, Sk = SS.v(0, 128), SS.k(0, 128)
        P.dma("pool", cc_s_src[i].ap(), S, Sk, [("cc_s_src", i)])
        P.op("pool", lambda g: g.collective_compute("AllGather", ALU.bypass, replica_groups=RG,
                                                    ins=[cc_s_src[i].ap().opt()], outs=[cc_s_dst[i].ap().opt()]),
             [("cc_s_src", i)], [("cc_s_dst", i)])
        P.dma("pool", SS.v(128, 128), cc_s_dst[i].ap()[0:128, :], [("cc_s_dst", i)], SS.k(128, 128))
        P.dma("pool", SS.v(256, 128), cc_s_dst[i].ap()[128:256, :], [("cc_s_dst", i)], SS.k(256, 128))

    def state_exchange_finish():
        S, Sk = SS.v(0, 128), SS.k(0, 128)
        ts(S, SS.v(128, 128), FLG.v(0, 1), None, ALU.mult, None, SS.k(128, 128) + FLG.k(0, 2), Sk)
        stt(S, SS.v(256, 128), FLG.v(1, 1), S, ALU.mult, ALU.add, SS.k(256, 128) + FLG.k(0, 2) + Sk, Sk)

    def conv_taps_g(pbuf, accb, woff, ntap, boff=None):
        a, ak = accb.v(0, T), accb.k(0, T)
        pk = pbuf.k(0, TW)
        if boff is None:
            ts(a, pbuf.v(0, T), pc(woff), None, ALU.mult, None, pk + PCK, ak)
        else:
            ts(a, pbuf.v(0, T), pc(woff), pc(boff), ALU.mult, ALU.add, pk + PCK, ak)
        yield
        for tap in range(1, ntap):
            stt(a, pbuf.v(tap, T), pc(woff + tap), a, ALU.mult, ALU.add, pk + PCK + ak, ak)
            yield

    def conv_taps(*a, **kw):
        run(conv_taps_g(*a, **kw))

    def sgu(l):
        P.dma("pool", WT.v(0, 1024), sgu_wT_d[l], (), WT.k(0, 1024))
        LNG, LNGk = pg(9, 8)
        LNB, LNBk = pg(17, 8)
        P.dma("sp", LNG, sgu_ln_d[l][:, 0:1024], (), LNGk)
        P.dma("sp", LNB, sgu_ln_d[l][:, 1024:2048], (), LNBk)
        BSB, BSBk = OB.v(0, T), OB.k(0, T)
        P.dma("sp", BSB, sgu_bs_d[l], (), BSBk)

        def gelu(x, xk, t1, t1k, out, outk):
            act(t1, x, AF.Square, xk, t1k)
            ts(t1, t1, 0.044715, 1.0, ALU.mult, ALU.add, t1k, t1k)
            tt(t1, t1, x, ALU.mult, t1k + xk, t1k)
            act(t1, t1, AF.Sigmoid, t1k, t1k, scale=1.5957691216057308)
            tt(out, t1, x, ALU.mult, t1k + xk, outk)

        for g in range(8):
            if g % 2 == 0:
                slab = wslab()
                wload(slab, 0, w_in_d[l][:, OFF_U + g * 128: OFF_U + g * 128 + 256], NCK, 256)
            proj_fm(slab, 0, 256, (g % 2) * 128, bufdst(TA), None, 0)
            gelu(TA.v(0, T), TA.k(0, T), TB.v(0, T), TB.k(0, T), big(g), bigk(g))
        WV0 = 8 * T
        wvk = BIG.k(WV0, 16 * T)
        wv3 = BIG.v(WV0, 16 * T).rearrange("p (k n) -> p k n", k=NCK)
        for hf in range(2):
            P.dma("pool", wv3[:, :, hf * 512:(hf + 1) * 512],
                  w_in_d[l][:, OFF_V + hf * 512: OFF_V + (hf + 1) * 512].rearrange("(k p) n -> p k n", p=128),
                  (), wvk)
        for n in range(8):
            bq = 6
            for half in range(2):
                for q in range(2):
                    cq = half * 512 + q * 256
                    for k in range(NCK):
                        mm(ps(bq + half, q * 256, 256), hb(k, n * 128, 128), BIG.v(WV0 + k * T + cq, 256),
                           k == 0, k == NCK - 1, wvk + hbk(k, n * 128, 128), psk(bq + half))
            for half in range(2):
                cp(TA.v(half * 512, 512), ps(bq + half), psk(bq + half), TA.k(0, T), eng="act")
            gelu(TA.v(0, T), TA.k(0, T), TB.v(0, T), TB.k(0, T), TC.v(0, T), TC.k(0, T))
            st, stk = PT.v(0, 8), PT.k(0, 8)
            mu = SS.v(128, 1)
            muk = SS.k(128, 8)
            P.op("dve", lambda e: e.tensor_reduce(SS.v(128, 1), TC.v(0, T), mybir.AxisListType.X, ALU.add),
                 TC.k(0, T), muk)
            ts(SS.v(129, 1), SS.v(128, 1), -1.0 / 1024, None, ALU.mult, None, muk, muk)
            ts(TC.v(0, T), TC.v(0, T), SS.v(129, 1), None, ALU.add, None, TC.k(0, T) + muk, TC.k(0, T))
            act(TB.v(0, T), TC.v(0, T), AF.Square, TC.k(0, T), TB.k(0, T))
            P.op("dve", lambda e: e.tensor_reduce(SS.v(130, 1), TB.v(0, T), mybir.AxisListType.X, ALU.add),
                 TB.k(0, T), muk)
            act(SS.v(131, 1), SS.v(130, 1), AF.Sqrt, muk, muk, scale=1.0 / 1024, bias=EPS)
            recip(SS.v(131, 1), SS.v(131, 1), muk, muk)
            stt(TC.v(0, T), TC.v(0, T), SS.v(131, 1), LNG, ALU.mult, ALU.mult, TC.k(0, T) + muk + LNGk, TC.k(0, T))
            VN, VNk = PT.v(0, T), PT.k(0, T)
            tt(VN, TC.v(0, T), LNB, ALU.add, TC.k(0, T) + LNBk, VNk)
            for half in range(2):
                b = psb()
                for gq in range(4):
                    g = half * 4 + gq
                    mm(ps(b, gq * 128, 128), PT.v(g * 128, 128), WT.v(g * 128, 128), True, True,
                       VNk + WT.k(0, 1024), psk(b))
                tmp, tmpk = TA.v(0, 512), TA.k(0, 512)
                tt(tmp, ps(b), OB.v(half * 512, 512), ALU.add, psk(b) + BSBk, tmpk)
                for gq in range(4):
                    g = half * 4 + gq
                    tt(big(g, n * 128, 128), big(g, n * 128, 128), TA.v(gq * 128, 128), ALU.mult,
                       bigk(g, n * 128, 128) + tmpk, bigk(g, n * 128, 128))

    def mixer(l):
        P.dma("sp", PCOL.v(0, NPC), pcol_d[l], (), PCK)
        P.dma("sp", PBC.v(0, 32), pbc_d[l], (), PBC.k(0, 64))
        act(PBC.v(32, 16), PBC.v(0, 16), AF.Exp, PBC.k(0, 64), PBC.k(0, 64))
        ts(PBC.v(32, 16), PBC.v(32, 16), -1.0, None, ALU.mult, None, PBC.k(0, 64), PBC.k(0, 64))
        rmsnorm(PC_NMG)
        if stage < 1:
            return
        halo_exchange()
        if stage < 2:
            return
        sgu(l)
        if stage < 3:
            return
        gdn_gates(l)
        if stage < 4:
            return
        Q_i, K_i, V_i = 16, 17, 18

        def head_pre(h):
            s1 = wslab()
            wload(s1, 0, w_in_d[l][:, h * 128: h * 128 + 128], NCK, 128)
            wload(s1, 2048, w_in_d[l][:, 1024 + h * 128: 1024 + h * 128 + 128], NCK, 128)
            s2 = wslab()
            wload(s2, 0, w_in_d[l][:, 2048 + h * 128: 2048 + h * 128 + 128], NCK, 128)
            wload(s2, 2048, w_in_d[l][:, OFF_Z + h * 128: OFF_Z + h * 128 + 128], NCK, 128)
            yield from proj_fm_g(s2, 2048, 128, 0, bigdst(8 + h), None, 0, func=AF.Silu)
            for j, (slab, off, dsti, norm) in enumerate(((s1, 0, Q_i, True), (s1, 2048, K_i, True), (s2, 0, V_i, False))):
                pdst = lambda s, n: (TA.v(s, n), TA.k(s, n))
                memset(TA.v(0, 2), 0.0, TA.k(0, 2))
                yield from proj_fm_g(slab, off, 128, 0, pdst, None, 2)
                yield from conv_taps_g(TA, TB, PC_QCW + (j * 8 + h) * 5, 5)
                yield from l2n_g(TB, dsti, norm)

        def gnorm_g(h):
            b = psb(2)
            for tb in range(2):
                sq, sqk = PT.v(tb * 512, 512), PT.k(tb * 512, 512)
                act(sq, OB.v(tb * 512, 512), AF.Square, OB.k(0, T), sqk)
                mm(ps(b + tb), ONB[0], sq, True, True, ONB[1] + sqk, psk(b + tb))
            for tb in range(2):
                act(TC.v(tb * 512, 512), ps(b + tb), AF.Ln, psk(b + tb), TC.k(0, T), scale=1.0 / 128, bias=EPS)
            act(TC.v(0, T), TC.v(0, T), AF.Exp, TC.k(0, T), TC.k(0, T), scale=-0.5)
            yield
            tt(TC.v(0, T), TC.v(0, T), OB.v(0, T), ALU.mult, TC.k(0, T) + OB.k(0, T), TC.k(0, T))
            yield
            stt(big(8 + h), TC.v(0, T), pc(PC_GNG), big(8 + h), ALU.mult, ALU.mult,
                TC.k(0, T) + PCK + bigk(8 + h), bigk(8 + h))
            yield

        run(head_pre(0))
        for h in range(nheads):
            memset(SS.v(0, 128), 0.0, SS.k(0, 128))
            gdn_prep(0, h, Q_i, K_i, V_i)
            run(gdn_scan_gen(0, h, True), prep_front(1, h, Q_i, K_i, 0))
            state_exchange_issue(h % 2)
            gdn_prep(1, h, Q_i, K_i, V_i, skip_front0=True)
            state_exchange_finish()
            post = seq(gdn_scan_gen(1, h, False), gnorm_g(h))
            if h + 1 < nheads:
                run(post, head_pre(h + 1))
            else:
                run(post)
        if stage < 5:
            return
        def mg(c, tb):
            if tb == 0:
                s_ = 16 * T + c * 512
                return BIG.v(s_, 512), BIG.k(s_, 512)
            if c < 12:
                return PG.v(c * 256, 256).bitcast(BF16), PG.k(c * 256, 256)
            return OB.v((c - 12) * 256, 256).bitcast(BF16), OB.k((c - 12) * 256, 256)

        for c in range(NCK):
            sa = wslab()
            wload(sa, 0, w_a_d[l][:, c * 128:(c + 1) * 128], 8, 128)
            wload(sa, 1024, w_b_d[l][:, c * 128:(c + 1) * 128], 8, 128)
            sg = wslab()
            wload(sg, 0, w_in_d[l][:, OFF_GA + c * 128: OFF_GA + (c + 1) * 128], NCK, 128)
            wload(sg, 2048, w_in_d[l][:, OFF_GB + c * 128: OFF_GB + (c + 1) * 128], NCK, 128)
            for tb in range(2):
                s0, b = tb * 512, 4 * tb
                for k in range(8):
                    mm(ps(b), wv(sa, 0, 8, 128, k), big(8 + k, s0, 512), k == 0, k == 7,
                       sa.k(0, 1024) + bigk(8 + k, s0, 512), psk(b))
            for tb in range(2):
                s0, b = tb * 512, 4 * tb
                for k in range(8):
                    mm(ps(b + 1), wv(sa, 1024, 8, 128, k), big(k, s0, 512), k == 0, k == 7,
                       sa.k(1024, 1024) + bigk(k, s0, 512), psk(b + 1))
            for tb in range(2):
                s0, b = tb * 512, 4 * tb
                for k in range(NCK):
                    mm(ps(b + 2), wv(sg, 0, NCK, 128, k), hb(k, s0, 512), k == 0, k == NCK - 1,
                       sg.k(0, 2048) + hbk(k, s0, 512), psk(b + 2))
            for tb in range(2):
                s0, b = tb * 512, 4 * tb
                for k in range(NCK):
                    mm(ps(b + 3), wv(sg, 2048, NCK, 128, k), hb(k, s0, 512), k == 0, k == NCK - 1,
                       sg.k(2048, 2048) + hbk(k, s0, 512), psk(b + 3))
            for tb in range(2):
                b = 4 * tb
                SG = TA if tb == 0 else TC
                act(SG.v(0, 512), ps(b + 2), AF.Sigmoid, psk(b + 2), SG.k(0, 512))
                act(SG.v(512, 512), ps(b + 3), AF.Sigmoid, psk(b + 3), SG.k(512, 512))
                tt(TB.v(tb * 512, 512), ps(b), SG.v(0, 512), ALU.mult, psk(b) + SG.k(0, 512), TB.k(tb * 512, 512))
                tt(SG.v(512, 512), ps(b + 1), SG.v(512, 512), ALU.mult, psk(b + 1) + SG.k(512, 512), SG.k(512, 512))
                m, mk_ = mg(c, tb)
                tt(m, TB.v(tb * 512, 512), SG.v(512, 512), ALU.add, TB.k(tb * 512, 512) + SG.k(512, 512), mk_)
        for co in range(NCK):
            so = wslab()
            wload(so, 0, w_out_d[l][:, co * 128:(co + 1) * 128], NCK, 128)
            for tb in range(2):
                s0 = tb * 512
                b = psb()
                for k in range(NCK):
                    m, mk_ = mg(k, tb)
                    mm(ps(b), wv(so, 0, NCK, 128, k), m, k == 0, k == NCK - 1, so.k(0, 2048) + mk_, psk(b))
                tt(xs(co, s0, 512), xs(co, s0, 512), ps(b), ALU.add, xsk(co, s0, 512) + psk(b), xsk(co, s0, 512))

    def ffn(l):
        rmsnorm(PC_NFG)
        halo_exchange()
        memset(TA.v(0, 1), 0.0, TA.k(0, 1))
        memset(TC.v(0, 1), 0.0, TC.k(0, 1))
        for qd in range(4):
            for j in range(11):
                cg = qd * 11 + j
                slab = wslab()
                wload(slab, 0, w_up_d[l][:, cg * 128:(cg + 1) * 128], NCK, 128)
                wload(slab, 2048, w_up_d[l][:, D_FF + cg * 128: D_FF + (cg + 1) * 128], NCK, 128)
                proj_fm(slab, 0, 128, 0, lambda s, n: (TA.v(s, n), TA.k(s, n)), None, 1)
                conv_taps(TA, TB, PC_FCW + cg * 3, 3, PC_FCB + cg)
                proj_fm(slab, 2048, 128, 0, lambda s, n: (TC.v(s, n), TC.k(s, n)), None, 1)
                conv_taps(TC, OB, PC_FCW + (44 + cg) * 3, 3, PC_FCB + 44 + cg)
                act(TB.v(0, T), TB.v(0, T), AF.Silu, TB.k(0, T), TB.k(0, T))
                tt(big(j), TB.v(0, T), OB.v(0, T), ALU.mult, TB.k(0, T) + OB.k(0, T), bigk(j))
            for co in range(NCK):
                slab = wslab()
                wload(slab, 0, w_down_d[l][qd * 1408:(qd + 1) * 1408, co * 128:(co + 1) * 128], 11, 128)
                for tb in range(2):
                    b = psb()
                    for k in range(11):
                        mm(ps(b), wv(slab, 0, 11, 128, k), big(k, tb * 512, 512), k == 0, k == 10,
                           slab.k(0, 1408) + bigk(k, tb * 512, 512), psk(b))
                    tt(xs(co, tb * 512, 512), xs(co, tb * 512, 512), ps(b), ALU.add,
                       xsk(co, tb * 512, 512) + psk(b), xsk(co, tb * 512, 512))

    for l in range(depth):
        mixer(l)
        if stage >= 6:
            ffn(l)

    outs = []
    for tb in range(2):
        s0 = tb * 512
        b = psb()
        for c in range(NCK):
            sq = PT.v((c % 2) * 512, 512)
            sqk = PT.k((c % 2) * 512, 512)
            act(sq, xs(c, s0, 512), AF.Square, xsk(c, s0, 512), sqk)
            mm(ps(b), ONB[0], sq, c == 0, c == NCK - 1, ONB[1] + sqk, psk(b))
        r, rk = TA.v(0, 512), TA.k(0, 512)
        act(r, ps(b), AF.Ln, psk(b), rk, scale=1.0 / D_MODEL, bias=EPS)
        act(r, r, AF.Exp, rk, rk, scale=-0.5)
        for c in range(NCK):
            stt(xs(c, s0, 512), xs(c, s0, 512), FNG.v(c, 1), r, ALU.mult, ALU.mult,
                xsk(c, s0, 512) + FNG.k(0, 16) + rk, xsk(c, s0, 512))
    for c in range(NCK):
        outs.append(P.dma("sp", y_d[:, c, :], xs(c), xsk(c), [("y", c)]))
    P.op("sp", lambda e: None, [("y", c) for c in range(NCK)], ())
    P.emit()
    return nc, P


def _masks():
    i = np.arange(128)
    m, c = i[:, None], i[None, :]
    UI = (m <= c).astype(np.float32)
    LS = (m > c).astype(np.float32)
    LI = (m >= c).astype(np.float32)
    US = (m < c).astype(np.float32)
    masks = np.concatenate([UI, LS, LI, US], axis=1)
    nlm = np.zeros((2, 7, 128, 128), np.float32)
    for lev in range(7):
        b = 1 << lev
        same = (m // (2 * b)) == (c // (2 * b))
        s_first = (m % (2 * b)) < b
        c_first = (c % (2 * b)) < b
        nlm[0, lev] = -1.0 * (same & s_first & ~c_first)
        nlm[1, lev] = -1.0 * (same & ~s_first & c_first)
    nlm = nlm.transpose(2, 0, 1, 3).reshape(128, 2 * 7 * 128)
    return np.ascontiguousarray(masks), np.ascontiguousarray(nlm)


def _prep_inputs(inp, depth):
    f = lambda a: np.ascontiguousarray(np.asarray(a, dtype=np.float32))
    x = f(inp["x"])
    w_in = f(inp["w_in"])[:depth]
    shared = {
        "w_in": w_in,
        "w_a": f(inp["w_branch_a"])[:depth], "w_b": f(inp["w_branch_b"])[:depth],
        "w_out": f(inp["w_out"])[:depth], "w_up": f(inp["w_up"])[:depth], "w_down": f(inp["w_down"])[:depth],
        "fng": np.ascontiguousarray(f(inp["final_norm_g"]).reshape(16, 128).T),
    }
    masks, nlm = _masks()
    shared["masks"] = masks
    shared["nlm"] = nlm
    per_par = []
    for par in range(2):
        rev = par == 1
        d = {}
        ab = w_in[:, :, OFF_A:OFF_A + 32].copy()
        alog = f(inp["a_log"])[:depth].copy()
        dtb = f(inp["dt_bias"])[:depth].copy()
        qcw = f(inp["qkv_conv_w"])[:depth].copy()
        fcw = f(inp["ffn_conv_w"])[:depth].copy()
        sw = f(inp["sgu_w"])[:depth].copy()
        sb = f(inp["sgu_b"])[:depth].copy()
        if rev:
            ab = np.concatenate([ab[:, :, 8:16], ab[:, :, 0:8], ab[:, :, 24:32], ab[:, :, 16:24]], axis=2)
            alog = alog[:, ::-1]
            dtb = dtb[:, ::-1]
            qcw = qcw[:, ::-1]
            fcw = fcw[:, ::-1]
            sw = sw[:, :, ::-1, ::-1]
            sb = sb[:, :, ::-1]
        d["w_ab"] = np.ascontiguousarray(ab)
        pcol = np.zeros((depth, 128, NPC), np.float32)
        pcol[:, :, PC_NMG:PC_NMG + 16] = f(inp["norm_mix_g"])[:depth].reshape(depth, 16, 128).transpose(0, 2, 1)
        pcol[:, :, PC_NFG:PC_NFG + 16] = f(inp["norm_ffn_g"])[:depth].reshape(depth, 16, 128).transpose(0, 2, 1)
        pcol[:, :, PC_QCW:PC_QCW + 120] = qcw.reshape(depth, 5, 24, 128).transpose(0, 3, 2, 1).reshape(depth, 128, 120)
        pcol[:, :, PC_FCW:PC_FCW + 264] = fcw.reshape(depth, 3, 88, 128).transpose(0, 3, 2, 1).reshape(depth, 128, 264)
        pcol[:, :, PC_FCB:PC_FCB + 88] = f(inp["ffn_conv_b"])[:depth].reshape(depth, 88, 128).transpose(0, 2, 1)
        pcol[:, :, PC_GNG] = f(inp["gdn_norm_g"])[:depth]
        d["pcol"] = pcol
        pbc = np.zeros((depth, 128, 32), np.float32)
        pbc[:, :, 0:16] = alog.reshape(depth, 1, 16)
        pbc[:, :, 16:32] = dtb.reshape(depth, 1, 16)
        d["pbc"] = pbc
        d["sgu_ln"] = np.ascontiguousarray(np.broadcast_to(
            np.concatenate([f(inp["sgu_ln_g"])[:depth], f(inp["sgu_ln_b"])[:depth]], axis=1)[:, None, :], (depth, 128, 2048)))
        d["sgu_bs"] = np.ascontiguousarray(np.broadcast_to(sb.reshape(depth, 1, 1024), (depth, 128, 1024)))
        d["sgu_wT"] = np.ascontiguousarray(sw.transpose(0, 3, 1, 2).reshape(depth, 128, 1024))
        fl = np.zeros((128, 2), np.float32)
        fl[:, 1 - par] = 1.0
        d["flags"] = fl
        per_par.append(d)
    in_maps = []
    for core in range(8):
        b, par = core // 2, core % 2
        xx = x[b, par * T:(par + 1) * T]
        if par == 1:
            xx = xx[::-1]
        xt = np.ascontiguousarray(xx.T.reshape(16, 128, T).transpose(1, 0, 2))
        m = dict(shared)
        m.update(per_par[par])
        m["x"] = xt
        in_maps.append(m)
    return in_maps


def _assemble(res):
    out = np.zeros((BATCH, SEQ, D_MODEL), np.float32)
    for core in range(8):
        b, par = core // 2, core % 2
        y = np.asarray(res[core]["y"])
        yt = y.transpose(1, 0, 2).reshape(D_MODEL, T).T
        if par == 1:
            yt = yt[::-1]
        out[b, par * T:(par + 1) * T] = yt
    return out


_NC_CACHE = {}


def kernel(**inputs):
    depth = 4
    if depth not in _NC_CACHE:
        _NC_CACHE[depth] = build(depth)[0]
    nc = _NC_CACHE[depth]
    in_maps = _prep_inputs(inputs, depth)
    res = run_bass_kernel_spmd(nc, in_maps, core_ids=list(range(8)))
    return _assemble(res.results)
```
